# Optimizing a Trainium2 kernel written in Bass

```python
import math
import jax, jax.numpy as jnp
from jax import lax
import numpy as np

D_MODEL = 1024
BATCH = 4
SEQ = 8192
DEPTH = 2
DEC_BATCH = 8
DEC_SEQ = 64
PAST_LEN = 4096

CHUNK = 64
N_EVEN = (DEPTH + 1) // 2
N_ODD = DEPTH // 2
EPS = 1e-6
LRU_WIDTH = D_MODEL // 2
LRU_BLOCKS = 8
LRU_BLOCK_DIM = LRU_WIDTH // LRU_BLOCKS
CONV_WIDTH = 4
LRU_C = 8.0
ATT_HEADS = 8
HEAD_DIM = (D_MODEL // 2) // ATT_HEADS
ATT_WIDTH = ATT_HEADS * HEAD_DIM
LEFT_CHUNKS = 8
BAND = (LEFT_CHUNKS + 1) * CHUNK
WINDOW = LEFT_CHUNKS * CHUNK
MAX_REL = 128
IN_AB = 2 * LRU_WIDTH + 3 * ATT_WIDTH
MIX_AB = LRU_WIDTH + ATT_WIDTH
SSM_WIDTH = D_MODEL
SSM_GROUP = 16
SSM_GROUPS = SSM_WIDTH // SSM_GROUP
SSM_STATE = 64
DT_MIN = 1e-3
DT_MAX = 1e-1
D_FF = 2816

kernel_name = "hybrid_streaming_encoder_step"


def _rms_norm(x, g):
    xf = x.astype(jnp.float32)
    y = xf * lax.rsqrt(jnp.mean(xf * xf, axis=-1, keepdims=True) + EPS)
    return (y * g.astype(jnp.float32)).astype(x.dtype)


def _swiglu(x, w_in, w_out):
    gate, up = jnp.split(x @ w_in, 2, axis=-1)
    return (jax.nn.silu(gate) * up) @ w_out


def _causal_dw_conv(x, prev, w, b):
    S = x.shape[1]
    xp = jnp.concatenate([prev.astype(x.dtype), x], axis=1)
    y = b.astype(x.dtype)
    for k in range(CONV_WIDTH):
        y = y + xp[:, k:k + S] * w[k]
    return y, xp[:, xp.shape[1] - (CONV_WIDTH - 1):]


def _rg_lru(xc, h0, wa, ba, wx, bx, lam):
    Bn, S, W = xc.shape
    xb = xc.reshape(Bn, S, LRU_BLOCKS, LRU_BLOCK_DIM)
    r = jax.nn.sigmoid(jnp.einsum('bshi,hij->bshj', xb, wa).reshape(Bn, S, W) + ba)
    i = jax.nn.sigmoid(jnp.einsum('bshi,hij->bshj', xb, wx).reshape(Bn, S, W) + bx)
    log_a = -LRU_C * r.astype(jnp.float32) * jax.nn.softplus(-lam.astype(jnp.float32))
    a = jnp.exp(log_a)
    u = jnp.sqrt(-jnp.expm1(2.0 * log_a)) * (i * xc).astype(jnp.float32)

    def step(h, au):
        a_t, u_t = au
        h = a_t * h + u_t
        return h, h

    h_last, hs = lax.scan(step, h0.astype(jnp.float32), (jnp.swapaxes(a, 0, 1), jnp.swapaxes(u, 0, 1)))
    return jnp.swapaxes(hs, 0, 1).astype(xc.dtype), h_last


def _band_attention(q, k, v, q_pos, k_pos, rel_table):
    rel = q_pos[:, :, None] - k_pos[:, None, :]
    bias = rel_table[jnp.clip(rel, -MAX_REL, MAX_REL) + MAX_REL]
    qc = (q_pos // CHUNK)[:, :, None]
    kc = (k_pos // CHUNK)[:, None, :]
    valid = (k_pos[:, None, :] >= 0) & (kc <= qc) & (kc >= qc - LEFT_CHUNKS)
    s = jnp.einsum('bnqhd,bnkhd->bnhqk', q, k).astype(jnp.float32) / math.sqrt(HEAD_DIM)
    s = s + jnp.transpose(bias, (0, 3, 1, 2)).astype(jnp.float32)[None]
    s = jnp.where(valid[None, :, None], s, -1e30)
    p = jax.nn.softmax(s, axis=-1).astype(v.dtype)
    return jnp.einsum('bnhqk,bnkhd->bnqhd', p, v)


def _prompt_band_attention(q, k, v, rel_table):
    Bn, S = q.shape[:2]
    nc = S // CHUNK
    qch = q.reshape(Bn, nc, CHUNK, ATT_HEADS, HEAD_DIM)
    pad = ((0, 0), (LEFT_CHUNKS * CHUNK, 0), (0, 0), (0, 0))
    kp = jnp.pad(k, pad).reshape(Bn, nc + LEFT_CHUNKS, CHUNK, ATT_HEADS, HEAD_DIM)
    vp = jnp.pad(v, pad).reshape(Bn, nc + LEFT_CHUNKS, CHUNK, ATT_HEADS, HEAD_DIM)
    k_band = jnp.concatenate([kp[:, o:o + nc] for o in range(LEFT_CHUNKS + 1)], axis=2)
    v_band = jnp.concatenate([vp[:, o:o + nc] for o in range(LEFT_CHUNKS + 1)], axis=2)
    q_pos = jnp.arange(S, dtype=jnp.int32).reshape(nc, CHUNK)
    k_pos = (jnp.arange(nc, dtype=jnp.int32)[:, None] - LEFT_CHUNKS) * CHUNK + jnp.arange(BAND, dtype=jnp.int32)[None]
    out = _band_attention(qch, k_band, v_band, q_pos, k_pos, rel_table)
    return out.reshape(Bn, S, ATT_HEADS, HEAD_DIM)


def _sample_band_attention(q, k, v, cache_k, cache_v, rel_table):
    T = q.shape[1]
    wc = cache_k.shape[1]
    kk = jnp.concatenate([cache_k.astype(k.dtype), k], axis=1)[:, None]
    vv = jnp.concatenate([cache_v.astype(v.dtype), v], axis=1)[:, None]
    q_pos = (PAST_LEN + jnp.arange(T, dtype=jnp.int32))[None]
    k_pos = (PAST_LEN - wc + jnp.arange(wc + T, dtype=jnp.int32))[None]
    return _band_attention(q[:, None], kk, vv, q_pos, k_pos, rel_table)[:, 0]


def _even_mixer(h, conv_prev, lru_prev, cache_k, cache_v, w_in, conv_w, conv_b, lru_wa, lru_ba,
                lru_wx, lru_bx, lru_lambda, q_gain, k_gain, rel_bias, w_out, prompt):
    Bn, S, _ = h.shape
    proj = h @ w_in
    xa, ga, q, k, v = jnp.split(proj, [LRU_WIDTH, 2 * LRU_WIDTH, 2 * LRU_WIDTH + ATT_WIDTH,
                                       2 * LRU_WIDTH + 2 * ATT_WIDTH], axis=-1)
    xc, conv_new = _causal_dw_conv(xa, conv_prev, conv_w, conv_b)
    ya, lru_new = _rg_lru(xc, lru_prev, lru_wa, lru_ba, lru_wx, lru_bx, lru_lambda)
    ya = ya * jax.nn.gelu(ga)
    q = _rms_norm(q.reshape(Bn, S, ATT_HEADS, HEAD_DIM), q_gain)
    k = _rms_norm(k.reshape(Bn, S, ATT_HEADS, HEAD_DIM), k_gain)
    v = v.reshape(Bn, S, ATT_HEADS, HEAD_DIM)
    if prompt:
        yb = _prompt_band_attention(q, k, v, rel_bias)
        keep = min(WINDOW, S)
        k_new, v_new = k[:, S - keep:], v[:, S - keep:]
    else:
        yb = _sample_band_attention(q, k, v, cache_k, cache_v, rel_bias)
        k_new, v_new = k, v
    y = jnp.concatenate([ya, yb.reshape(Bn, S, ATT_WIDTH)], axis=-1) @ w_out
    return y, conv_new, lru_new, k_new, v_new


def _s5(u, s0_re, s0_im, A_re, A_im, B_re, B_im, C_re, C_im, D_skip, log_dt):
    Bn, S, W = u.shape
    f32 = jnp.float32
    A = lax.complex(A_re.astype(f32), A_im.astype(f32))
    dt = jnp.exp(log_dt.astype(f32))[:, None]
    A_bar = jnp.exp(A * dt)
    Bc = lax.complex(B_re.astype(f32), B_im.astype(f32))
    B_bar = ((A_bar - 1.0) / A)[..., None] * Bc
    Cc = lax.complex(C_re.astype(f32), C_im.astype(f32))
    ug = u.reshape(Bn, S, SSM_GROUPS, SSM_GROUP).astype(f32)
    bu = jnp.einsum('bsgi,gpi->bsgp', ug.astype(jnp.complex64), B_bar)
    s0 = lax.complex(s0_re.astype(f32), s0_im.astype(f32))
    bu = bu.at[:, 0].add(A_bar * s0)
    a = jnp.broadcast_to(A_bar, (1, S, SSM_GROUPS, SSM_STATE))

    def combine(e1, e2):
        a1, b1 = e1
        a2, b2 = e2
        return a1 * a2, a2 * b1 + b2

    _, s = lax.associative_scan(combine, (a, bu), axis=1)
    y = jnp.einsum('gip,bsgp->bsgi', Cc, s).real.reshape(Bn, S, W) + D_skip.astype(f32) * u.astype(f32)
    s_last = s[:, -1]
    return y.astype(u.dtype), s_last.real, s_last.imag


def _odd_mixer(h, s_re, s_im, A_re, A_im, B_re, B_im, C_re, C_im, D_skip, log_dt, glu_w):
    y, n_re, n_im = _s5(h, s_re, s_im, A_re, A_im, B_re, B_im, C_re, C_im, D_skip, log_dt)
    a, g = jnp.split(y @ glu_w, 2, axis=-1)
    return a * jax.nn.sigmoid(g), n_re, n_im


def _trunk(x, p, conv_st, lru_st, cache_k, cache_v, ssm_re_st, ssm_im_st, prompt):
    Bn = x.shape[0]
    conv_out, lru_out, k_out, v_out, re_out, im_out = [], [], [], [], [], []
    for l in range(DEPTH):
        x = x + 0.5 * _swiglu(_rms_norm(x, p['ffn1_norm'][l]), p['ffn1_w_in'][l], p['ffn1_w_out'][l])
        h = _rms_norm(x, p['mix_norm'][l])
        if l % 2 == 0:
            e = l // 2
            if prompt:
                c_prev = jnp.zeros((Bn, CONV_WIDTH - 1, LRU_WIDTH), x.dtype)
                h_prev = jnp.zeros((Bn, LRU_WIDTH), jnp.float32)
                ck, cv = None, None
            else:
                c_prev, h_prev, ck, cv = conv_st[e], lru_st[e], cache_k[e], cache_v[e]
            y, c_new, h_new, k_new, v_new = _even_mixer(
                h, c_prev, h_prev, ck, cv, p['ab_w_in'][e], p['conv_w'][e], p['conv_b'][e],
                p['lru_wa'][e], p['lru_ba'][e], p['lru_wx'][e], p['lru_bx'][e], p['lru_lambda'][e],
                p['q_norm'][e], p['k_norm'][e], p['rel_bias'][e], p['ab_w_out'][e], prompt)
            conv_out.append(c_new)
            lru_out.append(h_new)
            k_out.append(k_new)
            v_out.append(v_new)
        else:
            o = l // 2
            if prompt:
                s_re = jnp.zeros((Bn, SSM_GROUPS, SSM_STATE), jnp.float32)
                s_im = jnp.zeros((Bn, SSM_GROUPS, SSM_STATE), jnp.float32)
            else:
                s_re, s_im = ssm_re_st[o], ssm_im_st[o]
            y, n_re, n_im = _odd_mixer(
                h, s_re, s_im, p['ssm_A_re'][o], p['ssm_A_im'][o], p['ssm_B_re'][o], p['ssm_B_im'][o],
                p['ssm_C_re'][o], p['ssm_C_im'][o], p['ssm_D'][o], p['ssm_log_dt'][o], p['glu_w'][o])
            re_out.append(n_re)
            im_out.append(n_im)
        x = x + y
        x = x + 0.5 * _swiglu(_rms_norm(x, p['ffn2_norm'][l]), p['ffn2_w_in'][l], p['ffn2_w_out'][l])
    return (x, jnp.stack(conv_out), jnp.stack(lru_out), jnp.stack(k_out), jnp.stack(v_out),
            jnp.stack(re_out), jnp.stack(im_out))


def _normal(k, shape, scale):
    return scale * jax.random.normal(k, shape, jnp.float32)


def setup_inputs(seed: int = 0) -> dict:
    key = jax.random.key(seed)
    k = jax.random.split(key, 40)
    wc = min(WINDOW, PAST_LEN)
    a0 = jax.random.uniform(k[22], (N_EVEN, LRU_WIDTH), jnp.float32, 0.9, 0.999)
    pa = a0 ** (1.0 / LRU_C)
    lam = jnp.log(pa) - jnp.log1p(-pa)
    n_idx = jnp.arange(SSM_STATE, dtype=jnp.float32)
    return {
        'x_prompt': _normal(k[0], (BATCH, SEQ, D_MODEL), 1.0),
        'x_sample': _normal(k[1], (DEC_BATCH, DEC_SEQ, D_MODEL), 1.0),
        'state_rglru_conv': _normal(k[2], (N_EVEN, DEC_BATCH, CONV_WIDTH - 1, LRU_WIDTH), 1.0),
        'state_rglru_h': _normal(k[3], (N_EVEN, DEC_BATCH, LRU_WIDTH), 0.5),
        'cache_band_k': _normal(k[4], (N_EVEN, DEC_BATCH, wc, ATT_HEADS, HEAD_DIM), 1.0),
        'cache_band_v': _normal(k[5], (N_EVEN, DEC_BATCH, wc, ATT_HEADS, HEAD_DIM), 1.0),
        'state_ssm_re': _normal(k[6], (N_ODD, DEC_BATCH, SSM_GROUPS, SSM_STATE), 0.5),
        'state_ssm_im': _normal(k[7], (N_ODD, DEC_BATCH, SSM_GROUPS, SSM_STATE), 0.5),
        'ffn1_norm': 1.0 + _normal(k[8], (DEPTH, D_MODEL), 0.02),
        'ffn1_w_in': _normal(k[9], (DEPTH, D_MODEL, 2 * D_FF), D_MODEL ** -0.5),
        'ffn1_w_out': _normal(k[10], (DEPTH, D_FF, D_MODEL), D_FF ** -0.5),
        'mix_norm': 1.0 + _normal(k[11], (DEPTH, D_MODEL), 0.02),
        'ffn2_norm': 1.0 + _normal(k[12], (DEPTH, D_MODEL), 0.02),
        'ffn2_w_in': _normal(k[13], (DEPTH, D_MODEL, 2 * D_FF), D_MODEL ** -0.5),
        'ffn2_w_out': _normal(k[14], (DEPTH, D_FF, D_MODEL), D_FF ** -0.5),
        'ab_w_in': _normal(k[15], (N_EVEN, D_MODEL, IN_AB), D_MODEL ** -0.5),
        'conv_w': _normal(k[16], (N_EVEN, CONV_WIDTH, LRU_WIDTH), CONV_WIDTH ** -0.5),
        'conv_b': _normal(k[17], (N_EVEN, LRU_WIDTH), 0.01),
        'lru_wa': _normal(k[18], (N_EVEN, LRU_BLOCKS, LRU_BLOCK_DIM, LRU_BLOCK_DIM), LRU_BLOCK_DIM ** -0.5),
        'lru_ba': _normal(k[19], (N_EVEN, LRU_WIDTH), 0.01),
        'lru_wx': _normal(k[20], (N_EVEN, LRU_BLOCKS, LRU_BLOCK_DIM, LRU_BLOCK_DIM), LRU_BLOCK_DIM ** -0.5),
        'lru_bx': _normal(k[21], (N_EVEN, LRU_WIDTH), 0.01),
        'lru_lambda': lam,
        'q_norm': 1.0 + _normal(k[23], (N_EVEN, HEAD_DIM), 0.02),
        'k_norm': 1.0 + _normal(k[24], (N_EVEN, HEAD_DIM), 0.02),
        'rel_bias': _normal(k[25], (N_EVEN, 2 * MAX_REL + 1, ATT_HEADS), 0.2),
        'ab_w_out': _normal(k[26], (N_EVEN, MIX_AB, D_MODEL), MIX_AB ** -0.5),
        'ssm_A_re': -0.5 + _normal(k[27], (N_ODD, SSM_GROUPS, SSM_STATE), 0.01),
        'ssm_A_im': math.pi * n_idx + _normal(k[28], (N_ODD, SSM_GROUPS, SSM_STATE), 0.01),
        'ssm_B_re': _normal(k[29], (N_ODD, SSM_GROUPS, SSM_STATE, SSM_GROUP), (0.5 / SSM_GROUP) ** 0.5),
        'ssm_B_im': _normal(k[30], (N_ODD, SSM_GROUPS, SSM_STATE, SSM_GROUP), (0.5 / SSM_GROUP) ** 0.5),
        'ssm_C_re': _normal(k[31], (N_ODD, SSM_GROUPS, SSM_GROUP, SSM_STATE), (0.5 / SSM_STATE) ** 0.5),
        'ssm_C_im': _normal(k[32], (N_ODD, SSM_GROUPS, SSM_GROUP, SSM_STATE), (0.5 / SSM_STATE) ** 0.5),
        'ssm_D': _normal(k[33], (N_ODD, SSM_WIDTH), 0.5),
        'ssm_log_dt': jax.random.uniform(k[34], (N_ODD, SSM_GROUPS), jnp.float32, math.log(DT_MIN), math.log(DT_MAX)),
        'glu_w': _normal(k[35], (N_ODD, SSM_WIDTH, 2 * D_MODEL), SSM_WIDTH ** -0.5),
    }


def reference(x_prompt, x_sample, state_rglru_conv, state_rglru_h, cache_band_k, cache_band_v,
              state_ssm_re, state_ssm_im, ffn1_norm, ffn1_w_in, ffn1_w_out, mix_norm, ffn2_norm,
              ffn2_w_in, ffn2_w_out, ab_w_in, conv_w, conv_b, lru_wa, lru_ba, lru_wx, lru_bx,
              lru_lambda, q_norm, k_norm, rel_bias, ab_w_out, ssm_A_re, ssm_A_im, ssm_B_re, ssm_B_im,
              ssm_C_re, ssm_C_im, ssm_D, ssm_log_dt, glu_w):
    p = dict(ffn1_norm=ffn1_norm, ffn1_w_in=ffn1_w_in, ffn1_w_out=ffn1_w_out, mix_norm=mix_norm,
             ffn2_norm=ffn2_norm, ffn2_w_in=ffn2_w_in, ffn2_w_out=ffn2_w_out, ab_w_in=ab_w_in,
             conv_w=conv_w, conv_b=conv_b, lru_wa=lru_wa, lru_ba=lru_ba, lru_wx=lru_wx, lru_bx=lru_bx,
             lru_lambda=lru_lambda, q_norm=q_norm, k_norm=k_norm, rel_bias=rel_bias, ab_w_out=ab_w_out,
             ssm_A_re=ssm_A_re, ssm_A_im=ssm_A_im, ssm_B_re=ssm_B_re, ssm_B_im=ssm_B_im,
             ssm_C_re=ssm_C_re, ssm_C_im=ssm_C_im, ssm_D=ssm_D, ssm_log_dt=ssm_log_dt, glu_w=glu_w)
    y_prompt, p_conv, p_h, p_k, p_v, p_re, p_im = _trunk(
        x_prompt, p, None, None, None, None, None, None, True)
    y_sample, s_conv, s_h, s_k, s_v, s_re, s_im = _trunk(
        x_sample, p, state_rglru_conv, state_rglru_h, cache_band_k, cache_band_v,
        state_ssm_re, state_ssm_im, False)
    return (y_prompt, y_sample, p_conv, p_h, p_k, p_v, p_re, p_im, s_conv, s_h, s_k, s_v, s_re, s_im)
```

```python
import contextlib
import os
import numpy as np
_SKIP = set(os.environ.get('K_SKIP', '').split(','))
_STOP = int(os.environ.get('K_STOP', '1000000000'))
_LIST = [int(v) for v in os.environ['K_LIST'].split(',')] if os.environ.get('K_LIST') else None
import concourse.bass as bass
import concourse.mybir as mybir
from concourse.bass_utils import run_bass_kernel_spmd

F32 = mybir.dt.float32
BF16 = mybir.dt.bfloat16
AF = mybir.ActivationFunctionType
ALU = mybir.AluOpType
AX = mybir.AxisListType

D = 1024
DFF = 2816
NF = 22
DK = 8
EPS = 1e-6
PAGE = 256


class Reg:
    __slots__ = ("space", "lo", "hi")

    def __init__(self, space, lo, hi):
        self.space, self.lo, self.hi = space, lo, hi

    def sub(self, lo, hi):
        assert 0 <= lo < hi <= self.hi - self.lo, (lo, hi, self.lo, self.hi)
        return Reg(self.space, self.lo + lo, self.lo + hi)


class Sched:
    def __init__(self, nc, es):
        self.nc = nc
        self.es = es
        self.eng = {"pe": nc.tensor, "act": nc.scalar, "dve": nc.vector, "pool": nc.gpsimd, "sp": nc.sync}
        self.sems = {}
        self.cnt = {}
        for k in self.eng:
            self.sems[k] = es.enter_context(nc.semaphore("s_" + k))
            self.cnt[k] = 0
        self.pending = {k: False for k in self.eng}
        self.waited = {k: {} for k in self.eng}
        self.pages = {}
        self.nops = 0

    def dma_sem(self, name):
        key = "d_" + name
        if key not in self.sems:
            self.sems[key] = self.es.enter_context(self.nc.semaphore(key))
            self.cnt[key] = 0
        return key

    def _pg(self, regs):
        for r in regs:
            for p in range(r.lo // PAGE, (r.hi + PAGE - 1) // PAGE):
                yield (r.space, p)

    def _collect(self, reads, writes):
        deps = {}

        def add(tok):
            if tok is not None:
                k, v = tok
                if deps.get(k, 0) < v:
                    deps[k] = v

        for pg in self._pg(reads):
            e = self.pages.get(pg)
            if e is not None:
                add(e[0])
        for pg in self._pg(writes):
            e = self.pages.get(pg)
            if e is not None:
                add(e[0])
                for k, v in e[1].items():
                    add((k, v))
        return deps

    def _waits(self, E, deps):
        w = self.waited[E]
        for k, v in deps.items():
            if k == E and E == "pe":
                continue
            if w.get(k, 0) < v:
                self.eng[E].wait_ge(self.sems[k], v)
                w[k] = v

    def _record(self, tok, reads, writes):
        k, v = tok
        for pg in self._pg(reads):
            e = self.pages.get(pg)
            if e is None:
                e = [None, {}]
                self.pages[pg] = e
            e[1][k] = v
        for pg in self._pg(writes):
            self.pages[pg] = [tok, {}]

    def op(self, E, fn, reads=(), writes=(), sig=True):
        if self.nops >= _STOP:
            self.nops += 1
            return None
        deps = self._collect(reads, writes)
        self._waits(E, deps)
        if _LIST and _LIST[0] <= self.nops < _LIST[1]:
            print("OP", self.nops, E, fn.__code__.co_firstlineno)
        ins = fn(self.eng[E])
        tick = self.cnt[E] + 1
        if sig:
            ins.then_inc(self.sems[E], 1)
            self.cnt[E] = tick
            self.pending[E] = False
        else:
            self.pending[E] = True
        self._record((E, tick), reads, writes)
        self.nops += 1
        return ins

    def dma(self, Q, out, in_, reads, writes, sem, slow=False):
        if self.nops >= _STOP:
            self.nops += 1
            return None
        key = self.dma_sem(sem)
        deps = self._collect(reads, writes)
        self._waits(Q, deps)
        if slow:
            ins = self.eng[Q].dma_start(out=out, in_=in_, allow_slow_non_contiguous=True)
        else:
            ins = self.eng[Q].dma_start(out=out, in_=in_)
        ins.then_inc(self.sems[key], 16)
        self.cnt[key] += 16
        self._record((key, self.cnt[key]), reads, writes)
        self.nops += 1
        return ins

    def finish(self):
        if _STOP >= 1000000000:
            assert not any(self.pending.values()), self.pending
        sp = self.eng["sp"]
        for k, v in self.cnt.items():
            if k != "sp" and v > 0:
                sp.wait_ge(self.sems[k], v)


class Arena:
    def __init__(self, nc, es, name, nbytes, space):
        self.t = es.enter_context(nc.sbuf_tensor(name, [128, nbytes // 4], F32))
        self.space = space
        self.nbytes = nbytes
        self.top = 0

    def alloc(self, nbytes, at=None):
        nbytes = (nbytes + PAGE - 1) // PAGE * PAGE
        if at is None:
            at = self.top
            self.top += nbytes
            assert self.top <= self.nbytes, ("arena overflow", self.top, self.nbytes)
        assert at + nbytes <= self.nbytes, ("arena overflow", at, nbytes, self.nbytes)
        return Reg(self.space, at, at + nbytes)

    def ap(self, reg, dtype, pattern=None, n=None, **kw):
        a = self.t[:, reg.lo // 4: reg.hi // 4]
        if dtype != F32:
            a = a.bitcast(dtype)
        if n is not None:
            a = a[:, 0:n]
        if pattern is not None:
            a = a.rearrange(pattern, **kw)
        return a


class Builder:
    def __init__(self, SEQ=8192, stages=("ffn", "mix0", "mix1"), with_sample=True, debug=False):
        self.SEQ = SEQ
        self.stages = stages
        self.with_sample = with_sample
        self.debug = debug
        self.nc = bass.Bass("TRN2", target_bir_lowering=False)
        self.es = contextlib.ExitStack()

    def din(self, name, shape, dtype=F32):
        return self.nc.dram_tensor(name, list(shape), dtype, kind="ExternalInput").ap()

    def dout(self, name, shape, dtype=F32):
        return self.nc.dram_tensor(name, list(shape), dtype, kind="ExternalOutput").ap()

    def dscr(self, name, shape, dtype=BF16):
        return self.nc.dram_tensor(name, list(shape), dtype, kind="Internal").ap()

    def build(self):
        with self.es:
            self._build()
        return self.nc

    def _build(self):
        nc, es = self.nc, self.es
        SEQ = self.SEQ
        S = self.S = Sched(nc, es)
        self.xp = self.din("xp", [max(SEQ, 8), D])
        self.xs = self.din("xs", [64, D])
        self.yp = self.dout("yp", [max(SEQ, 8), D])
        self.ys = self.dout("ys", [64, D])
        self.w_ffn_in = self.din("w_ffn_in", [4, NF * 128, 2048])
        self.w_ffn_out = self.din("w_ffn_out", [4, 128, NF * 1024])
        self.gam = self.din("gam", [128, 6 * 8])
        self.s_ffn_in = self.dscr("s_ffn_in", [4, NF * 128, 2048])
        self.s_ffn_out = self.dscr("s_ffn_out", [4, 128, NF * 1024])
        self.R_scr = Reg("dram", 0, PAGE)
        self.extra_pairs = []
        if "mix0" in self.stages:
            self.w_abin = self.din("w_abin", [16 * 128, 1024])
            self.w_abv = self.din("w_abv", [128, 8 * 512])
            self.w_about = self.din("w_about", [128, 8 * 1024])
            self.s_abin = self.dscr("s_abin", [16 * 128, 1024])
            self.s_abv = self.dscr("s_abv", [128, 8 * 512])
            self.s_about = self.dscr("s_about", [128, 8 * 1024])
            self.extra_pairs += [(self.w_abin, self.s_abin), (self.w_abv, self.s_abv), (self.w_about, self.s_about)]
            self.m0off = dict(cw=0, cb=16, ba=20, bx=24, lam=28, c1=32, gq=36, gk=37, bc=38)
            self.NM0 = 46
            self.m0c = self.din("m0c", [128, self.NM0])
            self.wbd = self.din("wbd", [128, 2 * 4 * 128])
            self.bblk = self.din("bblk", [128, 8 * 5 * 64])
            self.st_rg = self.din("st_rg", [128, 16])
            self.ck = self.din("ck", [512, 512])
            self.cv = self.din("cv", [512, 512])
            KR = max(8, min(512, SEQ))
            self.o_pconv = self.dout("o_pconv", [3, 512])
            self.o_ph = self.dout("o_ph", [512])
            self.o_pk = self.dout("o_pk", [KR, 512])
            self.o_pv = self.dout("o_pv", [KR, 512])
            self.o_sconv = self.dout("o_sconv", [3, 512])
            self.o_sh = self.dout("o_sh", [512])
            self.o_sk = self.dout("o_sk", [64, 512])
            self.o_sv = self.dout("o_sv", [64, 512])

        if "mix1" in self.stages:
            self.s5p = self.din("s5p", [64, 192])
            self.s5c = self.din("s5c", [64, 2048])
            self.w_bp = self.din("w_bp", [128, 8192])
            self.s_bp = self.dscr("s_bp", [128, 8192])
            self.s_cp = self.dscr("s_cp", [64, 2048])
            self.w_glu = self.din("w_glu", [128, 8 * 2048])
            self.s_glu = self.dscr("s_glu", [128, 8 * 2048])
            self.dfm = self.din("dfm", [128, 8])
            self.s0 = self.din("s0", [64, 128])
            self.extra_pairs += [(self.w_bp, self.s_bp), (self.w_glu, self.s_glu)]
            self.o_pre = self.dout("o_pre", [64, 64])
            self.o_pim = self.dout("o_pim", [64, 64])
            self.o_sre = self.dout("o_sre", [64, 64])
            self.o_sim = self.dout("o_sim", [64, 64])
        A = self.A = Arena(nc, es, "arena", 207 * 1024, "sb")
        self.rX8 = A.alloc(8 * 1024 * 4)
        self.rXNT = A.alloc(8 * 1024 * 2)
        self.rWIN = [A.alloc(2048 * 2) for _ in range(2)]
        self.rGAM = A.alloc(6 * 8 * 4)
        self.rIDB = A.alloc(128 * 2)
        self.rIDF = A.alloc(128 * 4)
        self.rSS = A.alloc(PAGE)
        self.rRS = A.alloc(PAGE)
        self.rSG = [A.alloc(512 * 4) for _ in range(2)]
        if "mix0" in self.stages:
            self.alloc_mix0_consts()
        if "mix1" in self.stages:
            self.alloc_mix1_consts()
        self.zone = A.top
        self.rWOUT = A.alloc(NF * 1024 * 2, at=self.zone)
        self.rH = A.alloc(NF * 512 * 2, at=self.zone + NF * 1024 * 2)
        self.rXS = A.alloc(8 * 1024 * 2, at=self.zone + NF * 1024 * 2)

        self.X8 = A.ap(self.rX8, F32, "p (j d) -> p j d", j=8)
        self.XNT = A.ap(self.rXNT, BF16, "p (k t) -> p k t", k=8)
        self.XS = A.ap(self.rXS, BF16, "p (j d) -> p j d", j=8)
        self.WIN = [A.ap(r, BF16, "p (k n) -> p k n", k=8) for r in self.rWIN]
        self.GAM = A.ap(self.rGAM, F32, "p (w k) -> p w k", n=48, w=6)
        self.IDB = A.ap(self.rIDB, BF16, n=128)
        self.IDF = A.ap(self.rIDF, F32, n=128)
        self.SS = A.ap(self.rSS, F32)
        self.RS = A.ap(self.rRS, F32)
        self.SG = [A.ap(r, F32) for r in self.rSG]
        self.WOUT = A.ap(self.rWOUT, BF16, "p (f n) -> p f n", f=NF)
        self.H = A.ap(self.rH, BF16, "p (f n) -> p f n", f=NF)

        self.PS = [es.enter_context(nc.psum_tensor("ps%d" % i, [128, 512], F32)) for i in range(8)]
        self.rPS = [Reg("ps", i * PAGE, (i + 1) * PAGE) for i in range(8)]
        self.ps_rr = 0
        self.win_rr = 0
        self.sg_rr = 0

        self.setup_consts()
        self.prepass()
        if "mix0" in self.stages:
            self.setup_mix0()
            self.setup_mix0_consts()
        if "mix1" in self.stages:
            self.setup_mix1()
        ntile = SEQ // 1024
        for t in range(ntile):
            self.macro_tile(self.xp, self.yp, t * 1024, 128, prompt=True, first=(t == 0), last=(t == ntile - 1))
        if self.with_sample:
            self.macro_tile(self.xs, self.ys, 0, 8, prompt=False, first=True, last=True)
        S.finish()

    def dbg(self, name, ap, reg, shape):
        if not self.debug:
            return
        o = self.dout("dbg_" + name, shape)
        self.S.dma("sp", out=o, in_=ap, reads=[reg], writes=[], sem="dbg_" + name)

    def next_ps(self, n=1):
        i = self.ps_rr
        self.ps_rr = (self.ps_rr + 1) % 8
        return i

    def setup_consts(self):
        S, A = self.S, self.A
        S.dma("sp", out=self.A.ap(self.rGAM, F32, n=48), in_=self.gam[:, :], reads=[], writes=[self.rGAM], sem="const1")
        idf = self.IDF
        S.op("pool", lambda e: e.memset(idf[:, :], 0.0), writes=[self.rIDF])
        S.op("pool", lambda e: e.affine_select(out=idf[:, :], in_=idf[:, :], pattern=[[1, 128]],
                                                compare_op=ALU.not_equal, fill=1.0, base=0, channel_multiplier=-1),
             reads=[self.rIDF], writes=[self.rIDF])
        S.op("pool", lambda e: e.tensor_copy(out=self.IDB[:, :], in_=idf[:, :]), reads=[self.rIDF], writes=[self.rIDB])

    def prepass(self):
        S = self.S
        pairs = []
        for w in range(4):
            pairs.append((self.w_ffn_in[w], self.s_ffn_in[w]))
            pairs.append((self.w_ffn_out[w], self.s_ffn_out[w]))
        pairs += getattr(self, "extra_pairs", [])
        for src, dst in pairs:
            rows, cols = src.shape
            step = max(1, (1 << 20) // cols)
            for r0 in range(0, rows, step):
                r1 = min(rows, r0 + step)
                S.dma("pool", out=dst[r0:r1, :], in_=src[r0:r1, :], reads=[], writes=[self.R_scr], sem="pre")

    def macro_tile(self, xin, yout, t0, NC, prompt, first, last):
        S = self.S
        T = 8 * NC
        X8 = self.X8
        S.dma("sp", out=X8[:NC, :, :], in_=xin[t0:t0 + T, :].rearrange("(c j) d -> c j d", j=8),
              reads=[], writes=[self.rX8], sem="xin")
        subt = [(0, 4), (4, 4)] if NC == 128 else [(0, 8)]
        for l in range(2):
            if "ffn" in self.stages:
                self.norm_fm(NC, 3 * l + 0)
                self.ffn(NC, 2 * l + 0, subt)
            if l == 0 and "mix0" in self.stages:
                self.mixer0(NC, t0, prompt, first, last)
            if l == 1 and "mix1" in self.stages:
                self.mixer1(NC, prompt, first, last)
            if "ffn" in self.stages:
                self.norm_fm(NC, 3 * l + 2)
                self.ffn(NC, 2 * l + 1, subt)
        S.dma("sp", out=yout[t0:t0 + T, :].rearrange("(c j) d -> c j d", j=8), in_=X8[:NC, :, :],
              reads=[self.rX8], writes=[], sem="yout")

    def norm_stats(self, NC):
        S = self.S
        X8, XS, SS, RS = self.X8, self.XS, self.SS, self.RS
        S.op("dve", lambda e: e.memset(SS[:NC, 0:8], 0.0), writes=[self.rSS])
        for j in range(8):
            S.op("act", lambda e, j=j: e.activation(out=XS[:NC, j, :], in_=X8[:NC, j, :], func=AF.Square,
                                                      accum_out=SS[:NC, j:j + 1]),
                 reads=[self.rX8, self.rSS], writes=[self.rXS, self.rSS])
        S.op("dve", lambda e: e.tensor_scalar(out=RS[:NC, 0:8], in0=SS[:NC, 0:8], scalar1=1.0 / D, scalar2=EPS,
                                              op0=ALU.mult, op1=ALU.add), reads=[self.rSS], writes=[self.rRS])
        S.op("act", lambda e: e.activation(out=RS[:NC, 0:8], in_=RS[:NC, 0:8], func=AF.Sqrt),
             reads=[self.rRS], writes=[self.rRS])
        S.op("dve", lambda e: e.reciprocal(out=RS[:NC, 0:8], in_=RS[:NC, 0:8]), reads=[self.rRS], writes=[self.rRS])
        for j in range(8):
            S.op("pool", lambda e, j=j: e.tensor_scalar(out=XS[:NC, j, :], in0=X8[:NC, j, :],
                                                         scalar1=RS[:NC, j:j + 1], scalar2=None, op0=ALU.mult),
                 reads=[self.rX8, self.rRS], writes=[self.rXS.sub(j * 2048, (j + 1) * 2048)])

    def norm_fm(self, NC, gidx):
        S = self.S
        self.norm_stats(NC)
        T = 8 * NC
        for dk in range(8):
            for jh in range(2):
                pi = self.next_ps()
                pt = self.PS[pi][:, 0:256].bitcast(BF16).rearrange("p (j c) -> p j c", j=4)
                for jl in range(4):
                    j = 4 * jh + jl
                    S.op("pe", lambda e, jl=jl, j=j: e.transpose(out=pt[:, jl, :NC], in_=self.XS[:NC, j, dk * 128:(dk + 1) * 128],
                                                                 identity=self.IDB[:NC, :NC]),
                         reads=[self.rXS.sub(j * 2048, (j + 1) * 2048), self.rIDB], writes=[self.rPS[pi]], sig=(jl == 3))
                dst = self.XNT[:, dk, 0:T].rearrange("p (c j) -> p j c", j=8)[:, 4 * jh:4 * jh + 4, :]
                g = self.GAM[:, gidx, dk:dk + 1]
                wr = [self.rXNT.sub(dk * 2048, dk * 2048 + T * 2)]
                if (dk + jh) % 2 == 0:
                    S.op("act", lambda e: e.activation(out=dst, in_=pt[:, :, :NC], func=AF.Copy, scale=g),
                         reads=[self.rPS[pi], self.rGAM], writes=wr)
                else:
                    S.op("dve", lambda e: e.tensor_scalar(out=dst, in0=pt[:, :, :NC], scalar1=g, scalar2=None, op0=ALU.mult),
                         reads=[self.rPS[pi], self.rGAM], writes=wr)

    def ffn(self, NC, widx, subt):
        S = self.S
        T = 8 * NC
        for f0 in range(0, NF, 11):
            S.dma("sp", out=self.WOUT[:, f0:f0 + 11, :],
                  in_=self.s_ffn_out[widx][:, f0 * 1024:(f0 + 11) * 1024].rearrange("p (f n) -> p f n", f=11),
                  reads=[self.R_scr], writes=[self.rWOUT.sub(f0 * 2048, (f0 + 11) * 2048)], sem="wout%d" % (f0 // 11))
        for (j0, nj) in subt:
            N = NC * nj
            rhs = [self.XNT[:, dk, 0:T].rearrange("p (c j) -> p c j", j=8)[:, :, j0:j0 + nj] for dk in range(8)]
            rd_x = [self.rXNT]
            for f in range(NF):
                sl = self.win_rr
                self.win_rr = (self.win_rr + 1) % 2
                S.dma("sp", out=self.WIN[sl][:, :, :],
                      in_=self.s_ffn_in[widx][f * 128:(f + 1) * 128, :].rearrange("p (k n) -> p k n", k=8),
                      reads=[self.R_scr], writes=[self.rWIN[sl]], sem="win%d" % sl)
                pg, pu = self.next_ps(), self.next_ps()
                for half, pi in ((0, pg), (1, pu)):
                    out = self.PS[pi][:, 0:N].rearrange("p (c j) -> p c j", j=nj)
                    for dk in range(8):
                        S.op("pe", lambda e, dk=dk, out=out, half=half: e.matmul(
                            out, lhsT=self.WIN[sl][:, dk, half * 128:(half + 1) * 128], rhs=rhs[dk],
                            start=(dk == 0), stop=(dk == 7)),
                            reads=[self.rWIN[sl]] + rd_x, writes=[self.rPS[pi]], sig=(dk == 7))
                sg = self.sg_rr
                self.sg_rr = (self.sg_rr + 1) % 2
                S.op("act", lambda e: e.activation(out=self.SG[sg][:, 0:N], in_=self.PS[pg][:, 0:N], func=AF.Silu),
                     reads=[self.rPS[pg]], writes=[self.rSG[sg]])
                S.op("dve", lambda e: e.tensor_tensor(out=self.H[:, f, 0:N], in0=self.SG[sg][:, 0:N],
                                                      in1=self.PS[pu][:, 0:N], op=ALU.mult),
                     reads=[self.rSG[sg], self.rPS[pu]], writes=[self.rH.sub(f * 1024, (f + 1) * 1024)])
            for jl in range(nj):
                for dh in range(2):
                    pi = self.next_ps()
                    for f in range(NF):
                        lhsT = self.H[:, f, 0:N].rearrange("p (c j) -> p c j", j=nj)[:, :, jl]
                        S.op("pe", lambda e, f=f, lhsT=lhsT: e.matmul(
                            self.PS[pi][:NC, :], lhsT=lhsT, rhs=self.WOUT[:, f, dh * 512:(dh + 1) * 512],
                            start=(f == 0), stop=(f == NF - 1)),
                            reads=[self.rH.sub(f * 1024, (f + 1) * 1024), self.rWOUT.sub(f * 2048, (f + 1) * 2048)],
                            writes=[self.rPS[pi]], sig=(f == NF - 1))
                    j = j0 + jl
                    xv = self.X8[:NC, j, dh * 512:(dh + 1) * 512]
                    S.op("dve", lambda e, xv=xv: e.scalar_tensor_tensor(out=xv, in0=self.PS[pi][:NC, :], scalar=0.5, in1=xv,
                                                                        op0=ALU.mult, op1=ALU.add),
                         reads=[self.rPS[pi], self.rX8], writes=[self.rX8])

    def setup_mix0(self):
        A, S, nc = self.A, self.S, self.nc
        z = self.zone
        ZK = 66 * 1024
        self.rKF = A.alloc(4 * 1536 * 2, at=z + ZK)
        self.rVA = A.alloc(12 * 520 * 2, at=self.rKF.hi)
        o = self.rVA.hi
        self.rMIXT = A.alloc(8 * 1024 * 2, at=o); o = self.rMIXT.hi
        self.rPT = [A.alloc(8 * 512 * 2, at=o), A.alloc(8 * 512 * 2, at=o + 8192)]; o += 16384
        self.rYB = A.alloc(8 * 512 * 2, at=o); o = self.rYB.hi
        self.mix_end = o
        o = z
        self.rABO = A.alloc(8 * 1024 * 2, at=o); o = self.rABO.hi
        self.rQF = A.alloc(4 * 1024 * 2, at=o); o = self.rQF.hi
        self.rXA = A.alloc(1028 * 4, at=o); o = self.rXA.hi
        self.rGA = A.alloc(1024 * 4, at=o); o = self.rGA.hi
        self.rXC = A.alloc(1024 * 4, at=o); o = self.rXC.hi
        self.rXCB = A.alloc(1024 * 2, at=o); o = self.rXCB.hi
        self.rT = []
        for _ in range(4):
            self.rT.append(A.alloc(1024 * 4, at=o)); o = self.rT[-1].hi
        self.rWV = A.alloc(8 * 512 * 2, at=o); o = self.rWV.hi
        assert o <= z + ZK, (o - z)
        self.KF = A.ap(self.rKF, BF16, "p (h t) -> p h t", h=4)
        self.VA = A.ap(self.rVA, BF16, "p (m h e) -> p m h e", n=12 * 520, m=12, h=8)
        self.MIXT = A.ap(self.rMIXT, BF16, "p (k t) -> p k t", k=8)
        self.PT = [A.ap(r, BF16, "p (m q) -> p m q", m=8) for r in self.rPT]
        self.YB = A.ap(self.rYB, BF16, "p (i c) -> p i c", i=8)
        self.ABO = A.ap(self.rABO, BF16, "p (k n) -> p k n", k=8)
        self.QF = A.ap(self.rQF, BF16, "p (h t) -> p h t", h=4)
        self.XA = A.ap(self.rXA, F32, n=1028)
        self.GA = A.ap(self.rGA, F32)
        self.XC = A.ap(self.rXC, F32)
        self.XCB = A.ap(self.rXCB, BF16)
        self.T = [A.ap(r, F32) for r in self.rT]
        self.WV = A.ap(self.rWV, BF16, "p (k n) -> p k n", k=8)

    def alloc_mix0_consts(self):
        A = self.A
        self.rM0 = A.alloc(self.NM0 * 4)
        self.M0 = A.ap(self.rM0, F32, n=self.NM0)
        self.rBD = A.alloc(2 * 4 * 128 * 2)
        self.BD = A.ap(self.rBD, BF16, "p (w c n) -> p w c n", w=2, c=4)
        self.rBB = A.alloc(8 * 5 * 64 * 2)
        self.BB = A.ap(self.rBB, BF16, "p (h j q) -> p h j q", h=8, j=5)
        self.rBON = A.alloc(128 * 2)
        self.BON = A.ap(self.rBON, BF16, n=128)
        self.rST = A.alloc(PAGE)
        self.ST = A.ap(self.rST, F32, n=64)

    def setup_mix0_consts(self):
        A, S = self.A, self.S
        NM0 = self.NM0
        S.dma("sp", out=self.M0[:, :], in_=self.m0c[:, :], reads=[], writes=[self.rM0], sem="const2")
        M0 = self.M0
        o = self.m0off
        stg = A.ap(self.rT[0].sub(0, 4096), F32, "p (w c n) -> p w c n", w=2, c=4)
        S.dma("sp", out=stg, in_=self.wbd[:, :].rearrange("p (w c n) -> p w c n", w=2, c=4), reads=[], writes=[self.rT[0]], sem="const3")
        S.op("dve", lambda e: e.tensor_copy(out=self.BD[:, :, :, :], in_=stg), reads=[self.rT[0]], writes=[self.rBD])
        rstg2 = Reg("sb", self.rT[1].lo, self.rT[3].hi)
        stg2 = A.ap(rstg2, F32, "p (h j q) -> p h j q", n=2560, h=8, j=5)
        S.dma("sp", out=stg2, in_=self.bblk[:, :].rearrange("p (h j q) -> p h j q", h=8, j=5), reads=[], writes=[rstg2], sem="const4")
        for h in range(8):
            S.op("dve", lambda e, h=h: e.tensor_scalar(out=self.BB[:, h, :, :], in0=stg2[:, h, :, :],
                                                        scalar1=M0[:, o["bc"] + h:o["bc"] + h + 1], scalar2=None, op0=ALU.subtract),
                 reads=[rstg2, self.rM0], writes=[self.rBB])
        S.op("act", lambda e: e.activation(out=self.BB[:, :, :, :], in_=self.BB[:, :, :, :], func=AF.Exp), reads=[self.rBB], writes=[self.rBB])
        S.op("pool", lambda e: e.memset(self.BON[:, :], 0.0), writes=[self.rBON])
        S.op("pool", lambda e: e.memset(self.BON[0:64, 0:64], 1.0), writes=[self.rBON])
        S.op("pool", lambda e: e.memset(self.BON[64:128, 64:128], 1.0), writes=[self.rBON])
        c1 = M0[:, o["c1"]:o["c1"] + 4]
        S.op("act", lambda e: e.activation(out=c1, in_=M0[:, o["lam"]:o["lam"] + 4], func=AF.Exp, scale=-1.0),
             reads=[self.rM0], writes=[self.rM0])
        S.op("act", lambda e: e.activation(out=c1, in_=c1, func=AF.Ln, bias=1.0), reads=[self.rM0], writes=[self.rM0])
        S.op("dve", lambda e: e.tensor_scalar(out=c1, in0=c1, scalar1=-4.0, scalar2=None, op0=ALU.mult),
             reads=[self.rM0], writes=[self.rM0])
        for nm in ("ba", "bx"):
            v = M0[:, o[nm]:o[nm] + 4]
            S.op("dve", lambda e, v=v: e.tensor_scalar(out=v, in0=v, scalar1=0.5, scalar2=None, op0=ALU.mult),
                 reads=[self.rM0], writes=[self.rM0])
        v = M0[:, o["gq"]:o["gq"] + 1]
        S.op("dve", lambda e: e.tensor_scalar(out=v, in0=v, scalar1=0.125, scalar2=None, op0=ALU.mult),
             reads=[self.rM0], writes=[self.rM0])
        S.op("pool", lambda e: e.memset(self.VA[:, :, :, 64:65], 1.0), writes=[self.rVA])

    def inproj_fm(self, NC, oc, consume):
        S = self.S
        T = 8 * NC
        sl = self.win_rr
        self.win_rr = (self.win_rr + 1) % 2
        w = self.WIN[sl][:, :, 0:128]
        S.dma("sp", out=w, in_=self.s_abin[oc * 128:(oc + 1) * 128, :].rearrange("p (k n) -> p k n", k=8),
              reads=[self.R_scr], writes=[self.rWIN[sl]], sem="win%d" % sl)
        for c0 in range(0, T, 512):
            n = min(512, T - c0)
            pi = self.next_ps()
            for dk in range(8):
                S.op("pe", lambda e, dk=dk: e.matmul(self.PS[pi][:, 0:n], lhsT=w[:, dk, :], rhs=self.XNT[:, dk, c0:c0 + n],
                                                     start=(dk == 0), stop=(dk == 7)),
                     reads=[self.rWIN[sl], self.rXNT], writes=[self.rPS[pi]], sig=(dk == 7))
            consume(pi, c0, n)

    def mixer0(self, NC, t0, prompt, first, last):
        S, A = self.S, self.A
        T = 8 * NC
        M0, o = self.M0, self.m0off
        self.norm_fm(NC, 1)
        ST = self.ST
        S.dma("sp", out=self.ABO[:, :, :], in_=self.s_about[:, :].rearrange("p (k n) -> p k n", k=8),
              reads=[self.R_scr], writes=[self.rABO], sem="abo")
        S.dma("sp", out=self.WV[:, :, :], in_=self.s_abv[:, :].rearrange("p (k n) -> p k n", k=8),
              reads=[self.R_scr], writes=[self.rWV], sem="wv")
        if first:
            if prompt:
                S.op("dve", lambda e: e.memset(ST[:, 0:16], 0.0), writes=[self.rST])
            else:
                S.dma("sp", out=ST[:, 0:16], in_=self.st_rg[:, :], reads=[], writes=[self.rST], sem="const5")
        if os.environ.get("K_MARK"): print("MARK rg", S.nops)
        XA, GA, XC, XCB, Tt = self.XA, self.GA, self.XC, self.XCB, self.T
        rT = self.rT
        for c in range(4):
            if "rg" in _SKIP:
                S.op("dve", lambda e: e.memset(self.MIXT[:, c, 0:T], 0.0), writes=[self.rMIXT.sub(c * 2048, (c + 1) * 2048)])
                continue
            S.op("dve", lambda e: e.tensor_copy(out=XA[:, 0:3], in_=ST[:, 4 + 3 * c:7 + 3 * c]), reads=[self.rST], writes=[self.rXA])
            self.inproj_fm(NC, c, lambda pi, c0, n: S.op(
                "act", lambda e: e.activation(out=XA[:, 3 + c0:3 + c0 + n], in_=self.PS[pi][:, 0:n], func=AF.Copy),
                reads=[self.rPS[pi]], writes=[self.rXA]))
            self.inproj_fm(NC, 4 + c, lambda pi, c0, n: S.op(
                "act", lambda e: e.activation(out=GA[:, c0:c0 + n], in_=self.PS[pi][:, 0:n], func=AF.Copy),
                reads=[self.rPS[pi]], writes=[self.rGA]))
            S.op("dve", lambda e: e.tensor_copy(out=ST[:, 4 + 3 * c:7 + 3 * c], in_=XA[:, T:T + 3]), reads=[self.rXA], writes=[self.rST])
            cw = lambda k: M0[:, o["cw"] + 4 * c + k:o["cw"] + 4 * c + k + 1]
            S.op("dve", lambda e: e.tensor_scalar(out=XC[:, 0:T], in0=XA[:, 0:T], scalar1=cw(0),
                                                   scalar2=M0[:, o["cb"] + c:o["cb"] + c + 1], op0=ALU.mult, op1=ALU.add),
                 reads=[self.rXA, self.rM0], writes=[self.rXC])
            for k in range(1, 4):
                S.op("dve", lambda e, k=k: e.scalar_tensor_tensor(out=XC[:, 0:T], in0=XA[:, k:k + T], scalar=cw(k), in1=XC[:, 0:T],
                                                                   op0=ALU.mult, op1=ALU.add),
                     reads=[self.rXA, self.rXC, self.rM0], writes=[self.rXC])
            S.op("pool", lambda e: e.tensor_copy(out=XCB[:, 0:T], in_=XC[:, 0:T]), reads=[self.rXC], writes=[self.rXCB])
            for c0 in range(0, T, 512):
                n = min(512, T - c0)
                for w, dst, bn in ((0, Tt[0], "ba"), (1, Tt[1], "bx")):
                    pi = self.next_ps()
                    S.op("pe", lambda e, w=w: e.matmul(self.PS[pi][:, 0:n], lhsT=self.BD[:, w, c, :], rhs=XCB[:, c0:c0 + n],
                                                       start=True, stop=True),
                         reads=[self.rBD, self.rXCB], writes=[self.rPS[pi]])
                    S.op("act", lambda e, dst=dst, bn=bn: e.activation(out=dst[:, c0:c0 + n], in_=self.PS[pi][:, 0:n], func=AF.Tanh,
                                                                       scale=0.5, bias=M0[:, o[bn] + c:o[bn] + c + 1]),
                         reads=[self.rPS[pi], self.rM0], writes=[rT[w]])
            c1 = M0[:, o["c1"] + c:o["c1"] + c + 1]
            S.op("act", lambda e: e.activation(out=Tt[2][:, 0:T], in_=Tt[0][:, 0:T], func=AF.Exp, scale=c1, bias=c1),
                 reads=[rT[0], self.rM0], writes=[rT[2]])
            S.op("act", lambda e: e.activation(out=Tt[3][:, 0:T], in_=Tt[0][:, 0:T], func=AF.Tanh, scale=c1, bias=c1),
                 reads=[rT[0], self.rM0], writes=[rT[3]])
            if c == 0 and first and prompt:
                self.dbg("tr", Tt[0][:, 0:T], rT[0], [128, T])
                self.dbg("a", Tt[2][:, 0:T], rT[2], [128, T])
                self.dbg("th", Tt[3][:, 0:T], rT[3], [128, T])
                self.dbg("m0", M0[:, :], self.rM0, [128, self.NM0])
                self.dbg("xc", XC[:, 0:T], self.rXC, [128, T])
            S.op("dve", lambda e: e.tensor_tensor(out=Tt[0][:, 0:T], in0=Tt[2][:, 0:T], in1=Tt[2][:, 0:T], op=ALU.mult),
                 reads=[rT[2]], writes=[rT[0]])
            S.op("dve", lambda e: e.scalar_tensor_tensor(out=Tt[0][:, 0:T], in0=Tt[0][:, 0:T], scalar=1.0, in1=Tt[3][:, 0:T],
                                                         op0=ALU.add, op1=ALU.mult), reads=[rT[0], rT[3]], writes=[rT[0]])
            S.op("dve", lambda e: e.tensor_scalar(out=Tt[0][:, 0:T], in0=Tt[0][:, 0:T], scalar1=-1.0, scalar2=0.0, op0=ALU.mult, op1=ALU.max),
                 reads=[rT[0]], writes=[rT[0]])
            S.op("act", lambda e: e.activation(out=Tt[0][:, 0:T], in_=Tt[0][:, 0:T], func=AF.Sqrt),
                 reads=[rT[0]], writes=[rT[0]])
            S.op("dve", lambda e: e.scalar_tensor_tensor(out=Tt[1][:, 0:T], in0=Tt[1][:, 0:T], scalar=1.0, in1=XC[:, 0:T],
                                                         op0=ALU.add, op1=ALU.mult), reads=[rT[1], self.rXC], writes=[rT[1]])
            S.op("dve", lambda e: e.scalar_tensor_tensor(out=Tt[1][:, 0:T], in0=Tt[1][:, 0:T], scalar=0.5, in1=Tt[0][:, 0:T],
                                                         op0=ALU.mult, op1=ALU.mult), reads=[rT[1], rT[0]], writes=[rT[1]])
            S.op("dve", lambda e: e.tensor_tensor_scan(out=Tt[3][:, 0:T], data0=Tt[2][:, 0:T], data1=Tt[1][:, 0:T],
                                                       initial=ST[:, c:c + 1], op0=ALU.mult, op1=ALU.add),
                 reads=[rT[2], rT[1], self.rST], writes=[rT[3]])
            S.op("dve", lambda e: e.tensor_copy(out=ST[:, c:c + 1], in_=Tt[3][:, T - 1:T]), reads=[rT[3]], writes=[self.rST])
            S.op("act", lambda e: e.activation(out=Tt[0][:, 0:T], in_=GA[:, 0:T], func=AF.Square), reads=[self.rGA], writes=[rT[0]])
            S.op("dve", lambda e: e.tensor_scalar(out=Tt[0][:, 0:T], in0=Tt[0][:, 0:T], scalar1=0.044715, scalar2=1.0,
                                                  op0=ALU.mult, op1=ALU.add), reads=[rT[0]], writes=[rT[0]])
            S.op("dve", lambda e: e.tensor_tensor(out=Tt[0][:, 0:T], in0=Tt[0][:, 0:T], in1=GA[:, 0:T], op=ALU.mult),
                 reads=[rT[0], self.rGA], writes=[rT[0]])
            S.op("act", lambda e: e.activation(out=Tt[0][:, 0:T], in_=Tt[0][:, 0:T], func=AF.Tanh, scale=0.7978845608028654),
                 reads=[rT[0]], writes=[rT[0]])
            S.op("dve", lambda e: e.scalar_tensor_tensor(out=Tt[0][:, 0:T], in0=Tt[0][:, 0:T], scalar=1.0, in1=GA[:, 0:T],
                                                         op0=ALU.add, op1=ALU.mult), reads=[rT[0], self.rGA], writes=[rT[0]])
            S.op("dve", lambda e: e.scalar_tensor_tensor(out=self.MIXT[:, c, 0:T], in0=Tt[0][:, 0:T], scalar=0.5, in1=Tt[3][:, 0:T],
                                                         op0=ALU.mult, op1=ALU.mult),
                 reads=[rT[0], rT[3]], writes=[self.rMIXT.sub(c * 2048, (c + 1) * 2048)])
        if last and "stout" not in _SKIP:
            oc_, oh_ = (self.o_pconv, self.o_ph) if prompt else (self.o_sconv, self.o_sh)
            S.dma("sp", out=oh_.rearrange("(c p) -> p c", p=128), in_=ST[:, 0:4], reads=[self.rST], writes=[], sem="o_st", slow=True)
            for c in range(4):
                S.dma("sp", out=oc_[:, c * 128:(c + 1) * 128].rearrange("k p -> p k"), in_=ST[:, 4 + 3 * c:7 + 3 * c],
                      reads=[self.rST], writes=[], sem="o_st", slow=True)
        if os.environ.get("K_MARK"): print("MARK qk", S.nops)
        KF, QF, VA = self.KF, self.QF, self.VA
        if first and not prompt and "cache" not in _SKIP:
            self.load_cache()
        want_kv = last and "kvout" not in _SKIP
        kv_lo = max(0, T - 512)
        for which in (0, 1):
            for hp in range(4):
                oc = 8 + 4 * which + hp
                gcol = o["gq"] if which == 0 else o["gk"]

                def consume(pi, c0, n, which=which, hp=hp, gcol=gcol):
                    S.op("act", lambda e: e.activation(out=self.SG[0][:, 0:n].bitcast(BF16)[:, 0:n], in_=self.PS[pi][:, 0:n], func=AF.Square),
                         reads=[self.rPS[pi]], writes=[self.rSG[0]])
                    p2 = self.next_ps()
                    S.op("pe", lambda e: e.matmul(self.PS[p2][:, 0:n], lhsT=self.BON[:, :], rhs=self.SG[0][:, 0:n].bitcast(BF16)[:, 0:n],
                                                  start=True, stop=True), reads=[self.rBON, self.rSG[0]], writes=[self.rPS[p2]])
                    if self.debug:
                        S.op("dve", lambda e: e.tensor_copy(out=self.T[3][:, 0:n], in_=self.PS[p2][:, 0:n]), reads=[self.rPS[p2]], writes=[self.rT[3]])
                        S.op("dve", lambda e: e.tensor_copy(out=self.T[2][:, 0:n], in_=self.PS[pi][:, 0:n]), reads=[self.rPS[pi]], writes=[self.rT[2]])
                        if which == 0 and hp == 0:
                            self.dbg("ss", self.T[3][:, 0:n], self.rT[3], [128, n])
                            self.dbg("q", self.T[2][:, 0:n], self.rT[2], [128, n])
                            self.dbg("sq", self.SG[0][:, 0:n], self.rSG[0], [128, n])
                        S.op("dve", lambda e: e.tensor_scalar(out=self.SG[1][:, 0:n], in0=self.PS[p2][:, 0:n], scalar1=0.0, scalar2=None, op0=ALU.max),
                             reads=[self.rPS[p2]], writes=[self.rSG[1]])
                        S.op("act", lambda e: e.activation(out=self.SG[1][:, 0:n], in_=self.SG[1][:, 0:n], func=AF.Sqrt, scale=1.0 / 64, bias=EPS),
                             reads=[self.rSG[1]], writes=[self.rSG[1]])
                    else:
                      S.op("act", lambda e: e.activation(out=self.SG[1][:, 0:n], in_=self.PS[p2][:, 0:n], func=AF.Sqrt, scale=1.0 / 64, bias=EPS),
                         reads=[self.rPS[p2]], writes=[self.rSG[1]])
                    S.op("dve", lambda e: e.reciprocal(out=self.SG[1][:, 0:n], in_=self.SG[1][:, 0:n]), reads=[self.rSG[1]], writes=[self.rSG[1]])
                    if which == 0:
                        S.op("dve", lambda e: e.scalar_tensor_tensor(out=QF[:, hp, c0:c0 + n], in0=self.PS[pi][:, 0:n],
                                                                     scalar=M0[:, gcol:gcol + 1], in1=self.SG[1][:, 0:n], op0=ALU.mult, op1=ALU.mult),
                             reads=[self.rPS[pi], self.rSG[1], self.rM0], writes=[self.rQF])
                    else:
                        S.op("dve", lambda e: e.scalar_tensor_tensor(out=self.SG[1][:, 0:n], in0=self.PS[pi][:, 0:n],
                                                                     scalar=M0[:, gcol:gcol + 1], in1=self.SG[1][:, 0:n], op0=ALU.mult, op1=ALU.mult),
                             reads=[self.rPS[pi], self.rSG[1], self.rM0], writes=[self.rSG[1]])
                        S.op("act", lambda e: e.activation(out=KF[:, hp, 512 + c0:512 + c0 + n], in_=self.SG[1][:, 0:n], func=AF.Copy),
                             reads=[self.rSG[1]], writes=[self.rKF])
                        if want_kv and c0 + n > kv_lo and "kout" not in _SKIP:
                            self.emit_k_out(prompt, hp, c0, n, kv_lo)
                self.inproj_fm(NC, oc, consume)
        if os.environ.get("K_MARK"): print("MARK v", S.nops)
        for tb in range((T + 127) // 128):
            rows = min(128, T - tb * 128)
            pi = self.next_ps()
            for dk in range(8):
                S.op("pe", lambda e, dk=dk: e.matmul(self.PS[pi][:rows, :], lhsT=self.XNT[:, dk, tb * 128:tb * 128 + rows], rhs=self.WV[:, dk, :],
                                                     start=(dk == 0), stop=(dk == 7)),
                     reads=[self.rXNT, self.rWV], writes=[self.rPS[pi]], sig=(dk == 7))
            S.op("act", lambda e: e.activation(out=VA[:rows, 4 + tb, :, 0:64], in_=self.PS[pi][:rows, :].rearrange("p (h e) -> p h e", h=8), func=AF.Copy),
                 reads=[self.rPS[pi]], writes=[self.rVA])
            if want_kv and tb * 128 + rows > kv_lo and "vout" not in _SKIP:
                veng = "act"
                if veng == "act":
                    S.op("act", lambda e: e.activation(out=self.SG[0][:rows, :], in_=self.PS[pi][:rows, :], func=AF.Copy), reads=[self.rPS[pi]], writes=[self.rSG[0]])
                else:
                    S.op("dve", lambda e: e.tensor_copy(out=self.SG[0][:rows, :], in_=self.PS[pi][:rows, :]), reads=[self.rPS[pi]], writes=[self.rSG[0]])
                ov = self.o_pv if prompt else self.o_sv
                r0 = tb * 128 - kv_lo
                S.dma("sp", out=ov[r0:r0 + rows, :], in_=self.SG[0][:rows, :], reads=[self.rSG[0]], writes=[], sem="o_sg0")
        if os.environ.get("K_MARK"): print("MARK attn", S.nops)
        nq = T // 64
        KCn = 8 + nq
        kmin = 8 if (prompt and first) else 0
        bc = o["bc"]
        for qs in range(0, nq, 8):
            if "attn" in _SKIP:
                for cc in range(4):
                    S.op("dve", lambda e, cc=cc: e.memset(self.MIXT[:, 4 + cc, 0:T], 0.0), writes=[self.rMIXT.sub((4 + cc) * 2048, (5 + cc) * 2048)])
                break
            qe = min(nq, qs + 8)
            blocks = [m for m in range(qs // 2, (qe + 8 + 1) // 2) if 2 * m >= kmin and 2 * m < KCn]
            for h in range(8):
                hp, base = h // 2, 64 * (h % 2)
                ptb = h % 2
                PTh = self.PT[ptb]
                binfo = {}
                for m in blocks:
                    rows = 128 if 2 * m + 1 < KCn else 64
                    i0, i1 = max(qs, 2 * m - 8), min(qe - 1, 2 * m + 1)
                    n = (i1 - i0 + 1) * 64
                    ml = m - blocks[0]
                    binfo[m] = (rows, i0, ml)
                    pi = self.next_ps()
                    ja, jb = max(0, i0 - (2 * m - 8)), min(4, i1 - (2 * m - 8))
                    hasb = jb >= ja and "nobias" not in _SKIP
                    S.op("pe", lambda e: e.matmul(self.PS[pi][:rows, 0:n], lhsT=KF[base:base + 64, hp, m * 128:m * 128 + rows],
                                                  rhs=QF[base:base + 64, hp, i0 * 64:(i1 + 1) * 64], start=True, stop=True),
                         reads=[self.rKF, self.rQF], writes=[self.rPS[pi]])
                    S.op("act", lambda e: e.activation(out=PTh[:rows, ml, 0:n], in_=self.PS[pi][:rows, 0:n], func=AF.Exp,
                                                       bias=M0[:rows, bc + h:bc + h + 1]),
                         reads=[self.rPS[pi], self.rM0], writes=[self.rPT[ptb].sub(ml * 1024, (ml + 1) * 1024)])
                    if hasb:
                        ia = ja + 2 * m - 8
                        nb = (jb - ja + 1) * 64
                        pv_ = PTh[:rows, ml, (ia - i0) * 64:(ia - i0) * 64 + nb].rearrange("p (j q) -> p j q", q=64)
                        S.op("dve", lambda e: e.tensor_tensor(out=pv_, in0=pv_, in1=self.BB[:rows, h, ja:jb + 1, :], op=ALU.mult),
                             reads=[self.rBB, self.rPT[ptb].sub(ml * 1024, (ml + 1) * 1024)], writes=[self.rPT[ptb].sub(ml * 1024, (ml + 1) * 1024)])
                for i in range(qs, qe):
                    if i % 2 == 1 and (i - 1) // 2 in binfo:
                        rows_, i0_, ml_ = binfo[(i - 1) // 2]
                        if rows_ == 128:
                            zc = PTh[0:64, ml_, (i - i0_) * 64:(i - i0_ + 1) * 64]
                            S.op("dve", lambda e, zc=zc: e.memset(zc, 0.0), writes=[self.rPT[ptb].sub(ml_ * 1024, (ml_ + 1) * 1024)])
                for g0 in range(qs, qe, 4):
                    if "pv" in _SKIP:
                        S.op("dve", lambda e: e.memset(self.YB[:64, :, h * 64:(h + 1) * 64], 0.0), writes=[self.rYB])
                        break
                    g1 = min(qe, g0 + 4)
                    pi = self.next_ps()
                    pv = self.PS[pi][:64, 0:260].rearrange("p (i e) -> p i e", e=65)
                    for i in range(g0, g1):
                        ms = [m for m in range(i // 2, (i + 8) // 2 + 1) if m in binfo]
                        parts = []
                        for m in ms:
                            rows, i0, ml = binfo[m]
                            r0 = 0 if (i <= 2 * m <= i + 8) else 64
                            r1 = 128 if (i <= 2 * m + 1 <= i + 8 and rows == 128) else 64
                            if r1 > r0:
                                if r0 == 64:
                                    r0 = 0
                                parts.append((m, r0, r1, i0, ml))
                        for pi_, (m, r0, r1, i0, ml) in enumerate(parts):
                            S.op("pe", lambda e, m=m, r0=r0, r1=r1, i0=i0, ml=ml, pi_=pi_: e.matmul(
                                pv[:, i - g0, :], lhsT=PTh[r0:r1, ml, (i - i0) * 64:(i - i0 + 1) * 64], rhs=VA[r0:r1, m, h, :],
                                start=(pi_ == 0), stop=(pi_ == len(parts) - 1)),
                                reads=[self.rPT[ptb].sub(ml * 1024, (ml + 1) * 1024), self.rVA], writes=[self.rPS[pi]],
                                sig=(pi_ == len(parts) - 1 and i == g1 - 1))
                    ng = g1 - g0
                    rd = self.RS[:64, 8:8 + ng]
                    S.op("dve", lambda e: e.reciprocal(out=rd, in_=pv[:, 0:ng, 64]), reads=[self.rPS[pi]], writes=[self.rRS])
                    S.op("dve", lambda e: e.tensor_tensor(out=self.YB[:64, g0 - qs:g1 - qs, h * 64:(h + 1) * 64], in0=pv[:, 0:ng, 0:64],
                                                          in1=rd.unsqueeze(2).to_broadcast([64, ng, 64]), op=ALU.mult),
                         reads=[self.rPS[pi], self.rRS], writes=[self.rYB])
            nqh = qe - qs
            for cc in range(4):
                pi = self.next_ps()
                pt = self.PS[pi][:, 0:256].bitcast(BF16).rearrange("p (i q) -> p i q", i=8)
                for i in range(nqh):
                    S.op("pe", lambda e, i=i: e.transpose(out=pt[:, i, :], in_=self.YB[:64, i, cc * 128:(cc + 1) * 128], identity=self.IDB[:64, :64]),
                         reads=[self.rYB, self.rIDB], writes=[self.rPS[pi]], sig=(i == nqh - 1))
                S.op("act", lambda e: e.activation(out=self.MIXT[:, 4 + cc, qs * 64:qe * 64].rearrange("p (i q) -> p i q", q=64),
                                                   in_=pt[:, 0:nqh, :], func=AF.Copy),
                     reads=[self.rPS[pi]], writes=[self.rMIXT.sub((4 + cc) * 2048, (5 + cc) * 2048)])
        if not last:
            S.op("pool", lambda e: e.tensor_copy(out=KF[:, :, 0:512], in_=KF[:, :, T:T + 512]), reads=[self.rKF], writes=[self.rKF])
            S.op("pool", lambda e: e.tensor_copy(out=VA[:, 0:4, :, 0:64], in_=VA[:, T // 128:T // 128 + 4, :, 0:64]), reads=[self.rVA], writes=[self.rVA])
        if os.environ.get("K_MARK"): print("MARK outproj", S.nops)
        for j in range(8):
            for dh in range(2):
                pi = self.next_ps()
                for kc in range(8):
                    lhsT = self.MIXT[:, kc, 0:T].rearrange("p (c j) -> p c j", j=8)[:, :, j]
                    S.op("pe", lambda e, kc=kc, lhsT=lhsT: e.matmul(self.PS[pi][:NC, :], lhsT=lhsT, rhs=self.ABO[:, kc, dh * 512:(dh + 1) * 512],
                                                                    start=(kc == 0), stop=(kc == 7)),
                         reads=[self.rMIXT, self.rABO], writes=[self.rPS[pi]], sig=(kc == 7))
                xv = self.X8[:NC, j, dh * 512:(dh + 1) * 512]
                S.op("dve", lambda e, xv=xv: e.tensor_tensor(out=xv, in0=self.PS[pi][:NC, :], in1=xv, op=ALU.add),
                     reads=[self.rPS[pi], self.rX8], writes=[self.rX8])

    def emit_k_out(self, prompt, hp, c0, n, kv_lo):
        S = self.S
        ok = self.o_pk if prompt else self.o_sk
        for b0 in range(0, n, 128):
            nb = min(128, n - b0)
            t_lo = c0 + b0
            if t_lo + nb <= kv_lo:
                continue
            pi = self.next_ps()
            S.op("pe", lambda e: e.matmul(self.PS[pi][:nb, 0:128], lhsT=self.SG[1][:, b0:b0 + nb], rhs=self.IDF[:, :], start=True, stop=True),
                 reads=[self.rSG[1], self.rIDF], writes=[self.rPS[pi]])
            S.op("dve", lambda e: e.tensor_copy(out=self.SG[0][:nb, 0:128], in_=self.PS[pi][:nb, 0:128]), reads=[self.rPS[pi]], writes=[self.rSG[0]])
            r0 = t_lo - kv_lo
            S.dma("sp", out=ok[r0:r0 + nb, hp * 128:(hp + 1) * 128], in_=self.SG[0][:nb, 0:128], reads=[self.rSG[0]], writes=[], sem="o_sg0")

    def load_cache(self):
        S = self.S
        kb16 = self.SG[1][:, 0:256].bitcast(BF16)
        for kb in range(4):
            S.dma("sp", out=self.SG[0][:, 0:512], in_=self.ck[kb * 128:(kb + 1) * 128, :], reads=[], writes=[self.rSG[0]], sem="ck")
            S.op("dve", lambda e: e.tensor_copy(out=kb16, in_=self.SG[0][:, 0:512]), reads=[self.rSG[0]], writes=[self.rSG[1]])
            for hp in range(4):
                pi = self.next_ps()
                pt = self.PS[pi][:, 0:64].bitcast(BF16)
                S.op("pe", lambda e, hp=hp, pt=pt: e.transpose(out=pt, in_=kb16[:, hp * 128:(hp + 1) * 128], identity=self.IDB[:, :]),
                     reads=[self.rSG[1], self.rIDB], writes=[self.rPS[pi]])
                S.op("act", lambda e, hp=hp, pt=pt: e.activation(out=self.KF[:, hp, kb * 128:(kb + 1) * 128], in_=pt, func=AF.Copy),
                     reads=[self.rPS[pi]], writes=[self.rKF])
            S.dma("sp", out=self.SG[0][:, 0:512], in_=self.cv[kb * 128:(kb + 1) * 128, :], reads=[], writes=[self.rSG[0]], sem="ck")
            S.op("dve", lambda e: e.tensor_copy(out=self.VA[:, kb, :, 0:64], in_=self.SG[0][:, 0:512].rearrange("p (h e) -> p h e", h=8)),
                 reads=[self.rSG[0]], writes=[self.rVA])

    def alloc_mix1_consts(self):
        A = self.A
        self.rS5 = A.alloc(16 * 64 * 4)
        self.S5 = A.ap(self.rS5, F32, "p (s g) -> p s g", s=16)
        self.rDFM = A.alloc(PAGE)
        self.DFM = A.ap(self.rDFM, F32, n=8)
        self.rSST = A.alloc(128 * 4)
        self.SST = A.ap(self.rSST, F32, n=128)

    def setup_mix1(self):
        A, S = self.A, self.S
        z = self.zone
        self.rBU = A.alloc(2 * 64 * 128 * 2, at=z)
        self.rHI = A.alloc(2 * 64 * 128 * 2, at=z + 32768)
        o = z + 66 * 1024 + (4 * 1536 * 2) + ((12 * 520 * 2 + PAGE - 1) // PAGE * PAGE)
        self.rBP = A.alloc(64 * 2 * 64 * 2, at=o); o = self.rBP.hi
        self.rCP = A.alloc(2 * 64 * 16 * 2, at=o); o = self.rCP.hi
        self.rGW = [A.alloc(8 * 512 * 2, at=o), A.alloc(8 * 512 * 2, at=o + 8192)]; o += 16384
        self.rYT = A.alloc(1024 * 2, at=o); o = self.rYT.hi
        self.rTM = A.alloc(3 * 128 * 4, at=o); o = self.rTM.hi
        self.BU = A.ap(self.rBU, BF16, "p (r g t) -> p r g t", r=2, g=64)
        self.HI = A.ap(self.rHI, BF16, "p (r g t) -> p r g t", r=2, g=64)
        self.BP = A.ap(self.rBP, BF16, "p (g r q) -> p g r q", g=64, r=2)
        self.CP = A.ap(self.rCP, BF16, "p (r g i) -> p r g i", r=2, g=64)
        self.GW = [A.ap(r, BF16, "p (k n) -> p k n", k=8) for r in self.rGW]
        self.YT = A.ap(self.rYT, BF16, n=1024)
        self.TM = A.ap(self.rTM, F32, "p (s w) -> p s w", n=384, s=3)
        S5 = self.S5
        stg = A.ap(Reg("sb", z, z + 3 * 64 * 4), F32, "p (s g) -> p s g", n=192, s=3)
        rstg = Reg("sb", z, z + 1024)
        S.dma("sp", out=stg[0:64], in_=self.s5p[:, :].rearrange("p (s g) -> p s g", s=3), reads=[], writes=[rstg], sem="c5a")
        S.dma("sp", out=self.DFM[:, :], in_=self.dfm[:, :], reads=[], writes=[self.rDFM], sem="c5b")
        rS5 = self.rS5
        P64 = slice(0, 64)

        def dv(fn, rd=(rS5, rstg), wr=(rS5,)):
            S.op("dve", fn, reads=list(rd), writes=list(wr))

        def ac(fn, rd=(rS5, rstg), wr=(rS5,)):
            S.op("act", fn, reads=list(rd), writes=list(wr))
        are, aim, ldt = stg[P64, 0, :], stg[P64, 1, :], stg[P64, 2, :]
        sl = lambda k: S5[P64, k, :]
        ac(lambda e: e.activation(out=sl(0), in_=ldt, func=AF.Exp))
        dv(lambda e: e.tensor_tensor(out=sl(1), in0=are, in1=sl(0), op=ALU.mult))
        dv(lambda e: e.tensor_tensor(out=sl(2), in0=aim, in1=sl(0), op=ALU.mult))
        ac(lambda e: e.activation(out=sl(3), in_=sl(1), func=AF.Exp))
        TWO_PI = 2.0 * np.pi
        I32 = mybir.dt.int32

        def sin_of(dst, shift):
            dv(lambda e: e.tensor_scalar(out=sl(13), in0=sl(2), scalar1=shift, scalar2=None, op0=ALU.add))
            dv(lambda e: e.tensor_scalar(out=sl(14), in0=sl(13), scalar1=1.0 / TWO_PI, scalar2=None, op0=ALU.mult))
            dv(lambda e: e.tensor_copy(out=sl(15).bitcast(I32), in_=sl(14)))
            dv(lambda e: e.tensor_copy(out=sl(14), in_=sl(15).bitcast(I32)))
            dv(lambda e: e.scalar_tensor_tensor(out=sl(13), in0=sl(14), scalar=-TWO_PI, in1=sl(13), op0=ALU.mult, op1=ALU.add))
            dv(lambda e: e.tensor_scalar(out=sl(14), in0=sl(13), scalar1=float(np.pi), scalar2=None, op0=ALU.is_gt))
            dv(lambda e: e.scalar_tensor_tensor(out=sl(13), in0=sl(14), scalar=-TWO_PI, in1=sl(13), op0=ALU.mult, op1=ALU.add))
            dv(lambda e: e.tensor_scalar(out=sl(14), in0=sl(13), scalar1=-float(np.pi), scalar2=None, op0=ALU.is_lt))
            dv(lambda e: e.scalar_tensor_tensor(out=sl(13), in0=sl(14), scalar=TWO_PI, in1=sl(13), op0=ALU.mult, op1=ALU.add))
            ac(lambda e: e.activation(out=dst, in_=sl(13), func=AF.Sin))
        sin_of(sl(4), 0.0)
        sin_of(sl(5), float(np.pi / 2))
        dv(lambda e: e.tensor_tensor(out=sl(6), in0=sl(3), in1=sl(5), op=ALU.mult))
        dv(lambda e: e.tensor_tensor(out=sl(7), in0=sl(3), in1=sl(4), op=ALU.mult))
        dv(lambda e: e.tensor_scalar(out=sl(8), in0=sl(7), scalar1=-1.0, scalar2=None, op0=ALU.mult))
        dv(lambda e: e.tensor_tensor(out=sl(15), in0=are, in1=are, op=ALU.mult))
        dv(lambda e: e.tensor_tensor(out=sl(13), in0=aim, in1=aim, op=ALU.mult))
        dv(lambda e: e.tensor_tensor(out=sl(15), in0=sl(15), in1=sl(13), op=ALU.add))
        dv(lambda e: e.reciprocal(out=sl(15), in_=sl(15)))
        dv(lambda e: e.tensor_scalar(out=sl(14), in0=sl(6), scalar1=-1.0, scalar2=None, op0=ALU.add))
        dv(lambda e: e.tensor_tensor(out=sl(9), in0=sl(14), in1=are, op=ALU.mult))
        dv(lambda e: e.tensor_tensor(out=sl(13), in0=sl(7), in1=aim, op=ALU.mult))
        dv(lambda e: e.tensor_tensor(out=sl(9), in0=sl(9), in1=sl(13), op=ALU.add))
        dv(lambda e: e.tensor_tensor(out=sl(9), in0=sl(9), in1=sl(15), op=ALU.mult))
        dv(lambda e: e.tensor_tensor(out=sl(10), in0=sl(7), in1=are, op=ALU.mult))
        dv(lambda e: e.tensor_tensor(out=sl(13), in0=sl(14), in1=aim, op=ALU.mult))
        dv(lambda e: e.tensor_tensor(out=sl(10), in0=sl(10), in1=sl(13), op=ALU.subtract))
        dv(lambda e: e.tensor_tensor(out=sl(10), in0=sl(10), in1=sl(15), op=ALU.mult))
        dv(lambda e: e.tensor_tensor(out=sl(15), in0=sl(9), in1=sl(9), op=ALU.mult))
        dv(lambda e: e.tensor_tensor(out=sl(13), in0=sl(10), in1=sl(10), op=ALU.mult))
        dv(lambda e: e.tensor_tensor(out=sl(15), in0=sl(15), in1=sl(13), op=ALU.add))
        dv(lambda e: e.reciprocal(out=sl(15), in_=sl(15)))
        dv(lambda e: e.tensor_tensor(out=sl(11), in0=sl(9), in1=sl(15), op=ALU.mult))
        dv(lambda e: e.tensor_tensor(out=sl(12), in0=sl(10), in1=sl(15), op=ALU.mult))
        dv(lambda e: e.tensor_scalar(out=sl(12), in0=sl(12), scalar1=-1.0, scalar2=None, op0=ALU.mult))
        rc = Reg("sb", z + 4096, z + 4096 + 2 * 64 * 16 * 4)
        cst = A.ap(rc, F32, "p (r g i) -> p r g i", r=2, g=64)
        S.dma("sp", out=cst[P64], in_=self.s5c[:, :].rearrange("p (r g i) -> p r g i", r=2, g=64), reads=[], writes=[rc], sem="c5c")
        ro = Reg("sb", z + 16384, z + 16384 + 2 * 64 * 16 * 4)
        cot = A.ap(ro, F32, "p (r g i) -> p r g i", r=2, g=64)
        rt = Reg("sb", z + 28672, z + 28672 + 64 * 16 * 4)
        tmp = A.ap(rt, F32, "p (g i) -> p g i", g=64)
        bc = lambda k: S5[P64, k, :].unsqueeze(2).to_broadcast([64, 64, 16])
        S.op("dve", lambda e: e.tensor_tensor(out=cot[P64, 0], in0=cst[P64, 0], in1=bc(9), op=ALU.mult), reads=[rc, rS5], writes=[ro])
        S.op("dve", lambda e: e.tensor_tensor(out=tmp[P64], in0=cst[P64, 1], in1=bc(10), op=ALU.mult), reads=[rc, rS5], writes=[rt])
        S.op("dve", lambda e: e.tensor_tensor(out=cot[P64, 0], in0=cot[P64, 0], in1=tmp[P64], op=ALU.subtract), reads=[ro, rt], writes=[ro])
        S.op("dve", lambda e: e.tensor_tensor(out=cot[P64, 1], in0=cst[P64, 0], in1=bc(10), op=ALU.mult), reads=[rc, rS5], writes=[ro])
        S.op("dve", lambda e: e.tensor_tensor(out=tmp[P64], in0=cst[P64, 1], in1=bc(9), op=ALU.mult), reads=[rc, rS5], writes=[rt])
        S.op("dve", lambda e: e.tensor_tensor(out=cot[P64, 1], in0=cot[P64, 1], in1=tmp[P64], op=ALU.add), reads=[ro, rt], writes=[ro])
        S.op("dve", lambda e: e.tensor_scalar(out=cot[P64, 1], in0=cot[P64, 1], scalar1=-1.0, scalar2=None, op0=ALU.mult), reads=[ro], writes=[ro])
        rcb = Reg("sb", z + 36864, z + 36864 + 2 * 64 * 16 * 2)
        cob = A.ap(rcb, BF16, "p (r g i) -> p r g i", r=2, g=64)
        S.op("dve", lambda e: e.tensor_copy(out=cob[P64], in_=cot[P64]), reads=[ro], writes=[rcb])
        S.dma("sp", out=self.s_cp[:, :].rearrange("p (r g i) -> p r g i", r=2, g=64), in_=cob[P64], reads=[rcb], writes=[self.R_scr], sem="c5d")

    def mixer1(self, NC, prompt, first, last):
        S, A = self.S, self.A
        T = 8 * NC
        S5, SST, TM = self.S5, self.SST, self.TM
        P64 = slice(0, 64)
        self.norm_fm(NC, 4)
        S.dma("sp", out=self.BP[:, :, :, :], in_=self.s_bp[:, :].rearrange("p (g r q) -> p g r q", g=64, r=2), reads=[self.R_scr], writes=[self.rBP], sem="bp")
        S.dma("sp", out=self.CP[P64], in_=self.s_cp[:, :].rearrange("p (r g i) -> p r g i", r=2, g=64), reads=[self.R_scr], writes=[self.rCP], sem="cp")
        rS5, rSST, rTM = self.rS5, self.rSST, self.rTM
        sre, sim = SST[P64, 0:64], SST[P64, 64:128]
        if first:
            if prompt:
                S.op("dve", lambda e: e.memset(SST[P64, :], 0.0), writes=[rSST])
            else:
                S.dma("sp", out=TM[P64, 0, :], in_=self.s0[:, :], reads=[], writes=[rTM], sem="s0")
                a, b = TM[P64, 0, 0:64], TM[P64, 0, 64:128]
                S.op("dve", lambda e: e.tensor_tensor(out=sre, in0=a, in1=S5[P64, 11, :], op=ALU.mult), reads=[rTM, rS5], writes=[rSST])
                S.op("dve", lambda e: e.tensor_tensor(out=TM[P64, 1, 0:64], in0=b, in1=S5[P64, 12, :], op=ALU.mult), reads=[rTM, rS5], writes=[rTM])
                S.op("dve", lambda e: e.tensor_tensor(out=sre, in0=sre, in1=TM[P64, 1, 0:64], op=ALU.subtract), reads=[rTM, rSST], writes=[rSST])
                S.op("dve", lambda e: e.tensor_tensor(out=sim, in0=a, in1=S5[P64, 12, :], op=ALU.mult), reads=[rTM, rS5], writes=[rSST])
                S.op("dve", lambda e: e.tensor_tensor(out=TM[P64, 1, 0:64], in0=b, in1=S5[P64, 11, :], op=ALU.mult), reads=[rTM, rS5], writes=[rTM])
                S.op("dve", lambda e: e.tensor_tensor(out=sim, in0=sim, in1=TM[P64, 1, 0:64], op=ALU.add), reads=[rTM, rSST], writes=[rSST])
        lr2 = S5[P64, 6:7, :].to_broadcast([64, 2, 64])
        st2 = SST[P64, :].rearrange("p (r g) -> p r g", r=2)
        t1 = TM[P64, 1, :]
        t1v = TM[P64, 1, :].rearrange("p (r g) -> p r g", r=2)
        t2 = TM[P64, 2, :]
        for t0 in range(0, T, 128):
            nt = min(128, T - t0)
            for g4 in range(0, 64, 2):
                pi = self.next_ps()
                for gi in range(2):
                    g = g4 + gi
                    for r in range(2):
                        S.op("pe", lambda e, g=g, r=r, gi=gi: e.matmul(self.PS[pi][:64, (gi * 2 + r) * 128:(gi * 2 + r) * 128 + nt], lhsT=self.BP[:, g, r, :],
                                                                      rhs=self.XNT[:, g // 8, t0:t0 + nt], start=True, stop=True),
                             reads=[self.rBP, self.rXNT], writes=[self.rPS[pi]], sig=(gi == 1 and r == 1))
                src = self.PS[pi][:64, :].rearrange("p (g r t) -> p r g t", g=2, r=2)[:, :, :, 0:nt]
                dst = self.BU[P64, :, g4:g4 + 2, 0:nt]
                eng = "act" if (g4 // 2) % 2 == 0 else "dve"
                if eng == "act":
                    for r in range(2):
                        S.op("act", lambda e, r=r: e.activation(out=dst[:, r], in_=src[:, r], func=AF.Copy), reads=[self.rPS[pi]], writes=[self.rBU])
                else:
                    for r in range(2):
                        S.op("dve", lambda e, r=r: e.tensor_copy(out=dst[:, r], in_=src[:, r]), reads=[self.rPS[pi]], writes=[self.rBU])
            for t in range(nt):
                S.op("dve", lambda e: e.tensor_tensor(out=t1v, in0=st2, in1=lr2, op=ALU.mult), reads=[rSST, rS5], writes=[rTM])
                S.op("dve", lambda e: e.tensor_tensor(out=t2[:, 0:64], in0=sim, in1=S5[P64, 8, :], op=ALU.mult), reads=[rSST, rS5], writes=[rTM])
                S.op("dve", lambda e: e.tensor_tensor(out=t2[:, 64:128], in0=sre, in1=S5[P64, 7, :], op=ALU.mult), reads=[rSST, rS5], writes=[rTM])
                S.op("dve", lambda e: e.tensor_tensor(out=t1, in0=t1, in1=t2, op=ALU.add), reads=[rTM], writes=[rTM])
                S.op("dve", lambda e, t=t: e.tensor_tensor(out=SST[P64, :], in0=t1, in1=self.BU[P64, :, :, t].rearrange("p r g -> p (r g)"), op=ALU.add),
                     reads=[rTM, self.rBU], writes=[rSST])
                S.op("dve", lambda e, t=t: e.tensor_copy(out=self.HI[P64, :, :, t].rearrange("p r g -> p (r g)"), in_=SST[P64, :]), reads=[rSST], writes=[self.rHI])
            pa, pb = self.next_ps(), self.next_ps()
            for g in range(64):
                pi = pa if g < 32 else pb
                col = (g % 32) * 16
                for r in range(2):
                    S.op("pe", lambda e, g=g, r=r: e.matmul(self.PS[pi][:nt, col:col + 16], lhsT=self.HI[P64, r, g, 0:nt], rhs=self.CP[P64, r, g, :],
                                                             start=(r == 0), stop=(r == 1)),
                         reads=[self.rHI, self.rCP], writes=[self.rPS[pi]], sig=(r == 1 and g % 32 == 31))
            for hh, pi in ((0, pa), (1, pb)):
                S.op("act", lambda e: e.activation(out=self.YT[:nt, hh * 512:(hh + 1) * 512], in_=self.PS[pi][:nt, :], func=AF.Copy),
                     reads=[self.rPS[pi]], writes=[self.rYT])
            for kc in range(8):
                pi = self.next_ps()
                pt = self.PS[pi][:, 0:64].bitcast(BF16)
                S.op("pe", lambda e: e.transpose(out=pt[:, 0:nt], in_=self.YT[:nt, kc * 128:(kc + 1) * 128], identity=self.IDB[:nt, :nt]),
                     reads=[self.rYT, self.rIDB], writes=[self.rPS[pi]])
                xc = self.XNT[:, kc, t0:t0 + nt]
                S.op("dve", lambda e, xc=xc: e.scalar_tensor_tensor(out=xc, in0=xc, scalar=self.DFM[:, kc:kc + 1], in1=pt[:, 0:nt], op0=ALU.mult, op1=ALU.add),
                     reads=[self.rPS[pi], self.rXNT, self.rDFM], writes=[self.rXNT])
        if last:
            ore, oim = (self.o_pre, self.o_pim) if prompt else (self.o_sre, self.o_sim)
            fr, fi = TM[P64, 1, 0:64], TM[P64, 1, 64:128]
            S.op("dve", lambda e: e.tensor_tensor(out=fr, in0=sre, in1=S5[P64, 9, :], op=ALU.mult), reads=[rSST, rS5], writes=[rTM])
            S.op("dve", lambda e: e.tensor_tensor(out=t2[:, 0:64], in0=sim, in1=S5[P64, 10, :], op=ALU.mult), reads=[rSST, rS5], writes=[rTM])
            S.op("dve", lambda e: e.tensor_tensor(out=fr, in0=fr, in1=t2[:, 0:64], op=ALU.subtract), reads=[rTM], writes=[rTM])
            S.op("dve", lambda e: e.tensor_tensor(out=fi, in0=sre, in1=S5[P64, 10, :], op=ALU.mult), reads=[rSST, rS5], writes=[rTM])
            S.op("dve", lambda e: e.tensor_tensor(out=t2[:, 0:64], in0=sim, in1=S5[P64, 9, :], op=ALU.mult), reads=[rSST, rS5], writes=[rTM])
            S.op("dve", lambda e: e.tensor_tensor(out=fi, in0=fi, in1=t2[:, 0:64], op=ALU.add), reads=[rTM], writes=[rTM])
            for src, od in ((fr, ore), (fi, oim)):
                pi = self.next_ps()
                S.op("pe", lambda e, src=src: e.matmul(self.PS[pi][:64, 0:64], lhsT=src, rhs=self.IDF[0:64, 0:64], start=True, stop=True),
                     reads=[rTM, self.rIDF], writes=[self.rPS[pi]])
                S.op("act", lambda e: e.activation(out=self.SG[0][:64, 0:64], in_=self.PS[pi][:64, 0:64], func=AF.Copy), reads=[self.rPS[pi]], writes=[self.rSG[0]])
                S.dma("sp", out=od[:, :], in_=self.SG[0][:64, 0:64], reads=[self.rSG[0]], writes=[], sem="o_sg0")
        for q in range(4):
            sl = q % 2
            gw = self.GW[sl]
            for half in range(2):
                S.dma("sp", out=gw[:, :, half * 256:(half + 1) * 256],
                      in_=self.s_glu[:, :].rearrange("p (k n) -> p k n", k=8)[:, :, half * 1024 + q * 256:half * 1024 + (q + 1) * 256],
                      reads=[self.R_scr], writes=[self.rGW[sl]], sem="gw%d_%d" % (sl, half))
            for j in range(8):
                pa, pb = self.next_ps(), self.next_ps()
                for half, pi in ((0, pa), (1, pb)):
                    for kc in range(8):
                        lhsT = self.XNT[:, kc, 0:T].rearrange("p (c j) -> p c j", j=8)[:, :, j]
                        S.op("pe", lambda e, kc=kc, lhsT=lhsT, half=half: e.matmul(self.PS[pi][:NC, 0:256], lhsT=lhsT, rhs=gw[:, kc, half * 256:(half + 1) * 256],
                                                                                   start=(kc == 0), stop=(kc == 7)),
                             reads=[self.rXNT, self.rGW[sl]], writes=[self.rPS[pi]], sig=(kc == 7))
                sg = self.SG[1][:NC, 0:256]
                S.op("act", lambda e: e.activation(out=sg, in_=self.PS[pb][:NC, 0:256], func=AF.Tanh, scale=0.5), reads=[self.rPS[pb]], writes=[self.rSG[1]])
                S.op("dve", lambda e: e.scalar_tensor_tensor(out=sg, in0=sg, scalar=1.0, in1=self.PS[pa][:NC, 0:256], op0=ALU.add, op1=ALU.mult),
                     reads=[self.rSG[1], self.rPS[pa]], writes=[self.rSG[1]])
                xv = self.X8[:NC, j, q * 256:(q + 1) * 256]
                S.op("dve", lambda e, xv=xv: e.scalar_tensor_tensor(out=xv, in0=sg, scalar=0.5, in1=xv, op0=ALU.mult, op1=ALU.add),
                     reads=[self.rSG[1], self.rX8], writes=[self.rX8])


def _lay_win(w):
    a = w.reshape(8, 128, 2, NF, 128)
    return np.ascontiguousarray(a.transpose(3, 1, 0, 2, 4)).reshape(NF * 128, 2048)


def _lay_rows(w):
    k = w.shape[0] // 128
    return np.ascontiguousarray(w.reshape(k, 128, w.shape[1]).transpose(1, 0, 2)).reshape(128, k * w.shape[1])


def _lay_cols(w, c0, nchunks):
    a = w[:, c0:c0 + nchunks * 128].reshape(8, 128, nchunks, 128)
    return np.ascontiguousarray(a.transpose(2, 1, 0, 3)).reshape(nchunks * 128, 1024)


def _fm(v):
    v = np.asarray(v, np.float32).reshape(-1, 4, 128)
    return np.ascontiguousarray(v.transpose(2, 0, 1)).reshape(128, -1)


def host_common(inp):
    f32 = np.float32
    d = {}
    w_in = [inp["ffn1_w_in"][0], inp["ffn2_w_in"][0], inp["ffn1_w_in"][1], inp["ffn2_w_in"][1]]
    w_out = [inp["ffn1_w_out"][0], inp["ffn2_w_out"][0], inp["ffn1_w_out"][1], inp["ffn2_w_out"][1]]
    d["w_ffn_in"] = np.stack([_lay_win(np.asarray(w, f32)) for w in w_in])
    d["w_ffn_out"] = np.stack([_lay_rows(np.asarray(w, f32)) for w in w_out])
    gam = np.stack([inp["ffn1_norm"][0], inp["mix_norm"][0], inp["ffn2_norm"][0],
                    inp["ffn1_norm"][1], inp["mix_norm"][1], inp["ffn2_norm"][1]]).astype(f32)
    d["gam"] = np.ascontiguousarray(gam.reshape(6, 8, 128).transpose(2, 0, 1)).reshape(128, 48)
    wab = np.asarray(inp["ab_w_in"][0], f32)
    d["w_abin"] = _lay_cols(wab, 0, 16)
    d["w_abv"] = _lay_rows(np.ascontiguousarray(wab[:, 2048:2560]))
    d["w_about"] = _lay_rows(np.asarray(inp["ab_w_out"][0], f32))
    m0c = np.zeros((128, 46), f32)
    cw = np.asarray(inp["conv_w"][0], f32)
    for c in range(4):
        for k in range(4):
            m0c[:, 4 * c + k] = cw[k, c * 128:(c + 1) * 128]
    m0c[:, 16:20] = _fm(inp["conv_b"][0])
    m0c[:, 20:24] = _fm(inp["lru_ba"][0])
    m0c[:, 24:28] = _fm(inp["lru_bx"][0])
    m0c[:, 28:32] = _fm(inp["lru_lambda"][0])
    m0c[:, 36] = np.tile(np.asarray(inp["q_norm"][0], f32), 2)
    m0c[:, 37] = np.tile(np.asarray(inp["k_norm"][0], f32), 2)
    rb = np.asarray(inp["rel_bias"][0], f32)
    m0c[:, 38:46] = rb[256][None, :]
    d["m0c"] = m0c
    wbd = np.zeros((128, 2, 4, 128), f32)
    for w, nm in enumerate(("lru_wa", "lru_wx")):
        W = np.asarray(inp[nm][0], f32)
        for c in range(4):
            wbd[0:64, w, c, 0:64] = W[2 * c]
            wbd[64:128, w, c, 64:128] = W[2 * c + 1]
    d["wbd"] = wbd.reshape(128, -1)
    p = np.arange(128)
    kl = np.where(p < 64, p, p - 64)
    bb = np.zeros((128, 8, 5, 64), f32)
    q = np.arange(64)
    for jj in range(5):
        cp = np.where(p < 64, jj, jj - 1)
        rel = q[None, :] - kl[:, None] + 64 * cp[:, None]
        idx = np.clip(rel, -128, 128) + 128
        bb[:, :, jj, :] = rb[idx].transpose(0, 2, 1)
    d["bblk"] = bb.reshape(128, -1)
    are = np.asarray(inp["ssm_A_re"][0], f32).T
    aim = np.asarray(inp["ssm_A_im"][0], f32).T
    ldt = np.broadcast_to(np.asarray(inp["ssm_log_dt"][0], f32)[None, :], (64, 64))
    d["s5p"] = np.ascontiguousarray(np.concatenate([are, aim, ldt], axis=1))
    cre = np.asarray(inp["ssm_C_re"][0], f32).transpose(2, 0, 1)
    cim = np.asarray(inp["ssm_C_im"][0], f32).transpose(2, 0, 1)
    d["s5c"] = np.ascontiguousarray(np.stack([cre, cim], axis=1)).reshape(64, 2048)
    bp = np.zeros((128, 64, 2, 64), f32)
    for r, nm in enumerate(("ssm_B_re", "ssm_B_im")):
        Bm = np.asarray(inp[nm][0], f32)
        for g in range(64):
            bp[16 * (g % 8):16 * (g % 8) + 16, g, r, :] = Bm[g].T
    d["w_bp"] = bp.reshape(128, 8192)
    d["w_glu"] = _lay_rows(np.asarray(inp["glu_w"][0], f32))
    d["dfm"] = np.ascontiguousarray(np.asarray(inp["ssm_D"][0], f32).reshape(8, 128).T)
    return d


def host_core(inp, common, c, SEQ):
    f32 = np.float32
    d = dict(common)
    d["xp"] = np.ascontiguousarray(np.asarray(inp["x_prompt"][c % inp["x_prompt"].shape[0]], f32)[:SEQ])
    d["xs"] = np.ascontiguousarray(np.asarray(inp["x_sample"][c], f32))
    st = np.zeros((128, 16), f32)
    st[:, 0:4] = _fm(inp["state_rglru_h"][0, c])
    cv = np.asarray(inp["state_rglru_conv"][0, c], f32)
    for cc in range(4):
        for k in range(3):
            st[:, 4 + 3 * cc + k] = cv[k, cc * 128:(cc + 1) * 128]
    d["st_rg"] = st
    d["ck"] = np.ascontiguousarray(np.asarray(inp["cache_band_k"][0, c], f32).reshape(512, 512))
    d["cv"] = np.ascontiguousarray(np.asarray(inp["cache_band_v"][0, c], f32).reshape(512, 512))
    d["s0"] = np.ascontiguousarray(np.concatenate([np.asarray(inp["state_ssm_re"][0, c], f32).T, np.asarray(inp["state_ssm_im"][0, c], f32).T], axis=1))
    return d


_STAGES = ("ffn", "mix0", "mix1")


def kernel(**inputs):
    inp = {k: np.asarray(v) for k, v in inputs.items()}
    SEQ = inp["x_prompt"].shape[1]
    B = inp["x_prompt"].shape[0]
    NS = inp["x_sample"].shape[0]
    common = host_common(inp)
    in_maps = [host_core(inp, common, c, SEQ) for c in range(8)]
    b = Builder(SEQ=SEQ, stages=_STAGES)
    nc = b.build()
    res = run_bass_kernel_spmd(nc, in_maps, core_ids=list(range(8)))
    rs = res.results
    f32 = np.float32
    KR = min(512, SEQ)

    def st(name, n, shape):
        return np.stack([np.asarray(rs[c][name], f32).reshape(shape) for c in range(n)])[None]

    y_prompt = np.stack([np.asarray(rs[c]["yp"], f32) for c in range(B)])
    y_sample = np.stack([np.asarray(rs[c]["ys"], f32) for c in range(NS)])
    return (y_prompt, y_sample,
            st("o_pconv", B, (3, 512)), st("o_ph", B, (512,)), st("o_pk", B, (KR, 8, 64)), st("o_pv", B, (KR, 8, 64)),
            st("o_pre", B, (64, 64)), st("o_pim", B, (64, 64)),
            st("o_sconv", NS, (3, 512)), st("o_sh", NS, (512,)), st("o_sk", NS, (64, 8, 64)), st("o_sv", NS, (64, 8, 64)),
            st("o_sre", NS, (64, 64)), st("o_sim", NS, (64, 64)))
```

```python
import contextlib
import os
import numpy as np
_SKIP = set(os.environ.get('K_SKIP', '').split(','))
_STOP = int(os.environ.get('K_STOP', '1000000000'))
_LIST = [int(v) for v in os.environ['K_LIST'].split(',')] if os.environ.get('K_LIST') else None
import concourse.bass as bass
import concourse.mybir as mybir
from concourse.bass_utils import run_bass_kernel_spmd

F32 = mybir.dt.float32
BF16 = mybir.dt.bfloat16
AF = mybir.ActivationFunctionType
ALU = mybir.AluOpType
AX = mybir.AxisListType

D = 1024
DFF = 2816
NF = 22
DK = 8
EPS = 1e-6
PAGE = 256


class Reg:
    __slots__ = ("space", "lo", "hi")

    def __init__(self, space, lo, hi):
        self.space, self.lo, self.hi = space, lo, hi

    def sub(self, lo, hi):
        assert 0 <= lo < hi <= self.hi - self.lo, (lo, hi, self.lo, self.hi)
        return Reg(self.space, self.lo + lo, self.lo + hi)


class Sched:
    def __init__(self, nc, es):
        self.nc = nc
        self.es = es
        self.eng = {"pe": nc.tensor, "act": nc.scalar, "dve": nc.vector, "pool": nc.gpsimd, "sp": nc.sync}
        self.sems = {}
        self.cnt = {}
        for k in self.eng:
            self.sems[k] = es.enter_context(nc.semaphore("s_" + k))
            self.cnt[k] = 0
        self.pending = {k: False for k in self.eng}
        self.waited = {k: {} for k in self.eng}
        self.pages = {}
        self.nops = 0

    def dma_sem(self, name):
        key = "d_" + name
        if key not in self.sems:
            self.sems[key] = self.es.enter_context(self.nc.semaphore(key))
            self.cnt[key] = 0
        return key

    def _pg(self, regs):
        for r in regs:
            for p in range(r.lo // PAGE, (r.hi + PAGE - 1) // PAGE):
                yield (r.space, p)

    def _collect(self, reads, writes):
        deps = {}

        def add(tok):
            if tok is not None:
                k, v = tok
                if deps.get(k, 0) < v:
                    deps[k] = v

        for pg in self._pg(reads):
            e = self.pages.get(pg)
            if e is not None:
                add(e[0])
        for pg in self._pg(writes):
            e = self.pages.get(pg)
            if e is not None:
                add(e[0])
                for k, v in e[1].items():
                    add((k, v))
        return deps

    def _waits(self, E, deps):
        w = self.waited[E]
        for k, v in deps.items():
            if k == E and E == "pe":
                continue
            if w.get(k, 0) < v:
                self.eng[E].wait_ge(self.sems[k], v)
                w[k] = v

    def _record(self, tok, reads, writes):
        k, v = tok
        for pg in self._pg(reads):
            e = self.pages.get(pg)
            if e is None:
                e = [None, {}]
                self.pages[pg] = e
            e[1][k] = v
        for pg in self._pg(writes):
            self.pages[pg] = [tok, {}]

    def op(self, E, fn, reads=(), writes=(), sig=True):
        if self.nops >= _STOP:
            self.nops += 1
            return None
        deps = self._collect(reads, writes)
        self._waits(E, deps)
        if _LIST and _LIST[0] <= self.nops < _LIST[1]:
            print("OP", self.nops, E, fn.__code__.co_firstlineno)
        ins = fn(self.eng[E])
        tick = self.cnt[E] + 1
        if sig:
            ins.then_inc(self.sems[E], 1)
            self.cnt[E] = tick
            self.pending[E] = False
        else:
            self.pending[E] = True
        self._record((E, tick), reads, writes)
        self.nops += 1
        return ins

    def dma(self, Q, out, in_, reads, writes, sem, slow=False):
        if self.nops >= _STOP:
            self.nops += 1
            return None
        key = self.dma_sem(sem)
        deps = self._collect(reads, writes)
        self._waits(Q, deps)
        if slow:
            ins = self.eng[Q].dma_start(out=out, in_=in_, allow_slow_non_contiguous=True)
        else:
            ins = self.eng[Q].dma_start(out=out, in_=in_)
        ins.then_inc(self.sems[key], 16)
        self.cnt[key] += 16
        self._record((key, self.cnt[key]), reads, writes)
        self.nops += 1
        return ins

    def finish(self):
        if _STOP >= 1000000000:
            assert not any(self.pending.values()), self.pending
        sp = self.eng["sp"]
        for k, v in self.cnt.items():
            if k != "sp" and v > 0:
                sp.wait_ge(self.sems[k], v)


class Arena:
    def __init__(self, nc, es, name, nbytes, space):
        self.t = es.enter_context(nc.sbuf_tensor(name, [128, nbytes // 4], F32))
        self.space = space
        self.nbytes = nbytes
        self.top = 0

    def alloc(self, nbytes, at=None):
        nbytes = (nbytes + PAGE - 1) // PAGE * PAGE
        if at is None:
            at = self.top
            self.top += nbytes
            assert self.top <= self.nbytes, ("arena overflow", self.top, self.nbytes)
        assert at + nbytes <= self.nbytes, ("arena overflow", at, nbytes, self.nbytes)
        return Reg(self.space, at, at + nbytes)

    def ap(self, reg, dtype, pattern=None, n=None, **kw):
        a = self.t[:, reg.lo // 4: reg.hi // 4]
        if dtype != F32:
            a = a.bitcast(dtype)
        if n is not None:
            a = a[:, 0:n]
        if pattern is not None:
            a = a.rearrange(pattern, **kw)
        return a


class Builder:
    def __init__(self, SEQ=8192, stages=("ffn", "mix0", "mix1"), with_sample=True, debug=False):
        self.SEQ = SEQ
        self.stages = stages
        self.with_sample = with_sample
        self.debug = debug
        self.nc = bass.Bass("TRN2", target_bir_lowering=False)
        self.es = contextlib.ExitStack()

    def din(self, name, shape, dtype=F32):
        return self.nc.dram_tensor(name, list(shape), dtype, kind="ExternalInput").ap()

    def dout(self, name, shape, dtype=F32):
        return self.nc.dram_tensor(name, list(shape), dtype, kind="ExternalOutput").ap()

    def dscr(self, name, shape, dtype=BF16):
        return self.nc.dram_tensor(name, list(shape), dtype, kind="Internal").ap()

    def build(self):
        with self.es:
            self._build()
        return self.nc

    def _build(self):
        nc, es = self.nc, self.es
        SEQ = self.SEQ
        S = self.S = Sched(nc, es)
        self.xp = self.din("xp", [max(SEQ, 8), D])
        self.xs = self.din("xs", [64, D])
        self.yp = self.dout("yp", [max(SEQ, 8), D])
        self.ys = self.dout("ys", [64, D])
        self.w_ffn_in = self.din("w_ffn_in", [4, NF * 128, 2048])
        self.w_ffn_out = self.din("w_ffn_out", [4, 128, NF * 1024])
        self.gam = self.din("gam", [128, 6 * 8])
        self.s_ffn_in = self.dscr("s_ffn_in", [4, NF * 128, 2048])
        self.s_ffn_out = self.dscr("s_ffn_out", [4, 128, NF * 1024])
        self.R_scr = Reg("dram", 0, PAGE)
        self.extra_pairs = []
        if "mix0" in self.stages:
            self.w_abin = self.din("w_abin", [16 * 128, 1024])
            self.w_abv = self.din("w_abv", [128, 8 * 512])
            self.w_about = self.din("w_about", [128, 8 * 1024])
            self.s_abin = self.dscr("s_abin", [16 * 128, 1024])
            self.s_abv = self.dscr("s_abv", [128, 8 * 512])
            self.s_about = self.dscr("s_about", [128, 8 * 1024])
            self.extra_pairs += [(self.w_abin, self.s_abin), (self.w_abv, self.s_abv), (self.w_about, self.s_about)]
            self.m0off = dict(cw=0, cb=16, ba=20, bx=24, lam=28, c1=32, gq=36, gk=37, bc=38)
            self.NM0 = 46
            self.m0c = self.din("m0c", [128, self.NM0])
            self.wbd = self.din("wbd", [128, 2 * 4 * 128])
            self.bblk = self.din("bblk", [128, 8 * 5 * 64])
            self.st_rg = self.din("st_rg", [128, 16])
            self.ck = self.din("ck", [512, 512])
            self.cv = self.din("cv", [512, 512])
            KR = max(8, min(512, SEQ))
            self.o_pconv = self.dout("o_pconv", [3, 512])
            self.o_ph = self.dout("o_ph", [512])
            self.o_pk = self.dout("o_pk", [KR, 512])
            self.o_pv = self.dout("o_pv", [KR, 512])
            self.o_sconv = self.dout("o_sconv", [3, 512])
            self.o_sh = self.dout("o_sh", [512])
            self.o_sk = self.dout("o_sk", [64, 512])
            self.o_sv = self.dout("o_sv", [64, 512])

        if "mix1" in self.stages:
            self.s5p = self.din("s5p", [64, 192])
            self.s5c = self.din("s5c", [64, 2048])
            self.w_bp = self.din("w_bp", [128, 8192])
            self.s_bp = self.dscr("s_bp", [128, 8192])
            self.s_cp = self.dscr("s_cp", [64, 2048])
            self.w_glu = self.din("w_glu", [128, 8 * 2048])
            self.s_glu = self.dscr("s_glu", [128, 8 * 2048])
            self.dfm = self.din("dfm", [128, 8])
            self.s0 = self.din("s0", [64, 128])
            self.extra_pairs += [(self.w_bp, self.s_bp), (self.w_glu, self.s_glu)]
            self.o_pre = self.dout("o_pre", [64, 64])
            self.o_pim = self.dout("o_pim", [64, 64])
            self.o_sre = self.dout("o_sre", [64, 64])
            self.o_sim = self.dout("o_sim", [64, 64])
        A = self.A = Arena(nc, es, "arena", 207 * 1024, "sb")
        self.rX8 = A.alloc(8 * 1024 * 4)
        self.rXNT = A.alloc(8 * 1024 * 2)
        self.rWIN = [A.alloc(2048 * 2) for _ in range(2)]
        self.rGAM = A.alloc(6 * 8 * 4)
        self.rIDB = A.alloc(128 * 2)
        self.rIDF = A.alloc(128 * 4)
        self.rSS = A.alloc(PAGE)
        self.rRS = A.alloc(PAGE)
        self.rSG = [A.alloc(512 * 4) for _ in range(2)]
        if "mix0" in self.stages:
            self.alloc_mix0_consts()
        if "mix1" in self.stages:
            self.alloc_mix1_consts()
        self.zone = A.top
        self.rWOUT = A.alloc(NF * 1024 * 2, at=self.zone)
        self.rH = A.alloc(NF * 512 * 2, at=self.zone + NF * 1024 * 2)
        self.rXS = A.alloc(8 * 1024 * 2, at=self.zone + NF * 1024 * 2)

        self.X8 = A.ap(self.rX8, F32, "p (j d) -> p j d", j=8)
        self.XNT = A.ap(self.rXNT, BF16, "p (k t) -> p k t", k=8)
        self.XS = A.ap(self.rXS, BF16, "p (j d) -> p j d", j=8)
        self.WIN = [A.ap(r, BF16, "p (k n) -> p k n", k=8) for r in self.rWIN]
        self.GAM = A.ap(self.rGAM, F32, "p (w k) -> p w k", n=48, w=6)
        self.IDB = A.ap(self.rIDB, BF16, n=128)
        self.IDF = A.ap(self.rIDF, F32, n=128)
        self.SS = A.ap(self.rSS, F32)
        self.RS = A.ap(self.rRS, F32)
        self.SG = [A.ap(r, F32) for r in self.rSG]
        self.WOUT = A.ap(self.rWOUT, BF16, "p (f n) -> p f n", f=NF)
        self.H = A.ap(self.rH, BF16, "p (f n) -> p f n", f=NF)

        self.PS = [es.enter_context(nc.psum_tensor("ps%d" % i, [128, 512], F32)) for i in range(8)]
        self.rPS = [Reg("ps", i * PAGE, (i + 1) * PAGE) for i in range(8)]
        self.ps_rr = 0
        self.win_rr = 0
        self.sg_rr = 0

        self.setup_consts()
        self.prepass()
        if "mix0" in self.stages:
            self.setup_mix0()
            self.setup_mix0_consts()
        if "mix1" in self.stages:
            self.setup_mix1()
        ntile = SEQ // 1024
        for t in range(ntile):
            self.macro_tile(self.xp, self.yp, t * 1024, 128, prompt=True, first=(t == 0), last=(t == ntile - 1))
        if self.with_sample:
            self.macro_tile(self.xs, self.ys, 0, 8, prompt=False, first=True, last=True)
        S.finish()

    def dbg(self, name, ap, reg, shape):
        if not self.debug:
            return
        o = self.dout("dbg_" + name, shape)
        self.S.dma("sp", out=o, in_=ap, reads=[reg], writes=[], sem="dbg_" + name)

    def next_ps(self, n=1):
        i = self.ps_rr
        self.ps_rr = (self.ps_rr + 1) % 8
        return i

    def setup_consts(self):
        S, A = self.S, self.A
        S.dma("sp", out=self.A.ap(self.rGAM, F32, n=48), in_=self.gam[:, :], reads=[], writes=[self.rGAM], sem="const1")
        idf = self.IDF
        S.op("pool", lambda e: e.memset(idf[:, :], 0.0), writes=[self.rIDF])
        S.op("pool", lambda e: e.affine_select(out=idf[:, :], in_=idf[:, :], pattern=[[1, 128]],
                                                compare_op=ALU.not_equal, fill=1.0, base=0, channel_multiplier=-1),
             reads=[self.rIDF], writes=[self.rIDF])
        S.op("pool", lambda e: e.tensor_copy(out=self.IDB[:, :], in_=idf[:, :]), reads=[self.rIDF], writes=[self.rIDB])

    def prepass(self):
        S = self.S
        pairs = []
        for w in range(4):
            pairs.append((self.w_ffn_in[w], self.s_ffn_in[w]))
            pairs.append((self.w_ffn_out[w], self.s_ffn_out[w]))
        pairs += getattr(self, "extra_pairs", [])
        for src, dst in pairs:
            rows, cols = src.shape
            step = max(1, (1 << 20) // cols)
            for r0 in range(0, rows, step):
                r1 = min(rows, r0 + step)
                S.dma("pool", out=dst[r0:r1, :], in_=src[r0:r1, :], reads=[], writes=[self.R_scr], sem="pre")

    def macro_tile(self, xin, yout, t0, NC, prompt, first, last):
        S = self.S
        T = 8 * NC
        X8 = self.X8
        S.dma("sp", out=X8[:NC, :, :], in_=xin[t0:t0 + T, :].rearrange("(c j) d -> c j d", j=8),
              reads=[], writes=[self.rX8], sem="xin")
        subt = [(0, 4), (4, 4)] if NC == 128 else [(0, 8)]
        for l in range(2):
            if "ffn" in self.stages:
                self.norm_fm(NC, 3 * l + 0)
                self.ffn(NC, 2 * l + 0, subt)
            if l == 0 and "mix0" in self.stages:
                self.mixer0(NC, t0, prompt, first, last)
            if l == 1 and "mix1" in self.stages:
                self.mixer1(NC, prompt, first, last)
            if "ffn" in self.stages:
                self.norm_fm(NC, 3 * l + 2)
                self.ffn(NC, 2 * l + 1, subt)
        S.dma("sp", out=yout[t0:t0 + T, :].rearrange("(c j) d -> c j d", j=8), in_=X8[:NC, :, :],
              reads=[self.rX8], writes=[], sem="yout")

    def norm_stats(self, NC):
        S = self.S
        X8, XS, SS, RS = self.X8, self.XS, self.SS, self.RS
        S.op("dve", lambda e: e.memset(SS[:NC, 0:8], 0.0), writes=[self.rSS])
        for j in range(8):
            S.op("act", lambda e, j=j: e.activation(out=XS[:NC, j, :], in_=X8[:NC, j, :], func=AF.Square,
                                                      accum_out=SS[:NC, j:j + 1]),
                 reads=[self.rX8, self.rSS], writes=[self.rXS, self.rSS])
        S.op("dve", lambda e: e.tensor_scalar(out=RS[:NC, 0:8], in0=SS[:NC, 0:8], scalar1=1.0 / D, scalar2=EPS,
                                              op0=ALU.mult, op1=ALU.add), reads=[self.rSS], writes=[self.rRS])
        S.op("act", lambda e: e.activation(out=RS[:NC, 0:8], in_=RS[:NC, 0:8], func=AF.Sqrt),
             reads=[self.rRS], writes=[self.rRS])
        S.op("dve", lambda e: e.reciprocal(out=RS[:NC, 0:8], in_=RS[:NC, 0:8]), reads=[self.rRS], writes=[self.rRS])
        for j in range(8):
            S.op("pool", lambda e, j=j: e.tensor_scalar(out=XS[:NC, j, :], in0=X8[:NC, j, :],
                                                         scalar1=RS[:NC, j:j + 1], scalar2=None, op0=ALU.mult),
                 reads=[self.rX8, self.rRS], writes=[self.rXS.sub(j * 2048, (j + 1) * 2048)])

    def norm_fm(self, NC, gidx):
        S = self.S
        self.norm_stats(NC)
        T = 8 * NC
        for dk in range(8):
            for jh in range(2):
                pi = self.next_ps()
                pt = self.PS[pi][:, 0:256].bitcast(BF16).rearrange("p (j c) -> p j c", j=4)
                for jl in range(4):
                    j = 4 * jh + jl
                    S.op("pe", lambda e, jl=jl, j=j: e.transpose(out=pt[:, jl, :NC], in_=self.XS[:NC, j, dk * 128:(dk + 1) * 128],
                                                                 identity=self.IDB[:NC, :NC]),
                         reads=[self.rXS.sub(j * 2048, (j + 1) * 2048), self.rIDB], writes=[self.rPS[pi]], sig=(jl == 3))
                dst = self.XNT[:, dk, 0:T].rearrange("p (c j) -> p j c", j=8)[:, 4 * jh:4 * jh + 4, :]
                g = self.GAM[:, gidx, dk:dk + 1]
                wr = [self.rXNT.sub(dk * 2048, dk * 2048 + T * 2)]
                if (dk + jh) % 2 == 0:
                    S.op("act", lambda e: e.activation(out=dst, in_=pt[:, :, :NC], func=AF.Copy, scale=g),
                         reads=[self.rPS[pi], self.rGAM], writes=wr)
                else:
                    S.op("dve", lambda e: e.tensor_scalar(out=dst, in0=pt[:, :, :NC], scalar1=g, scalar2=None, op0=ALU.mult),
                         reads=[self.rPS[pi], self.rGAM], writes=wr)

    def ffn(self, NC, widx, subt):
        S = self.S
        T = 8 * NC
        for f0 in range(0, NF, 11):
            S.dma("sp", out=self.WOUT[:, f0:f0 + 11, :],
                  in_=self.s_ffn_out[widx][:, f0 * 1024:(f0 + 11) * 1024].rearrange("p (f n) -> p f n", f=11),
                  reads=[self.R_scr], writes=[self.rWOUT.sub(f0 * 2048, (f0 + 11) * 2048)], sem="wout%d" % (f0 // 11))
        for (j0, nj) in subt:
            N = NC * nj
            rhs = [self.XNT[:, dk, 0:T].rearrange("p (c j) -> p c j", j=8)[:, :, j0:j0 + nj] for dk in range(8)]
            rd_x = [self.rXNT]
            for f in range(NF):
                sl = self.win_rr
                self.win_rr = (self.win_rr + 1) % 2
                S.dma("sp", out=self.WIN[sl][:, :, :],
                      in_=self.s_ffn_in[widx][f * 128:(f + 1) * 128, :].rearrange("p (k n) -> p k n", k=8),
                      reads=[self.R_scr], writes=[self.rWIN[sl]], sem="win%d" % sl)
                pg, pu = self.next_ps(), self.next_ps()
                for half, pi in ((0, pg), (1, pu)):
                    out = self.PS[pi][:, 0:N].rearrange("p (c j) -> p c j", j=nj)
                    for dk in range(8):
                        S.op("pe", lambda e, dk=dk, out=out, half=half: e.matmul(
                            out, lhsT=self.WIN[sl][:, dk, half * 128:(half + 1) * 128], rhs=rhs[dk],
                            start=(dk == 0), stop=(dk == 7)),
                            reads=[self.rWIN[sl]] + rd_x, writes=[self.rPS[pi]], sig=(dk == 7))
                sg = self.sg_rr
                self.sg_rr = (self.sg_rr + 1) % 2
                S.op("act", lambda e: e.activation(out=self.SG[sg][:, 0:N], in_=self.PS[pg][:, 0:N], func=AF.Silu),
                     reads=[self.rPS[pg]], writes=[self.rSG[sg]])
                S.op("dve", lambda e: e.tensor_tensor(out=self.H[:, f, 0:N], in0=self.SG[sg][:, 0:N],
                                                      in1=self.PS[pu][:, 0:N], op=ALU.mult),
                     reads=[self.rSG[sg], self.rPS[pu]], writes=[self.rH.sub(f * 1024, (f + 1) * 1024)])
            for jl in range(nj):
                for dh in range(2):
                    pi = self.next_ps()
                    for f in range(NF):
                        lhsT = self.H[:, f, 0:N].rearrange("p (c j) -> p c j", j=nj)[:, :, jl]
                        S.op("pe", lambda e, f=f, lhsT=lhsT: e.matmul(
                            self.PS[pi][:NC, :], lhsT=lhsT, rhs=self.WOUT[:, f, dh * 512:(dh + 1) * 512],
                            start=(f == 0), stop=(f == NF - 1)),
                            reads=[self.rH.sub(f * 1024, (f + 1) * 1024), self.rWOUT.sub(f * 2048, (f + 1) * 2048)],
                            writes=[self.rPS[pi]], sig=(f == NF - 1))
                    j = j0 + jl
                    xv = self.X8[:NC, j, dh * 512:(dh + 1) * 512]
                    S.op("dve", lambda e, xv=xv: e.scalar_tensor_tensor(out=xv, in0=self.PS[pi][:NC, :], scalar=0.5, in1=xv,
                                                                        op0=ALU.mult, op1=ALU.add),
                         reads=[self.rPS[pi], self.rX8], writes=[self.rX8])

    def setup_mix0(self):
        A, S, nc = self.A, self.S, self.nc
        z = self.zone
        ZK = 66 * 1024
        self.rKF = A.alloc(4 * 1536 * 2, at=z + ZK)
        self.rVA = A.alloc(12 * 520 * 2, at=self.rKF.hi)
        o = self.rVA.hi
        self.rMIXT = A.alloc(8 * 1024 * 2, at=o); o = self.rMIXT.hi
        self.rPT = [A.alloc(8 * 512 * 2, at=o), A.alloc(8 * 512 * 2, at=o + 8192)]; o += 16384
        self.rYB = A.alloc(8 * 512 * 2, at=o); o = self.rYB.hi
        self.mix_end = o
        o = z
        self.rABO = A.alloc(8 * 1024 * 2, at=o); o = self.rABO.hi
        self.rQF = A.alloc(4 * 1024 * 2, at=o); o = self.rQF.hi
        self.rXA = A.alloc(1028 * 4, at=o); o = self.rXA.hi
        self.rGA = A.alloc(1024 * 4, at=o); o = self.rGA.hi
        self.rXC = A.alloc(1024 * 4, at=o); o = self.rXC.hi
        self.rXCB = A.alloc(1024 * 2, at=o); o = self.rXCB.hi
        self.rT = []
        for _ in range(4):
            self.rT.append(A.alloc(1024 * 4, at=o)); o = self.rT[-1].hi
        self.rWV = A.alloc(8 * 512 * 2, at=o); o = self.rWV.hi
        assert o <= z + ZK, (o - z)
        self.KF = A.ap(self.rKF, BF16, "p (h t) -> p h t", h=4)
        self.VA = A.ap(self.rVA, BF16, "p (m h e) -> p m h e", n=12 * 520, m=12, h=8)
        self.MIXT = A.ap(self.rMIXT, BF16, "p (k t) -> p k t", k=8)
        self.PT = [A.ap(r, BF16, "p (m q) -> p m q", m=8) for r in self.rPT]
        self.YB = A.ap(self.rYB, BF16, "p (i c) -> p i c", i=8)
        self.ABO = A.ap(self.rABO, BF16, "p (k n) -> p k n", k=8)
        self.QF = A.ap(self.rQF, BF16, "p (h t) -> p h t", h=4)
        self.XA = A.ap(self.rXA, F32, n=1028)
        self.GA = A.ap(self.rGA, F32)
        self.XC = A.ap(self.rXC, F32)
        self.XCB = A.ap(self.rXCB, BF16)
        self.T = [A.ap(r, F32) for r in self.rT]
        self.WV = A.ap(self.rWV, BF16, "p (k n) -> p k n", k=8)

    def alloc_mix0_consts(self):
        A = self.A
        self.rM0 = A.alloc(self.NM0 * 4)
        self.M0 = A.ap(self.rM0, F32, n=self.NM0)
        self.rBD = A.alloc(2 * 4 * 128 * 2)
        self.BD = A.ap(self.rBD, BF16, "p (w c n) -> p w c n", w=2, c=4)
        self.rBB = A.alloc(8 * 5 * 64 * 2)
        self.BB = A.ap(self.rBB, BF16, "p (h j q) -> p h j q", h=8, j=5)
        self.rBON = A.alloc(128 * 2)
        self.BON = A.ap(self.rBON, BF16, n=128)
        self.rST = A.alloc(PAGE)
        self.ST = A.ap(self.rST, F32, n=64)

    def setup_mix0_consts(self):
        A, S = self.A, self.S
        NM0 = self.NM0
        S.dma("sp", out=self.M0[:, :], in_=self.m0c[:, :], reads=[], writes=[self.rM0], sem="const2")
        M0 = self.M0
        o = self.m0off
        stg = A.ap(self.rT[0].sub(0, 4096), F32, "p (w c n) -> p w c n", w=2, c=4)
        S.dma("sp", out=stg, in_=self.wbd[:, :].rearrange("p (w c n) -> p w c n", w=2, c=4), reads=[], writes=[self.rT[0]], sem="const3")
        S.op("dve", lambda e: e.tensor_copy(out=self.BD[:, :, :, :], in_=stg), reads=[self.rT[0]], writes=[self.rBD])
        rstg2 = Reg("sb", self.rT[1].lo, self.rT[3].hi)
        stg2 = A.ap(rstg2, F32, "p (h j q) -> p h j q", n=2560, h=8, j=5)
        S.dma("sp", out=stg2, in_=self.bblk[:, :].rearrange("p (h j q) -> p h j q", h=8, j=5), reads=[], writes=[rstg2], sem="const4")
        for h in range(8):
            S.op("dve", lambda e, h=h: e.tensor_scalar(out=self.BB[:, h, :, :], in0=stg2[:, h, :, :],
                                                        scalar1=M0[:, o["bc"] + h:o["bc"] + h + 1], scalar2=None, op0=ALU.subtract),
                 reads=[rstg2, self.rM0], writes=[self.rBB])
        S.op("act", lambda e: e.activation(out=self.BB[:, :, :, :], in_=self.BB[:, :, :, :], func=AF.Exp), reads=[self.rBB], writes=[self.rBB])
        S.op("pool", lambda e: e.memset(self.BON[:, :], 0.0), writes=[self.rBON])
        S.op("pool", lambda e: e.memset(self.BON[0:64, 0:64], 1.0), writes=[self.rBON])
        S.op("pool", lambda e: e.memset(self.BON[64:128, 64:128], 1.0), writes=[self.rBON])
        c1 = M0[:, o["c1"]:o["c1"] + 4]
        S.op("act", lambda e: e.activation(out=c1, in_=M0[:, o["lam"]:o["lam"] + 4], func=AF.Exp, scale=-1.0),
             reads=[self.rM0], writes=[self.rM0])
        S.op("act", lambda e: e.activation(out=c1, in_=c1, func=AF.Ln, bias=1.0), reads=[self.rM0], writes=[self.rM0])
        S.op("dve", lambda e: e.tensor_scalar(out=c1, in0=c1, scalar1=-4.0, scalar2=None, op0=ALU.mult),
             reads=[self.rM0], writes=[self.rM0])
        for nm in ("ba", "bx"):
            v = M0[:, o[nm]:o[nm] + 4]
            S.op("dve", lambda e, v=v: e.tensor_scalar(out=v, in0=v, scalar1=0.5, scalar2=None, op0=ALU.mult),
                 reads=[self.rM0], writes=[self.rM0])
        v = M0[:, o["gq"]:o["gq"] + 1]
        S.op("dve", lambda e: e.tensor_scalar(out=v, in0=v, scalar1=0.125, scalar2=None, op0=ALU.mult),
             reads=[self.rM0], writes=[self.rM0])
        S.op("pool", lambda e: e.memset(self.VA[:, :, :, 64:65], 1.0), writes=[self.rVA])

    def inproj_fm(self, NC, oc, consume):
        S = self.S
        T = 8 * NC
        sl = self.win_rr
        self.win_rr = (self.win_rr + 1) % 2
        w = self.WIN[sl][:, :, 0:128]
        S.dma("sp", out=w, in_=self.s_abin[oc * 128:(oc + 1) * 128, :].rearrange("p (k n) -> p k n", k=8),
              reads=[self.R_scr], writes=[self.rWIN[sl]], sem="win%d" % sl)
        for c0 in range(0, T, 512):
            n = min(512, T - c0)
            pi = self.next_ps()
            for dk in range(8):
                S.op("pe", lambda e, dk=dk: e.matmul(self.PS[pi][:, 0:n], lhsT=w[:, dk, :], rhs=self.XNT[:, dk, c0:c0 + n],
                                                     start=(dk == 0), stop=(dk == 7)),
                     reads=[self.rWIN[sl], self.rXNT], writes=[self.rPS[pi]], sig=(dk == 7))
            consume(pi, c0, n)

    def mixer0(self, NC, t0, prompt, first, last):
        S, A = self.S, self.A
        T = 8 * NC
        M0, o = self.M0, self.m0off
        self.norm_fm(NC, 1)
        ST = self.ST
        S.dma("sp", out=self.ABO[:, :, :], in_=self.s_about[:, :].rearrange("p (k n) -> p k n", k=8),
              reads=[self.R_scr], writes=[self.rABO], sem="abo")
        S.dma("sp", out=self.WV[:, :, :], in_=self.s_abv[:, :].rearrange("p (k n) -> p k n", k=8),
              reads=[self.R_scr], writes=[self.rWV], sem="wv")
        if first:
            if prompt:
                S.op("dve", lambda e: e.memset(ST[:, 0:16], 0.0), writes=[self.rST])
            else:
                S.dma("sp", out=ST[:, 0:16], in_=self.st_rg[:, :], reads=[], writes=[self.rST], sem="const5")
        if os.environ.get("K_MARK"): print("MARK rg", S.nops)
        XA, GA, XC, XCB, Tt = self.XA, self.GA, self.XC, self.XCB, self.T
        rT = self.rT
        for c in range(4):
            if "rg" in _SKIP:
                S.op("dve", lambda e: e.memset(self.MIXT[:, c, 0:T], 0.0), writes=[self.rMIXT.sub(c * 2048, (c + 1) * 2048)])
                continue
            S.op("dve", lambda e: e.tensor_copy(out=XA[:, 0:3], in_=ST[:, 4 + 3 * c:7 + 3 * c]), reads=[self.rST], writes=[self.rXA])
            self.inproj_fm(NC, c, lambda pi, c0, n: S.op(
                "act", lambda e: e.activation(out=XA[:, 3 + c0:3 + c0 + n], in_=self.PS[pi][:, 0:n], func=AF.Copy),
                reads=[self.rPS[pi]], writes=[self.rXA]))
            self.inproj_fm(NC, 4 + c, lambda pi, c0, n: S.op(
                "act", lambda e: e.activation(out=GA[:, c0:c0 + n], in_=self.PS[pi][:, 0:n], func=AF.Copy),
                reads=[self.rPS[pi]], writes=[self.rGA]))
            S.op("dve", lambda e: e.tensor_copy(out=ST[:, 4 + 3 * c:7 + 3 * c], in_=XA[:, T:T + 3]), reads=[self.rXA], writes=[self.rST])
            cw = lambda k: M0[:, o["cw"] + 4 * c + k:o["cw"] + 4 * c + k + 1]
            S.op("dve", lambda e: e.tensor_scalar(out=XC[:, 0:T], in0=XA[:, 0:T], scalar1=cw(0),
                                                   scalar2=M0[:, o["cb"] + c:o["cb"] + c + 1], op0=ALU.mult, op1=ALU.add),
                 reads=[self.rXA, self.rM0], writes=[self.rXC])
            for k in range(1, 4):
                S.op("dve", lambda e, k=k: e.scalar_tensor_tensor(out=XC[:, 0:T], in0=XA[:, k:k + T], scalar=cw(k), in1=XC[:, 0:T],
                                                                   op0=ALU.mult, op1=ALU.add),
                     reads=[self.rXA, self.rXC, self.rM0], writes=[self.rXC])
            S.op("pool", lambda e: e.tensor_copy(out=XCB[:, 0:T], in_=XC[:, 0:T]), reads=[self.rXC], writes=[self.rXCB])
            for c0 in range(0, T, 512):
                n = min(512, T - c0)
                for w, dst, bn in ((0, Tt[0], "ba"), (1, Tt[1], "bx")):
                    pi = self.next_ps()
                    S.op("pe", lambda e, w=w: e.matmul(self.PS[pi][:, 0:n], lhsT=self.BD[:, w, c, :], rhs=XCB[:, c0:c0 + n],
                                                       start=True, stop=True),
                         reads=[self.rBD, self.rXCB], writes=[self.rPS[pi]])
                    S.op("act", lambda e, dst=dst, bn=bn: e.activation(out=dst[:, c0:c0 + n], in_=self.PS[pi][:, 0:n], func=AF.Tanh,
                                                                       scale=0.5, bias=M0[:, o[bn] + c:o[bn] + c + 1]),
                         reads=[self.rPS[pi], self.rM0], writes=[rT[w]])
            c1 = M0[:, o["c1"] + c:o["c1"] + c + 1]
            S.op("act", lambda e: e.activation(out=Tt[2][:, 0:T], in_=Tt[0][:, 0:T], func=AF.Exp, scale=c1, bias=c1),
                 reads=[rT[0], self.rM0], writes=[rT[2]])
            S.op("act", lambda e: e.activation(out=Tt[3][:, 0:T], in_=Tt[0][:, 0:T], func=AF.Tanh, scale=c1, bias=c1),
                 reads=[rT[0], self.rM0], writes=[rT[3]])
            if c == 0 and first and prompt:
                self.dbg("tr", Tt[0][:, 0:T], rT[0], [128, T])
                self.dbg("a", Tt[2][:, 0:T], rT[2], [128, T])
                self.dbg("th", Tt[3][:, 0:T], rT[3], [128, T])
                self.dbg("m0", M0[:, :], self.rM0, [128, self.NM0])
                self.dbg("xc", XC[:, 0:T], self.rXC, [128, T])
            S.op("dve", lambda e: e.tensor_tensor(out=Tt[0][:, 0:T], in0=Tt[2][:, 0:T], in1=Tt[2][:, 0:T], op=ALU.mult),
                 reads=[rT[2]], writes=[rT[0]])
            S.op("dve", lambda e: e.scalar_tensor_tensor(out=Tt[0][:, 0:T], in0=Tt[0][:, 0:T], scalar=1.0, in1=Tt[3][:, 0:T],
                                                         op0=ALU.add, op1=ALU.mult), reads=[rT[0], rT[3]], writes=[rT[0]])
            S.op("dve", lambda e: e.tensor_scalar(out=Tt[0][:, 0:T], in0=Tt[0][:, 0:T], scalar1=-1.0, scalar2=0.0, op0=ALU.mult, op1=ALU.max),
                 reads=[rT[0]], writes=[rT[0]])
            S.op("act", lambda e: e.activation(out=Tt[0][:, 0:T], in_=Tt[0][:, 0:T], func=AF.Sqrt),
                 reads=[rT[0]], writes=[rT[0]])
            S.op("dve", lambda e: e.scalar_tensor_tensor(out=Tt[1][:, 0:T], in0=Tt[1][:, 0:T], scalar=1.0, in1=XC[:, 0:T],
                                                         op0=ALU.add, op1=ALU.mult), reads=[rT[1], self.rXC], writes=[rT[1]])
            S.op("dve", lambda e: e.scalar_tensor_tensor(out=Tt[1][:, 0:T], in0=Tt[1][:, 0:T], scalar=0.5, in1=Tt[0][:, 0:T],
                                                         op0=ALU.mult, op1=ALU.mult), reads=[rT[1], rT[0]], writes=[rT[1]])
            S.op("dve", lambda e: e.tensor_tensor_scan(out=Tt[3][:, 0:T], data0=Tt[2][:, 0:T], data1=Tt[1][:, 0:T],
                                                       initial=ST[:, c:c + 1], op0=ALU.mult, op1=ALU.add),
                 reads=[rT[2], rT[1], self.rST], writes=[rT[3]])
            S.op("dve", lambda e: e.tensor_copy(out=ST[:, c:c + 1], in_=Tt[3][:, T - 1:T]), reads=[rT[3]], writes=[self.rST])
            S.op("act", lambda e: e.activation(out=Tt[0][:, 0:T], in_=GA[:, 0:T], func=AF.Square), reads=[self.rGA], writes=[rT[0]])
            S.op("dve", lambda e: e.tensor_scalar(out=Tt[0][:, 0:T], in0=Tt[0][:, 0:T], scalar1=0.044715, scalar2=1.0,
                                                  op0=ALU.mult, op1=ALU.add), reads=[rT[0]], writes=[rT[0]])
            S.op("dve", lambda e: e.tensor_tensor(out=Tt[0][:, 0:T], in0=Tt[0][:, 0:T], in1=GA[:, 0:T], op=ALU.mult),
                 reads=[rT[0], self.rGA], writes=[rT[0]])
            S.op("act", lambda e: e.activation(out=Tt[0][:, 0:T], in_=Tt[0][:, 0:T], func=AF.Tanh, scale=0.7978845608028654),
                 reads=[rT[0]], writes=[rT[0]])
            S.op("dve", lambda e: e.scalar_tensor_tensor(out=Tt[0][:, 0:T], in0=Tt[0][:, 0:T], scalar=1.0, in1=GA[:, 0:T],
                                                         op0=ALU.add, op1=ALU.mult), reads=[rT[0], self.rGA], writes=[rT[0]])
            S.op("dve", lambda e: e.scalar_tensor_tensor(out=self.MIXT[:, c, 0:T], in0=Tt[0][:, 0:T], scalar=0.5, in1=Tt[3][:, 0:T],
                                                         op0=ALU.mult, op1=ALU.mult),
                 reads=[rT[0], rT[3]], writes=[self.rMIXT.sub(c * 2048, (c + 1) * 2048)])
        if last and "stout" not in _SKIP:
            oc_, oh_ = (self.o_pconv, self.o_ph) if prompt else (self.o_sconv, self.o_sh)
            S.dma("sp", out=oh_.rearrange("(c p) -> p c", p=128), in_=ST[:, 0:4], reads=[self.rST], writes=[], sem="o_st", slow=True)
            for c in range(4):
                S.dma("sp", out=oc_[:, c * 128:(c + 1) * 128].rearrange("k p -> p k"), in_=ST[:, 4 + 3 * c:7 + 3 * c],
                      reads=[self.rST], writes=[], sem="o_st", slow=True)
        if os.environ.get("K_MARK"): print("MARK qk", S.nops)
        KF, QF, VA = self.KF, self.QF, self.VA
        if first and not prompt and "cache" not in _SKIP:
            self.load_cache()
        want_kv = last and "kvout" not in _SKIP
        kv_lo = max(0, T - 512)
        for which in (0, 1):
            for hp in range(4):
                oc = 8 + 4 * which + hp
                gcol = o["gq"] if which == 0 else o["gk"]

                def consume(pi, c0, n, which=which, hp=hp, gcol=gcol):
                    S.op("act", lambda e: e.activation(out=self.SG[0][:, 0:n].bitcast(BF16)[:, 0:n], in_=self.PS[pi][:, 0:n], func=AF.Square),
                         reads=[self.rPS[pi]], writes=[self.rSG[0]])
                    p2 = self.next_ps()
                    S.op("pe", lambda e: e.matmul(self.PS[p2][:, 0:n], lhsT=self.BON[:, :], rhs=self.SG[0][:, 0:n].bitcast(BF16)[:, 0:n],
                                                  start=True, stop=True), reads=[self.rBON, self.rSG[0]], writes=[self.rPS[p2]])
                    if self.debug:
                        S.op("dve", lambda e: e.tensor_copy(out=self.T[3][:, 0:n], in_=self.PS[p2][:, 0:n]), reads=[self.rPS[p2]], writes=[self.rT[3]])
                        S.op("dve", lambda e: e.tensor_copy(out=self.T[2][:, 0:n], in_=self.PS[pi][:, 0:n]), reads=[self.rPS[pi]], writes=[self.rT[2]])
                        if which == 0 and hp == 0:
                            self.dbg("ss", self.T[3][:, 0:n], self.rT[3], [128, n])
                            self.dbg("q", self.T[2][:, 0:n], self.rT[2], [128, n])
                            self.dbg("sq", self.SG[0][:, 0:n], self.rSG[0], [128, n])
                        S.op("dve", lambda e: e.tensor_scalar(out=self.SG[1][:, 0:n], in0=self.PS[p2][:, 0:n], scalar1=0.0, scalar2=None, op0=ALU.max),
                             reads=[self.rPS[p2]], writes=[self.rSG[1]])
                        S.op("act", lambda e: e.activation(out=self.SG[1][:, 0:n], in_=self.SG[1][:, 0:n], func=AF.Sqrt, scale=1.0 / 64, bias=EPS),
                             reads=[self.rSG[1]], writes=[self.rSG[1]])
                    else:
                      S.op("act", lambda e: e.activation(out=self.SG[1][:, 0:n], in_=self.PS[p2][:, 0:n], func=AF.Sqrt, scale=1.0 / 64, bias=EPS),
                         reads=[self.rPS[p2]], writes=[self.rSG[1]])
                    S.op("dve", lambda e: e.reciprocal(out=self.SG[1][:, 0:n], in_=self.SG[1][:, 0:n]), reads=[self.rSG[1]], writes=[self.rSG[1]])
                    if which == 0:
                        S.op("dve", lambda e: e.scalar_tensor_tensor(out=QF[:, hp, c0:c0 + n], in0=self.PS[pi][:, 0:n],
                                                                     scalar=M0[:, gcol:gcol + 1], in1=self.SG[1][:, 0:n], op0=ALU.mult, op1=ALU.mult),
                             reads=[self.rPS[pi], self.rSG[1], self.rM0], writes=[self.rQF])
                    else:
                        S.op("dve", lambda e: e.scalar_tensor_tensor(out=self.SG[1][:, 0:n], in0=self.PS[pi][:, 0:n],
                                                                     scalar=M0[:, gcol:gcol + 1], in1=self.SG[1][:, 0:n], op0=ALU.mult, op1=ALU.mult),
                             reads=[self.rPS[pi], self.rSG[1], self.rM0], writes=[self.rSG[1]])
                        S.op("act", lambda e: e.activation(out=KF[:, hp, 512 + c0:512 + c0 + n], in_=self.SG[1][:, 0:n], func=AF.Copy),
                             reads=[self.rSG[1]], writes=[self.rKF])
                        if want_kv and c0 + n > kv_lo and "kout" not in _SKIP:
                            self.emit_k_out(prompt, hp, c0, n, kv_lo)
                self.inproj_fm(NC, oc, consume)
        if os.environ.get("K_MARK"): print("MARK v", S.nops)
        for tb in range((T + 127) // 128):
            rows = min(128, T - tb * 128)
            pi = self.next_ps()
            for dk in range(8):
                S.op("pe", lambda e, dk=dk: e.matmul(self.PS[pi][:rows, :], lhsT=self.XNT[:, dk, tb * 128:tb * 128 + rows], rhs=self.WV[:, dk, :],
                                                     start=(dk == 0), stop=(dk == 7)),
                     reads=[self.rXNT, self.rWV], writes=[self.rPS[pi]], sig=(dk == 7))
            S.op("act", lambda e: e.activation(out=VA[:rows, 4 + tb, :, 0:64], in_=self.PS[pi][:rows, :].rearrange("p (h e) -> p h e", h=8), func=AF.Copy),
                 reads=[self.rPS[pi]], writes=[self.rVA])
            if want_kv and tb * 128 + rows > kv_lo and "vout" not in _SKIP:
                veng = "act"
                if veng == "act":
                    S.op("act", lambda e: e.activation(out=self.SG[0][:rows, :], in_=self.PS[pi][:rows, :], func=AF.Copy), reads=[self.rPS[pi]], writes=[self.rSG[0]])
                else:
                    S.op("dve", lambda e: e.tensor_copy(out=self.SG[0][:rows, :], in_=self.PS[pi][:rows, :]), reads=[self.rPS[pi]], writes=[self.rSG[0]])
                ov = self.o_pv if prompt else self.o_sv
                r0 = tb * 128 - kv_lo
                S.dma("sp", out=ov[r0:r0 + rows, :], in_=self.SG[0][:rows, :], reads=[self.rSG[0]], writes=[], sem="o_sg0")
        if os.environ.get("K_MARK"): print("MARK attn", S.nops)
        nq = T // 64
        KCn = 8 + nq
        kmin = 8 if (prompt and first) else 0
        bc = o["bc"]
        for qs in range(0, nq, 8):
            if "attn" in _SKIP:
                for cc in range(4):
                    S.op("dve", lambda e, cc=cc: e.memset(self.MIXT[:, 4 + cc, 0:T], 0.0), writes=[self.rMIXT.sub((4 + cc) * 2048, (5 + cc) * 2048)])
                break
            qe = min(nq, qs + 8)
            blocks = [m for m in range(qs // 2, (qe + 8 + 1) // 2) if 2 * m >= kmin and 2 * m < KCn]
            for h in range(8):
                hp, base = h // 2, 64 * (h % 2)
                ptb = h % 2
                PTh = self.PT[ptb]
                binfo = {}
                for m in blocks:
                    rows = 128 if 2 * m + 1 < KCn else 64
                    i0, i1 = max(qs, 2 * m - 8), min(qe - 1, 2 * m + 1)
                    n = (i1 - i0 + 1) * 64
                    ml = m - blocks[0]
                    binfo[m] = (rows, i0, ml)
                    pi = self.next_ps()
                    ja, jb = max(0, i0 - (2 * m - 8)), min(4, i1 - (2 * m - 8))
                    hasb = jb >= ja and "nobias" not in _SKIP
                    S.op("pe", lambda e: e.matmul(self.PS[pi][:rows, 0:n], lhsT=KF[base:base + 64, hp, m * 128:m * 128 + rows],
                                                  rhs=QF[base:base + 64, hp, i0 * 64:(i1 + 1) * 64], start=True, stop=True),
                         reads=[self.rKF, self.rQF], writes=[self.rPS[pi]])
                    S.op("act", lambda e: e.activation(out=PTh[:rows, ml, 0:n], in_=self.PS[pi][:rows, 0:n], func=AF.Exp,
                                                       bias=M0[:rows, bc + h:bc + h + 1]),
                         reads=[self.rPS[pi], self.rM0], writes=[self.rPT[ptb].sub(ml * 1024, (ml + 1) * 1024)])
                    if hasb:
                        ia = ja + 2 * m - 8
                        nb = (jb - ja + 1) * 64
                        pv_ = PTh[:rows, ml, (ia - i0) * 64:(ia - i0) * 64 + nb].rearrange("p (j q) -> p j q", q=64)
                        S.op("dve", lambda e: e.tensor_tensor(out=pv_, in0=pv_, in1=self.BB[:rows, h, ja:jb + 1, :], op=ALU.mult),
                             reads=[self.rBB, self.rPT[ptb].sub(ml * 1024, (ml + 1) * 1024)], writes=[self.rPT[ptb].sub(ml * 1024, (ml + 1) * 1024)])
                for i in range(qs, qe):
                    if i % 2 == 1 and (i - 1) // 2 in binfo:
                        rows_, i0_, ml_ = binfo[(i - 1) // 2]
                        if rows_ == 128:
                            zc = PTh[0:64, ml_, (i - i0_) * 64:(i - i0_ + 1) * 64]
                            S.op("dve", lambda e, zc=zc: e.memset(zc, 0.0), writes=[self.rPT[ptb].sub(ml_ * 1024, (ml_ + 1) * 1024)])
                for g0 in range(qs, qe, 4):
                    if "pv" in _SKIP:
                        S.op("dve", lambda e: e.memset(self.YB[:64, :, h * 64:(h + 1) * 64], 0.0), writes=[self.rYB])
                        break
                    g1 = min(qe, g0 + 4)
                    pi = self.next_ps()
                    pv = self.PS[pi][:64, 0:260].rearrange("p (i e) -> p i e", e=65)
                    for i in range(g0, g1):
                        ms = [m for m in range(i // 2, (i + 8) // 2 + 1) if m in binfo]
                        parts = []
                        for m in ms:
                            rows, i0, ml = binfo[m]
                            r0 = 0 if (i <= 2 * m <= i + 8) else 64
                            r1 = 128 if (i <= 2 * m + 1 <= i + 8 and rows == 128) else 64
                            if r1 > r0:
                                if r0 == 64:
                                    r0 = 0
                                parts.append((m, r0, r1, i0, ml))
                        for pi_, (m, r0, r1, i0, ml) in enumerate(parts):
                            S.op("pe", lambda e, m=m, r0=r0, r1=r1, i0=i0, ml=ml, pi_=pi_: e.matmul(
                                pv[:, i - g0, :], lhsT=PTh[r0:r1, ml, (i - i0) * 64:(i - i0 + 1) * 64], rhs=VA[r0:r1, m, h, :],
                                start=(pi_ == 0), stop=(pi_ == len(parts) - 1)),
                                reads=[self.rPT[ptb].sub(ml * 1024, (ml + 1) * 1024), self.rVA], writes=[self.rPS[pi]],
                                sig=(pi_ == len(parts) - 1 and i == g1 - 1))
                    ng = g1 - g0
                    rd = self.RS[:64, 8:8 + ng]
                    S.op("dve", lambda e: e.reciprocal(out=rd, in_=pv[:, 0:ng, 64]), reads=[self.rPS[pi]], writes=[self.rRS])
                    S.op("dve", lambda e: e.tensor_tensor(out=self.YB[:64, g0 - qs:g1 - qs, h * 64:(h + 1) * 64], in0=pv[:, 0:ng, 0:64],
                                                          in1=rd.unsqueeze(2).to_broadcast([64, ng, 64]), op=ALU.mult),
                         reads=[self.rPS[pi], self.rRS], writes=[self.rYB])
            nqh = qe - qs
            for cc in range(4):
                pi = self.next_ps()
                pt = self.PS[pi][:, 0:256].bitcast(BF16).rearrange("p (i q) -> p i q", i=8)
                for i in range(nqh):
                    S.op("pe", lambda e, i=i: e.transpose(out=pt[:, i, :], in_=self.YB[:64, i, cc * 128:(cc + 1) * 128], identity=self.IDB[:64, :64]),
                         reads=[self.rYB, self.rIDB], writes=[self.rPS[pi]], sig=(i == nqh - 1))
                S.op("act", lambda e: e.activation(out=self.MIXT[:, 4 + cc, qs * 64:qe * 64].rearrange("p (i q) -> p i q", q=64),
                                                   in_=pt[:, 0:nqh, :], func=AF.Copy),
                     reads=[self.rPS[pi]], writes=[self.rMIXT.sub((4 + cc) * 2048, (5 + cc) * 2048)])
        if not last:
            S.op("pool", lambda e: e.tensor_copy(out=KF[:, :, 0:512], in_=KF[:, :, T:T + 512]), reads=[self.rKF], writes=[self.rKF])
            S.op("pool", lambda e: e.tensor_copy(out=VA[:, 0:4, :, 0:64], in_=VA[:, T // 128:T // 128 + 4, :, 0:64]), reads=[self.rVA], writes=[self.rVA])
        if os.environ.get("K_MARK"): print("MARK outproj", S.nops)
        for j in range(8):
            for dh in range(2):
                pi = self.next_ps()
                for kc in range(8):
                    lhsT = self.MIXT[:, kc, 0:T].rearrange("p (c j) -> p c j", j=8)[:, :, j]
                    S.op("pe", lambda e, kc=kc, lhsT=lhsT: e.matmul(self.PS[pi][:NC, :], lhsT=lhsT, rhs=self.ABO[:, kc, dh * 512:(dh + 1) * 512],
                                                                    start=(kc == 0), stop=(kc == 7)),
                         reads=[self.rMIXT, self.rABO], writes=[self.rPS[pi]], sig=(kc == 7))
                xv = self.X8[:NC, j, dh * 512:(dh + 1) * 512]
                S.op("dve", lambda e, xv=xv: e.tensor_tensor(out=xv, in0=self.PS[pi][:NC, :], in1=xv, op=ALU.add),
                     reads=[self.rPS[pi], self.rX8], writes=[self.rX8])

    def emit_k_out(self, prompt, hp, c0, n, kv_lo):
        S = self.S
        ok = self.o_pk if prompt else self.o_sk
        for b0 in range(0, n, 128):
            nb = min(128, n - b0)
            t_lo = c0 + b0
            if t_lo + nb <= kv_lo:
                continue
            pi = self.next_ps()
            S.op("pe", lambda e: e.matmul(self.PS[pi][:nb, 0:128], lhsT=self.SG[1][:, b0:b0 + nb], rhs=self.IDF[:, :], start=True, stop=True),
                 reads=[self.rSG[1], self.rIDF], writes=[self.rPS[pi]])
            S.op("dve", lambda e: e.tensor_copy(out=self.SG[0][:nb, 0:128], in_=self.PS[pi][:nb, 0:128]), reads=[self.rPS[pi]], writes=[self.rSG[0]])
            r0 = t_lo - kv_lo
            S.dma("sp", out=ok[r0:r0 + nb, hp * 128:(hp + 1) * 128], in_=self.SG[0][:nb, 0:128], reads=[self.rSG[0]], writes=[], sem="o_sg0")

    def load_cache(self):
        S = self.S
        kb16 = self.SG[1][:, 0:256].bitcast(BF16)
        for kb in range(4):
            S.dma("sp", out=self.SG[0][:, 0:512], in_=self.ck[kb * 128:(kb + 1) * 128, :], reads=[], writes=[self.rSG[0]], sem="ck")
            S.op("dve", lambda e: e.tensor_copy(out=kb16, in_=self.SG[0][:, 0:512]), reads=[self.rSG[0]], writes=[self.rSG[1]])
            for hp in range(4):
                pi = self.next_ps()
                pt = self.PS[pi][:, 0:64].bitcast(BF16)
                S.op("pe", lambda e, hp=hp, pt=pt: e.transpose(out=pt, in_=kb16[:, hp * 128:(hp + 1) * 128], identity=self.IDB[:, :]),
                     reads=[self.rSG[1], self.rIDB], writes=[self.rPS[pi]])
                S.op("act", lambda e, hp=hp, pt=pt: e.activation(out=self.KF[:, hp, kb * 128:(kb + 1) * 128], in_=pt, func=AF.Copy),
                     reads=[self.rPS[pi]], writes=[self.rKF])
            S.dma("sp", out=self.SG[0][:, 0:512], in_=self.cv[kb * 128:(kb + 1) * 128, :], reads=[], writes=[self.rSG[0]], sem="ck")
            S.op("dve", lambda e: e.tensor_copy(out=self.VA[:, kb, :, 0:64], in_=self.SG[0][:, 0:512].rearrange("p (h e) -> p h e", h=8)),
                 reads=[self.rSG[0]], writes=[self.rVA])

    def alloc_mix1_consts(self):
        A = self.A
        self.rS5 = A.alloc(16 * 64 * 4)
        self.S5 = A.ap(self.rS5, F32, "p (s g) -> p s g", s=16)
        self.rDFM = A.alloc(PAGE)
        self.DFM = A.ap(self.rDFM, F32, n=8)
        self.rSST = A.alloc(128 * 4)
        self.SST = A.ap(self.rSST, F32, n=128)

    def setup_mix1(self):
        A, S = self.A, self.S
        z = self.zone
        self.rBU = A.alloc(2 * 64 * 128 * 2, at=z)
        self.rHI = A.alloc(2 * 64 * 128 * 2, at=z + 32768)
        o = z + 66 * 1024 + (4 * 1536 * 2) + ((12 * 520 * 2 + PAGE - 1) // PAGE * PAGE)
        self.rBP = A.alloc(64 * 2 * 64 * 2, at=o); o = self.rBP.hi
        self.rCP = A.alloc(2 * 64 * 16 * 2, at=o); o = self.rCP.hi
        self.rGW = [A.alloc(8 * 512 * 2, at=o), A.alloc(8 * 512 * 2, at=o + 8192)]; o += 16384
        self.rYT = A.alloc(1024 * 2, at=o); o = self.rYT.hi
        self.rTM = A.alloc(3 * 128 * 4, at=o); o = self.rTM.hi
        self.BU = A.ap(self.rBU, BF16, "p (r g t) -> p r g t", r=2, g=64)
        self.HI = A.ap(self.rHI, BF16, "p (r g t) -> p r g t", r=2, g=64)
        self.BP = A.ap(self.rBP, BF16, "p (g r q) -> p g r q", g=64, r=2)
        self.CP = A.ap(self.rCP, BF16, "p (r g i) -> p r g i", r=2, g=64)
        self.GW = [A.ap(r, BF16, "p (k n) -> p k n", k=8) for r in self.rGW]
        self.YT = A.ap(self.rYT, BF16, n=1024)
        self.TM = A.ap(self.rTM, F32, "p (s w) -> p s w", n=384, s=3)
        S5 = self.S5
        stg = A.ap(Reg("sb", z, z + 3 * 64 * 4), F32, "p (s g) -> p s g", n=192, s=3)
        rstg = Reg("sb", z, z + 1024)
        S.dma("sp", out=stg[0:64], in_=self.s5p[:, :].rearrange("p (s g) -> p s g", s=3), reads=[], writes=[rstg], sem="c5a")
        S.dma("sp", out=self.DFM[:, :], in_=self.dfm[:, :], reads=[], writes=[self.rDFM], sem="c5b")
        rS5 = self.rS5
        P64 = slice(0, 64)

        def dv(fn, rd=(rS5, rstg), wr=(rS5,)):
            S.op("dve", fn, reads=list(rd), writes=list(wr))

        def ac(fn, rd=(rS5, rstg), wr=(rS5,)):
            S.op("act", fn, reads=list(rd), writes=list(wr))
        are, aim, ldt = stg[P64, 0, :], stg[P64, 1, :], stg[P64, 2, :]
        sl = lambda k: S5[P64, k, :]
        ac(lambda e: e.activation(out=sl(0), in_=ldt, func=AF.Exp))
        dv(lambda e: e.tensor_tensor(out=sl(1), in0=are, in1=sl(0), op=ALU.mult))
        dv(lambda e: e.tensor_tensor(out=sl(2), in0=aim, in1=sl(0), op=ALU.mult))
        ac(lambda e: e.activation(out=sl(3), in_=sl(1), func=AF.Exp))
        TWO_PI = 2.0 * np.pi
        I32 = mybir.dt.int32

        def sin_of(dst, shift):
            dv(lambda e: e.tensor_scalar(out=sl(13), in0=sl(2), scalar1=shift, scalar2=None, op0=ALU.add))
            dv(lambda e: e.tensor_scalar(out=sl(14), in0=sl(13), scalar1=1.0 / TWO_PI, scalar2=None, op0=ALU.mult))
            dv(lambda e: e.tensor_copy(out=sl(15).bitcast(I32), in_=sl(14)))
            dv(lambda e: e.tensor_copy(out=sl(14), in_=sl(15).bitcast(I32)))
            dv(lambda e: e.scalar_tensor_tensor(out=sl(13), in0=sl(14), scalar=-TWO_PI, in1=sl(13), op0=ALU.mult, op1=ALU.add))
            dv(lambda e: e.tensor_scalar(out=sl(14), in0=sl(13), scalar1=float(np.pi), scalar2=None, op0=ALU.is_gt))
            dv(lambda e: e.scalar_tensor_tensor(out=sl(13), in0=sl(14), scalar=-TWO_PI, in1=sl(13), op0=ALU.mult, op1=ALU.add))
            dv(lambda e: e.tensor_scalar(out=sl(14), in0=sl(13), scalar1=-float(np.pi), scalar2=None, op0=ALU.is_lt))
            dv(lambda e: e.scalar_tensor_tensor(out=sl(13), in0=sl(14), scalar=TWO_PI, in1=sl(13), op0=ALU.mult, op1=ALU.add))
            ac(lambda e: e.activation(out=dst, in_=sl(13), func=AF.Sin))
        sin_of(sl(4), 0.0)
        sin_of(sl(5), float(np.pi / 2))
        dv(lambda e: e.tensor_tensor(out=sl(6), in0=sl(3), in1=sl(5), op=ALU.mult))
        dv(lambda e: e.tensor_tensor(out=sl(7), in0=sl(3), in1=sl(4), op=ALU.mult))
        dv(lambda e: e.tensor_scalar(out=sl(8), in0=sl(7), scalar1=-1.0, scalar2=None, op0=ALU.mult))
        dv(lambda e: e.tensor_tensor(out=sl(15), in0=are, in1=are, op=ALU.mult))
        dv(lambda e: e.tensor_tensor(out=sl(13), in0=aim, in1=aim, op=ALU.mult))
        dv(lambda e: e.tensor_tensor(out=sl(15), in0=sl(15), in1=sl(13), op=ALU.add))
        dv(lambda e: e.reciprocal(out=sl(15), in_=sl(15)))
        dv(lambda e: e.tensor_scalar(out=sl(14), in0=sl(6), scalar1=-1.0, scalar2=None, op0=ALU.add))
        dv(lambda e: e.tensor_tensor(out=sl(9), in0=sl(14), in1=are, op=ALU.mult))
        dv(lambda e: e.tensor_tensor(out=sl(13), in0=sl(7), in1=aim, op=ALU.mult))
        dv(lambda e: e.tensor_tensor(out=sl(9), in0=sl(9), in1=sl(13), op=ALU.add))
        dv(lambda e: e.tensor_tensor(out=sl(9), in0=sl(9), in1=sl(15), op=ALU.mult))
        dv(lambda e: e.tensor_tensor(out=sl(10), in0=sl(7), in1=are, op=ALU.mult))
        dv(lambda e: e.tensor_tensor(out=sl(13), in0=sl(14), in1=aim, op=ALU.mult))
        dv(lambda e: e.tensor_tensor(out=sl(10), in0=sl(10), in1=sl(13), op=ALU.subtract))
        dv(lambda e: e.tensor_tensor(out=sl(10), in0=sl(10), in1=sl(15), op=ALU.mult))
        dv(lambda e: e.tensor_tensor(out=sl(15), in0=sl(9), in1=sl(9), op=ALU.mult))
        dv(lambda e: e.tensor_tensor(out=sl(13), in0=sl(10), in1=sl(10), op=ALU.mult))
        dv(lambda e: e.tensor_tensor(out=sl(15), in0=sl(15), in1=sl(13), op=ALU.add))
        dv(lambda e: e.reciprocal(out=sl(15), in_=sl(15)))
        dv(lambda e: e.tensor_tensor(out=sl(11), in0=sl(9), in1=sl(15), op=ALU.mult))
        dv(lambda e: e.tensor_tensor(out=sl(12), in0=sl(10), in1=sl(15), op=ALU.mult))
        dv(lambda e: e.tensor_scalar(out=sl(12), in0=sl(12), scalar1=-1.0, scalar2=None, op0=ALU.mult))
        rc = Reg("sb", z + 4096, z + 4096 + 2 * 64 * 16 * 4)
        cst = A.ap(rc, F32, "p (r g i) -> p r g i", r=2, g=64)
        S.dma("sp", out=cst[P64], in_=self.s5c[:, :].rearrange("p (r g i) -> p r g i", r=2, g=64), reads=[], writes=[rc], sem="c5c")
        ro = Reg("sb", z + 16384, z + 16384 + 2 * 64 * 16 * 4)
        cot = A.ap(ro, F32, "p (r g i) -> p r g i", r=2, g=64)
        rt = Reg("sb", z + 28672, z + 28672 + 64 * 16 * 4)
        tmp = A.ap(rt, F32, "p (g i) -> p g i", g=64)
        bc = lambda k: S5[P64, k, :].unsqueeze(2).to_broadcast([64, 64, 16])
        S.op("dve", lambda e: e.tensor_tensor(out=cot[P64, 0], in0=cst[P64, 0], in1=bc(9), op=ALU.mult), reads=[rc, rS5], writes=[ro])
        S.op("dve", lambda e: e.tensor_tensor(out=tmp[P64], in0=cst[P64, 1], in1=bc(10), op=ALU.mult), reads=[rc, rS5], writes=[rt])
        S.op("dve", lambda e: e.tensor_tensor(out=cot[P64, 0], in0=cot[P64, 0], in1=tmp[P64], op=ALU.subtract), reads=[ro, rt], writes=[ro])
        S.op("dve", lambda e: e.tensor_tensor(out=cot[P64, 1], in0=cst[P64, 0], in1=bc(10), op=ALU.mult), reads=[rc, rS5], writes=[ro])
        S.op("dve", lambda e: e.tensor_tensor(out=tmp[P64], in0=cst[P64, 1], in1=bc(9), op=ALU.mult), reads=[rc, rS5], writes=[rt])
        S.op("dve", lambda e: e.tensor_tensor(out=cot[P64, 1], in0=cot[P64, 1], in1=tmp[P64], op=ALU.add), reads=[ro, rt], writes=[ro])
        S.op("dve", lambda e: e.tensor_scalar(out=cot[P64, 1], in0=cot[P64, 1], scalar1=-1.0, scalar2=None, op0=ALU.mult), reads=[ro], writes=[ro])
        rcb = Reg("sb", z + 36864, z + 36864 + 2 * 64 * 16 * 2)
        cob = A.ap(rcb, BF16, "p (r g i) -> p r g i", r=2, g=64)
        S.op("dve", lambda e: e.tensor_copy(out=cob[P64], in_=cot[P64]), reads=[ro], writes=[rcb])
        S.dma("sp", out=self.s_cp[:, :].rearrange("p (r g i) -> p r g i", r=2, g=64), in_=cob[P64], reads=[rcb], writes=[self.R_scr], sem="c5d")

    def mixer1(self, NC, prompt, first, last):
        S, A = self.S, self.A
        T = 8 * NC
        S5, SST, TM = self.S5, self.SST, self.TM
        P64 = slice(0, 64)
        self.norm_fm(NC, 4)
        S.dma("sp", out=self.BP[:, :, :, :], in_=self.s_bp[:, :].rearrange("p (g r q) -> p g r q", g=64, r=2), reads=[self.R_scr], writes=[self.rBP], sem="bp")
        S.dma("sp", out=self.CP[P64], in_=self.s_cp[:, :].rearrange("p (r g i) -> p r g i", r=2, g=64), reads=[self.R_scr], writes=[self.rCP], sem="cp")
        rS5, rSST, rTM = self.rS5, self.rSST, self.rTM
        sre, sim = SST[P64, 0:64], SST[P64, 64:128]
        if first:
            if prompt:
                S.op("dve", lambda e: e.memset(SST[P64, :], 0.0), writes=[rSST])
            else:
                S.dma("sp", out=TM[P64, 0, :], in_=self.s0[:, :], reads=[], writes=[rTM], sem="s0")
                a, b = TM[P64, 0, 0:64], TM[P64, 0, 64:128]
                S.op("dve", lambda e: e.tensor_tensor(out=sre, in0=a, in1=S5[P64, 11, :], op=ALU.mult), reads=[rTM, rS5], writes=[rSST])
                S.op("dve", lambda e: e.tensor_tensor(out=TM[P64, 1, 0:64], in0=b, in1=S5[P64, 12, :], op=ALU.mult), reads=[rTM, rS5], writes=[rTM])
                S.op("dve", lambda e: e.tensor_tensor(out=sre, in0=sre, in1=TM[P64, 1, 0:64], op=ALU.subtract), reads=[rTM, rSST], writes=[rSST])
                S.op("dve", lambda e: e.tensor_tensor(out=sim, in0=a, in1=S5[P64, 12, :], op=ALU.mult), reads=[rTM, rS5], writes=[rSST])
                S.op("dve", lambda e: e.tensor_tensor(out=TM[P64, 1, 0:64], in0=b, in1=S5[P64, 11, :], op=ALU.mult), reads=[rTM, rS5], writes=[rTM])
                S.op("dve", lambda e: e.tensor_tensor(out=sim, in0=sim, in1=TM[P64, 1, 0:64], op=ALU.add), reads=[rTM, rSST], writes=[rSST])
        lr2 = S5[P64, 6:7, :].to_broadcast([64, 2, 64])
        st2 = SST[P64, :].rearrange("p (r g) -> p r g", r=2)
        t1 = TM[P64, 1, :]
        t1v = TM[P64, 1, :].rearrange("p (r g) -> p r g", r=2)
        t2 = TM[P64, 2, :]
        for t0 in range(0, T, 128):
            nt = min(128, T - t0)
            for g4 in range(0, 64, 2):
                pi = self.next_ps()
                for gi in range(2):
                    g = g4 + gi
                    for r in range(2):
                        S.op("pe", lambda e, g=g, r=r, gi=gi: e.matmul(self.PS[pi][:64, (gi * 2 + r) * 128:(gi * 2 + r) * 128 + nt], lhsT=self.BP[:, g, r, :],
                                                                      rhs=self.XNT[:, g // 8, t0:t0 + nt], start=True, stop=True),
                             reads=[self.rBP, self.rXNT], writes=[self.rPS[pi]], sig=(gi == 1 and r == 1))
                src = self.PS[pi][:64, :].rearrange("p (g r t) -> p r g t", g=2, r=2)[:, :, :, 0:nt]
                dst = self.BU[P64, :, g4:g4 + 2, 0:nt]
                eng = "act" if (g4 // 2) % 2 == 0 else "dve"
                if eng == "act":
                    for r in range(2):
                        S.op("act", lambda e, r=r: e.activation(out=dst[:, r], in_=src[:, r], func=AF.Copy), reads=[self.rPS[pi]], writes=[self.rBU])
                else:
                    for r in range(2):
                        S.op("dve", lambda e, r=r: e.tensor_copy(out=dst[:, r], in_=src[:, r]), reads=[self.rPS[pi]], writes=[self.rBU])
            for t in range(nt):
                S.op("dve", lambda e: e.tensor_tensor(out=t1v, in0=st2, in1=lr2, op=ALU.mult), reads=[rSST, rS5], writes=[rTM])
                S.op("dve", lambda e: e.tensor_tensor(out=t2[:, 0:64], in0=sim, in1=S5[P64, 8, :], op=ALU.mult), reads=[rSST, rS5], writes=[rTM])
                S.op("dve", lambda e: e.tensor_tensor(out=t2[:, 64:128], in0=sre, in1=S5[P64, 7, :], op=ALU.mult), reads=[rSST, rS5], writes=[rTM])
                S.op("dve", lambda e: e.tensor_tensor(out=t1, in0=t1, in1=t2, op=ALU.add), reads=[rTM], writes=[rTM])
                S.op("dve", lambda e, t=t: e.tensor_tensor(out=SST[P64, :], in0=t1, in1=self.BU[P64, :, :, t].rearrange("p r g -> p (r g)"), op=ALU.add),
                     reads=[rTM, self.rBU], writes=[rSST])
                S.op("act", lambda e, t=t: e.activation(out=self.HI[P64, :, :, t].rearrange("p r g -> p (r g)"), in_=SST[P64, :], func=AF.Copy),
                     reads=[rSST], writes=[self.rHI])
            pa, pb = self.next_ps(), self.next_ps()
            for g in range(64):
                pi = pa if g < 32 else pb
                col = (g % 32) * 16
                for r in range(2):
                    S.op("pe", lambda e, g=g, r=r: e.matmul(self.PS[pi][:nt, col:col + 16], lhsT=self.HI[P64, r, g, 0:nt], rhs=self.CP[P64, r, g, :],
                                                             start=(r == 0), stop=(r == 1)),
                         reads=[self.rHI, self.rCP], writes=[self.rPS[pi]], sig=(r == 1 and g % 32 == 31))
            for hh, pi in ((0, pa), (1, pb)):
                S.op("act", lambda e: e.activation(out=self.YT[:nt, hh * 512:(hh + 1) * 512], in_=self.PS[pi][:nt, :], func=AF.Copy),
                     reads=[self.rPS[pi]], writes=[self.rYT])
            for kc in range(8):
                pi = self.next_ps()
                pt = self.PS[pi][:, 0:64].bitcast(BF16)
                S.op("pe", lambda e: e.transpose(out=pt[:, 0:nt], in_=self.YT[:nt, kc * 128:(kc + 1) * 128], identity=self.IDB[:nt, :nt]),
                     reads=[self.rYT, self.rIDB], writes=[self.rPS[pi]])
                xc = self.XNT[:, kc, t0:t0 + nt]
                S.op("dve", lambda e, xc=xc: e.scalar_tensor_tensor(out=xc, in0=xc, scalar=self.DFM[:, kc:kc + 1], in1=pt[:, 0:nt], op0=ALU.mult, op1=ALU.add),
                     reads=[self.rPS[pi], self.rXNT, self.rDFM], writes=[self.rXNT])
        if last:
            ore, oim = (self.o_pre, self.o_pim) if prompt else (self.o_sre, self.o_sim)
            fr, fi = TM[P64, 1, 0:64], TM[P64, 1, 64:128]
            S.op("dve", lambda e: e.tensor_tensor(out=fr, in0=sre, in1=S5[P64, 9, :], op=ALU.mult), reads=[rSST, rS5], writes=[rTM])
            S.op("dve", lambda e: e.tensor_tensor(out=t2[:, 0:64], in0=sim, in1=S5[P64, 10, :], op=ALU.mult), reads=[rSST, rS5], writes=[rTM])
            S.op("dve", lambda e: e.tensor_tensor(out=fr, in0=fr, in1=t2[:, 0:64], op=ALU.subtract), reads=[rTM], writes=[rTM])
            S.op("dve", lambda e: e.tensor_tensor(out=fi, in0=sre, in1=S5[P64, 10, :], op=ALU.mult), reads=[rSST, rS5], writes=[rTM])
            S.op("dve", lambda e: e.tensor_tensor(out=t2[:, 0:64], in0=sim, in1=S5[P64, 9, :], op=ALU.mult), reads=[rSST, rS5], writes=[rTM])
            S.op("dve", lambda e: e.tensor_tensor(out=fi, in0=fi, in1=t2[:, 0:64], op=ALU.add), reads=[rTM], writes=[rTM])
            for src, od in ((fr, ore), (fi, oim)):
                pi = self.next_ps()
                S.op("pe", lambda e, src=src: e.matmul(self.PS[pi][:64, 0:64], lhsT=src, rhs=self.IDF[0:64, 0:64], start=True, stop=True),
                     reads=[rTM, self.rIDF], writes=[self.rPS[pi]])
                S.op("act", lambda e: e.activation(out=self.SG[0][:64, 0:64], in_=self.PS[pi][:64, 0:64], func=AF.Copy), reads=[self.rPS[pi]], writes=[self.rSG[0]])
                S.dma("sp", out=od[:, :], in_=self.SG[0][:64, 0:64], reads=[self.rSG[0]], writes=[], sem="o_sg0")
        for q in range(4):
            sl = q % 2
            gw = self.GW[sl]
            for half in range(2):
                S.dma("sp", out=gw[:, :, half * 256:(half + 1) * 256],
                      in_=self.s_glu[:, :].rearrange("p (k n) -> p k n", k=8)[:, :, half * 1024 + q * 256:half * 1024 + (q + 1) * 256],
                      reads=[self.R_scr], writes=[self.rGW[sl]], sem="gw%d_%d" % (sl, half))
            for j in range(8):
                pa, pb = self.next_ps(), self.next_ps()
                for half, pi in ((0, pa), (1, pb)):
                    for kc in range(8):
                        lhsT = self.XNT[:, kc, 0:T].rearrange("p (c j) -> p c j", j=8)[:, :, j]
                        S.op("pe", lambda e, kc=kc, lhsT=lhsT, half=half: e.matmul(self.PS[pi][:NC, 0:256], lhsT=lhsT, rhs=gw[:, kc, half * 256:(half + 1) * 256],
                                                                                   start=(kc == 0), stop=(kc == 7)),
                             reads=[self.rXNT, self.rGW[sl]], writes=[self.rPS[pi]], sig=(kc == 7))
                sg = self.SG[1][:NC, 0:256]
                S.op("act", lambda e: e.activation(out=sg, in_=self.PS[pb][:NC, 0:256], func=AF.Tanh, scale=0.5), reads=[self.rPS[pb]], writes=[self.rSG[1]])
                S.op("dve", lambda e: e.scalar_tensor_tensor(out=sg, in0=sg, scalar=1.0, in1=self.PS[pa][:NC, 0:256], op0=ALU.add, op1=ALU.mult),
                     reads=[self.rSG[1], self.rPS[pa]], writes=[self.rSG[1]])
                xv = self.X8[:NC, j, q * 256:(q + 1) * 256]
                S.op("dve", lambda e, xv=xv: e.scalar_tensor_tensor(out=xv, in0=sg, scalar=0.5, in1=xv, op0=ALU.mult, op1=ALU.add),
                     reads=[self.rSG[1], self.rX8], writes=[self.rX8])


def _lay_win(w):
    a = w.reshape(8, 128, 2, NF, 128)
    return np.ascontiguousarray(a.transpose(3, 1, 0, 2, 4)).reshape(NF * 128, 2048)


def _lay_rows(w):
    k = w.shape[0] // 128
    return np.ascontiguousarray(w.reshape(k, 128, w.shape[1]).transpose(1, 0, 2)).reshape(128, k * w.shape[1])


def _lay_cols(w, c0, nchunks):
    a = w[:, c0:c0 + nchunks * 128].reshape(8, 128, nchunks, 128)
    return np.ascontiguousarray(a.transpose(2, 1, 0, 3)).reshape(nchunks * 128, 1024)


def _fm(v):
    v = np.asarray(v, np.float32).reshape(-1, 4, 128)
    return np.ascontiguousarray(v.transpose(2, 0, 1)).reshape(128, -1)


def host_common(inp):
    f32 = np.float32
    d = {}
    w_in = [inp["ffn1_w_in"][0], inp["ffn2_w_in"][0], inp["ffn1_w_in"][1], inp["ffn2_w_in"][1]]
    w_out = [inp["ffn1_w_out"][0], inp["ffn2_w_out"][0], inp["ffn1_w_out"][1], inp["ffn2_w_out"][1]]
    d["w_ffn_in"] = np.stack([_lay_win(np.asarray(w, f32)) for w in w_in])
    d["w_ffn_out"] = np.stack([_lay_rows(np.asarray(w, f32)) for w in w_out])
    gam = np.stack([inp["ffn1_norm"][0], inp["mix_norm"][0], inp["ffn2_norm"][0],
                    inp["ffn1_norm"][1], inp["mix_norm"][1], inp["ffn2_norm"][1]]).astype(f32)
    d["gam"] = np.ascontiguousarray(gam.reshape(6, 8, 128).transpose(2, 0, 1)).reshape(128, 48)
    wab = np.asarray(inp["ab_w_in"][0], f32)
    d["w_abin"] = _lay_cols(wab, 0, 16)
    d["w_abv"] = _lay_rows(np.ascontiguousarray(wab[:, 2048:2560]))
    d["w_about"] = _lay_rows(np.asarray(inp["ab_w_out"][0], f32))
    m0c = np.zeros((128, 46), f32)
    cw = np.asarray(inp["conv_w"][0], f32)
    for c in range(4):
        for k in range(4):
            m0c[:, 4 * c + k] = cw[k, c * 128:(c + 1) * 128]
    m0c[:, 16:20] = _fm(inp["conv_b"][0])
    m0c[:, 20:24] = _fm(inp["lru_ba"][0])
    m0c[:, 24:28] = _fm(inp["lru_bx"][0])
    m0c[:, 28:32] = _fm(inp["lru_lambda"][0])
    m0c[:, 36] = np.tile(np.asarray(inp["q_norm"][0], f32), 2)
    m0c[:, 37] = np.tile(np.asarray(inp["k_norm"][0], f32), 2)
    rb = np.asarray(inp["rel_bias"][0], f32)
    m0c[:, 38:46] = rb[256][None, :]
    d["m0c"] = m0c
    wbd = np.zeros((128, 2, 4, 128), f32)
    for w, nm in enumerate(("lru_wa", "lru_wx")):
        W = np.asarray(inp[nm][0], f32)
        for c in range(4):
            wbd[0:64, w, c, 0:64] = W[2 * c]
            wbd[64:128, w, c, 64:128] = W[2 * c + 1]
    d["wbd"] = wbd.reshape(128, -1)
    p = np.arange(128)
    kl = np.where(p < 64, p, p - 64)
    bb = np.zeros((128, 8, 5, 64), f32)
    q = np.arange(64)
    for jj in range(5):
        cp = np.where(p < 64, jj, jj - 1)
        rel = q[None, :] - kl[:, None] + 64 * cp[:, None]
        idx = np.clip(rel, -128, 128) + 128
        bb[:, :, jj, :] = rb[idx].transpose(0, 2, 1)
    d["bblk"] = bb.reshape(128, -1)
    are = np.asarray(inp["ssm_A_re"][0], f32).T
    aim = np.asarray(inp["ssm_A_im"][0], f32).T
    ldt = np.broadcast_to(np.asarray(inp["ssm_log_dt"][0], f32)[None, :], (64, 64))
    d["s5p"] = np.ascontiguousarray(np.concatenate([are, aim, ldt], axis=1))
    cre = np.asarray(inp["ssm_C_re"][0], f32).transpose(2, 0, 1)
    cim = np.asarray(inp["ssm_C_im"][0], f32).transpose(2, 0, 1)
    d["s5c"] = np.ascontiguousarray(np.stack([cre, cim], axis=1)).reshape(64, 2048)
    bp = np.zeros((128, 64, 2, 64), f32)
    for r, nm in enumerate(("ssm_B_re", "ssm_B_im")):
        Bm = np.asarray(inp[nm][0], f32)
        for g in range(64):
            bp[16 * (g % 8):16 * (g % 8) + 16, g, r, :] = Bm[g].T
    d["w_bp"] = bp.reshape(128, 8192)
    d["w_glu"] = _lay_rows(np.asarray(inp["glu_w"][0], f32))
    d["dfm"] = np.ascontiguousarray(np.asarray(inp["ssm_D"][0], f32).reshape(8, 128).T)
    return d


def host_core(inp, common, c, SEQ):
    f32 = np.float32
    d = dict(common)
    d["xp"] = np.ascontiguousarray(np.asarray(inp["x_prompt"][c % inp["x_prompt"].shape[0]], f32)[:SEQ])
    d["xs"] = np.ascontiguousarray(np.asarray(inp["x_sample"][c], f32))
    st = np.zeros((128, 16), f32)
    st[:, 0:4] = _fm(inp["state_rglru_h"][0, c])
    cv = np.asarray(inp["state_rglru_conv"][0, c], f32)
    for cc in range(4):
        for k in range(3):
            st[:, 4 + 3 * cc + k] = cv[k, cc * 128:(cc + 1) * 128]
    d["st_rg"] = st
    d["ck"] = np.ascontiguousarray(np.asarray(inp["cache_band_k"][0, c], f32).reshape(512, 512))
    d["cv"] = np.ascontiguousarray(np.asarray(inp["cache_band_v"][0, c], f32).reshape(512, 512))
    d["s0"] = np.ascontiguousarray(np.concatenate([np.asarray(inp["state_ssm_re"][0, c], f32).T, np.asarray(inp["state_ssm_im"][0, c], f32).T], axis=1))
    return d


_STAGES = ("ffn", "mix0", "mix1")


def kernel(**inputs):
    inp = {k: np.asarray(v) for k, v in inputs.items()}
    SEQ = inp["x_prompt"].shape[1]
    B = inp["x_prompt"].shape[0]
    NS = inp["x_sample"].shape[0]
    common = host_common(inp)
    in_maps = [host_core(inp, common, c, SEQ) for c in range(8)]
    b = Builder(SEQ=SEQ, stages=_STAGES)
    nc = b.build()
    res = run_bass_kernel_spmd(nc, in_maps, core_ids=list(range(8)))
    rs = res.results
    f32 = np.float32
    KR = min(512, SEQ)

    def st(name, n, shape):
        return np.stack([np.asarray(rs[c][name], f32).reshape(shape) for c in range(n)])[None]

    y_prompt = np.stack([np.asarray(rs[c]["yp"], f32) for c in range(B)])
    y_sample = np.stack([np.asarray(rs[c]["ys"], f32) for c in range(NS)])
    return (y_prompt, y_sample,
            st("o_pconv", B, (3, 512)), st("o_ph", B, (512,)), st("o_pk", B, (KR, 8, 64)), st("o_pv", B, (KR, 8, 64)),
            st("o_pre", B, (64, 64)), st("o_pim", B, (64, 64)),
            st("o_sconv", NS, (3, 512)), st("o_sh", NS, (512,)), st("o_sk", NS, (64, 8, 64)), st("o_sv", NS, (64, 8, 64)),
            st("o_sre", NS, (64, 64)), st("o_sim", NS, (64, 64)))
```

```python
import contextlib
import os
import numpy as np
_SKIP = set(os.environ.get('K_SKIP', '').split(','))
_STOP = int(os.environ.get('K_STOP', '1000000000'))
_LIST = [int(v) for v in os.environ['K_LIST'].split(',')] if os.environ.get('K_LIST') else None
import concourse.bass as bass
import concourse.mybir as mybir
from concourse.bass_utils import run_bass_kernel_spmd

F32 = mybir.dt.float32
BF16 = mybir.dt.bfloat16
AF = mybir.ActivationFunctionType
ALU = mybir.AluOpType
AX = mybir.AxisListType

D = 1024
DFF = 2816
NF = 22
DK = 8
EPS = 1e-6
PAGE = 256


class Reg:
    __slots__ = ("space", "lo", "hi")

    def __init__(self, space, lo, hi):
        self.space, self.lo, self.hi = space, lo, hi

    def sub(self, lo, hi):
        assert 0 <= lo < hi <= self.hi - self.lo, (lo, hi, self.lo, self.hi)
        return Reg(self.space, self.lo + lo, self.lo + hi)


class Sched:
    def __init__(self, nc, es):
        self.nc = nc
        self.es = es
        self.eng = {"pe": nc.tensor, "act": nc.scalar, "dve": nc.vector, "pool": nc.gpsimd, "sp": nc.sync}
        self.sems = {}
        self.cnt = {}
        for k in self.eng:
            self.sems[k] = es.enter_context(nc.semaphore("s_" + k))
            self.cnt[k] = 0
        self.pending = {k: False for k in self.eng}
        self.waited = {k: {} for k in self.eng}
        self.pages = {}
        self.nops = 0

    def dma_sem(self, name):
        key = "d_" + name
        if key not in self.sems:
            self.sems[key] = self.es.enter_context(self.nc.semaphore(key))
            self.cnt[key] = 0
        return key

    def _pg(self, regs):
        for r in regs:
            for p in range(r.lo // PAGE, (r.hi + PAGE - 1) // PAGE):
                yield (r.space, p)

    def _collect(self, reads, writes):
        deps = {}

        def add(tok):
            if tok is not None:
                k, v = tok
                if deps.get(k, 0) < v:
                    deps[k] = v

        for pg in self._pg(reads):
            e = self.pages.get(pg)
            if e is not None:
                add(e[0])
        for pg in self._pg(writes):
            e = self.pages.get(pg)
            if e is not None:
                add(e[0])
                for k, v in e[1].items():
                    add((k, v))
        return deps

    def _waits(self, E, deps):
        w = self.waited[E]
        for k, v in deps.items():
            if k == E and E == "pe":
                continue
            if w.get(k, 0) < v:
                self.eng[E].wait_ge(self.sems[k], v)
                w[k] = v

    def _record(self, tok, reads, writes):
        k, v = tok
        for pg in self._pg(reads):
            e = self.pages.get(pg)
            if e is None:
                e = [None, {}]
                self.pages[pg] = e
            e[1][k] = v
        for pg in self._pg(writes):
            self.pages[pg] = [tok, {}]

    def op(self, E, fn, reads=(), writes=(), sig=True):
        if self.nops >= _STOP:
            self.nops += 1
            return None
        deps = self._collect(reads, writes)
        self._waits(E, deps)
        if _LIST and _LIST[0] <= self.nops < _LIST[1]:
            print("OP", self.nops, E, fn.__code__.co_firstlineno)
        ins = fn(self.eng[E])
        tick = self.cnt[E] + 1
        if sig:
            ins.then_inc(self.sems[E], 1)
            self.cnt[E] = tick
            self.pending[E] = False
        else:
            self.pending[E] = True
        self._record((E, tick), reads, writes)
        self.nops += 1
        return ins

    def dma(self, Q, out, in_, reads, writes, sem, slow=False):
        if self.nops >= _STOP:
            self.nops += 1
            return None
        key = self.dma_sem(sem)
        deps = self._collect(reads, writes)
        self._waits(Q, deps)
        if slow:
            ins = self.eng[Q].dma_start(out=out, in_=in_, allow_slow_non_contiguous=True)
        else:
            ins = self.eng[Q].dma_start(out=out, in_=in_)
        ins.then_inc(self.sems[key], 16)
        self.cnt[key] += 16
        self._record((key, self.cnt[key]), reads, writes)
        self.nops += 1
        return ins

    def finish(self):
        if _STOP >= 1000000000:
            assert not any(self.pending.values()), self.pending
        sp = self.eng["sp"]
        for k, v in self.cnt.items():
            if k != "sp" and v > 0:
                sp.wait_ge(self.sems[k], v)


class Arena:
    def __init__(self, nc, es, name, nbytes, space):
        self.t = es.enter_context(nc.sbuf_tensor(name, [128, nbytes // 4], F32))
        self.space = space
        self.nbytes = nbytes
        self.top = 0

    def alloc(self, nbytes, at=None):
        nbytes = (nbytes + PAGE - 1) // PAGE * PAGE
        if at is None:
            at = self.top
            self.top += nbytes
            assert self.top <= self.nbytes, ("arena overflow", self.top, self.nbytes)
        assert at + nbytes <= self.nbytes, ("arena overflow", at, nbytes, self.nbytes)
        return Reg(self.space, at, at + nbytes)

    def ap(self, reg, dtype, pattern=None, n=None, **kw):
        a = self.t[:, reg.lo // 4: reg.hi // 4]
        if dtype != F32:
            a = a.bitcast(dtype)
        if n is not None:
            a = a[:, 0:n]
        if pattern is not None:
            a = a.rearrange(pattern, **kw)
        return a


class Builder:
    def __init__(self, SEQ=8192, stages=("ffn", "mix0", "mix1"), with_sample=True, debug=False):
        self.SEQ = SEQ
        self.stages = stages
        self.with_sample = with_sample
        self.debug = debug
        self.nc = bass.Bass("TRN2", target_bir_lowering=False)
        self.es = contextlib.ExitStack()

    def din(self, name, shape, dtype=F32):
        return self.nc.dram_tensor(name, list(shape), dtype, kind="ExternalInput").ap()

    def dout(self, name, shape, dtype=F32):
        return self.nc.dram_tensor(name, list(shape), dtype, kind="ExternalOutput").ap()

    def dscr(self, name, shape, dtype=BF16):
        return self.nc.dram_tensor(name, list(shape), dtype, kind="Internal").ap()

    def build(self):
        with self.es:
            self._build()
        return self.nc

    def _build(self):
        nc, es = self.nc, self.es
        SEQ = self.SEQ
        S = self.S = Sched(nc, es)
        self.xp = self.din("xp", [max(SEQ, 8), D])
        self.xs = self.din("xs", [64, D])
        self.yp = self.dout("yp", [max(SEQ, 8), D])
        self.ys = self.dout("ys", [64, D])
        self.w_ffn_in = self.din("w_ffn_in", [4, NF * 128, 2048])
        self.w_ffn_out = self.din("w_ffn_out", [4, 128, NF * 1024])
        self.gam = self.din("gam", [128, 6 * 8])
        self.s_ffn_in = self.dscr("s_ffn_in", [4, NF * 128, 2048])
        self.s_ffn_out = self.dscr("s_ffn_out", [4, 128, NF * 1024])
        self.R_scr = Reg("dram", 0, PAGE)
        self.extra_pairs = []
        if "mix0" in self.stages:
            self.w_abin = self.din("w_abin", [16 * 128, 1024])
            self.w_abv = self.din("w_abv", [128, 8 * 512])
            self.w_about = self.din("w_about", [128, 8 * 1024])
            self.s_abin = self.dscr("s_abin", [16 * 128, 1024])
            self.s_abv = self.dscr("s_abv", [128, 8 * 512])
            self.s_about = self.dscr("s_about", [128, 8 * 1024])
            self.extra_pairs += [(self.w_abin, self.s_abin), (self.w_abv, self.s_abv), (self.w_about, self.s_about)]
            self.m0off = dict(cw=0, cb=16, ba=20, bx=24, lam=28, c1=32, gq=36, gk=37, bc=38)
            self.NM0 = 46
            self.m0c = self.din("m0c", [128, self.NM0])
            self.wbd = self.din("wbd", [128, 2 * 4 * 128])
            self.bblk = self.din("bblk", [128, 8 * 5 * 64])
            self.st_rg = self.din("st_rg", [128, 16])
            self.ck = self.din("ck", [512, 512])
            self.cv = self.din("cv", [512, 512])
            KR = max(8, min(512, SEQ))
            self.o_pconv = self.dout("o_pconv", [3, 512])
            self.o_ph = self.dout("o_ph", [512])
            self.o_pk = self.dout("o_pk", [KR, 512])
            self.o_pv = self.dout("o_pv", [KR, 512])
            self.o_sconv = self.dout("o_sconv", [3, 512])
            self.o_sh = self.dout("o_sh", [512])
            self.o_sk = self.dout("o_sk", [64, 512])
            self.o_sv = self.dout("o_sv", [64, 512])

        if "mix1" in self.stages:
            self.s5p = self.din("s5p", [64, 192])
            self.s5c = self.din("s5c", [64, 2048])
            self.w_bp = self.din("w_bp", [128, 8192])
            self.s_bp = self.dscr("s_bp", [128, 8192])
            self.s_cp = self.dscr("s_cp", [64, 2048])
            self.w_glu = self.din("w_glu", [128, 8 * 2048])
            self.s_glu = self.dscr("s_glu", [128, 8 * 2048])
            self.dfm = self.din("dfm", [128, 8])
            self.s0 = self.din("s0", [64, 128])
            self.extra_pairs += [(self.w_bp, self.s_bp), (self.w_glu, self.s_glu)]
            self.o_pre = self.dout("o_pre", [64, 64])
            self.o_pim = self.dout("o_pim", [64, 64])
            self.o_sre = self.dout("o_sre", [64, 64])
            self.o_sim = self.dout("o_sim", [64, 64])
        A = self.A = Arena(nc, es, "arena", 207 * 1024, "sb")
        self.rX8 = A.alloc(8 * 1024 * 4)
        self.rXNT = A.alloc(8 * 1024 * 2)
        self.rWIN = [A.alloc(2048 * 2) for _ in range(2)]
        self.rGAM = A.alloc(6 * 8 * 4)
        self.rIDB = A.alloc(128 * 2)
        self.rIDF = A.alloc(128 * 4)
        self.rSS = A.alloc(PAGE)
        self.rRS = A.alloc(PAGE)
        self.rSG = [A.alloc(512 * 4) for _ in range(2)]
        if "mix0" in self.stages:
            self.alloc_mix0_consts()
        if "mix1" in self.stages:
            self.alloc_mix1_consts()
        self.zone = A.top
        self.rWOUT = A.alloc(NF * 1024 * 2, at=self.zone)
        self.rH = A.alloc(NF * 512 * 2, at=self.zone + NF * 1024 * 2)
        self.rXS = A.alloc(8 * 1024 * 2, at=self.zone + NF * 1024 * 2)

        self.X8 = A.ap(self.rX8, F32, "p (j d) -> p j d", j=8)
        self.XNT = A.ap(self.rXNT, BF16, "p (k t) -> p k t", k=8)
        self.XS = A.ap(self.rXS, BF16, "p (j d) -> p j d", j=8)
        self.WIN = [A.ap(r, BF16, "p (k n) -> p k n", k=8) for r in self.rWIN]
        self.GAM = A.ap(self.rGAM, F32, "p (w k) -> p w k", n=48, w=6)
        self.IDB = A.ap(self.rIDB, BF16, n=128)
        self.IDF = A.ap(self.rIDF, F32, n=128)
        self.SS = A.ap(self.rSS, F32)
        self.RS = A.ap(self.rRS, F32)
        self.SG = [A.ap(r, F32) for r in self.rSG]
        self.WOUT = A.ap(self.rWOUT, BF16, "p (f n) -> p f n", f=NF)
        self.H = A.ap(self.rH, BF16, "p (f n) -> p f n", f=NF)

        self.PS = [es.enter_context(nc.psum_tensor("ps%d" % i, [128, 512], F32)) for i in range(8)]
        self.rPS = [Reg("ps", i * PAGE, (i + 1) * PAGE) for i in range(8)]
        self.ps_rr = 0
        self.win_rr = 0
        self.sg_rr = 0

        self.setup_consts()
        self.prepass()
        if "mix0" in self.stages:
            self.setup_mix0()
            self.setup_mix0_consts()
        if "mix1" in self.stages:
            self.setup_mix1()
        ntile = SEQ // 1024
        for t in range(ntile):
            self.macro_tile(self.xp, self.yp, t * 1024, 128, prompt=True, first=(t == 0), last=(t == ntile - 1))
        if self.with_sample:
            self.macro_tile(self.xs, self.ys, 0, 8, prompt=False, first=True, last=True)
        S.finish()

    def dbg(self, name, ap, reg, shape):
        if not self.debug:
            return
        o = self.dout("dbg_" + name, shape)
        self.S.dma("sp", out=o, in_=ap, reads=[reg], writes=[], sem="dbg_" + name)

    def next_ps(self, n=1):
        i = self.ps_rr
        self.ps_rr = (self.ps_rr + 1) % 8
        return i

    def setup_consts(self):
        S, A = self.S, self.A
        S.dma("sp", out=self.A.ap(self.rGAM, F32, n=48), in_=self.gam[:, :], reads=[], writes=[self.rGAM], sem="const1")
        idf = self.IDF
        S.op("pool", lambda e: e.memset(idf[:, :], 0.0), writes=[self.rIDF])
        S.op("pool", lambda e: e.affine_select(out=idf[:, :], in_=idf[:, :], pattern=[[1, 128]],
                                                compare_op=ALU.not_equal, fill=1.0, base=0, channel_multiplier=-1),
             reads=[self.rIDF], writes=[self.rIDF])
        S.op("pool", lambda e: e.tensor_copy(out=self.IDB[:, :], in_=idf[:, :]), reads=[self.rIDF], writes=[self.rIDB])

    def prepass(self):
        S = self.S
        pairs = []
        for w in range(4):
            pairs.append((self.w_ffn_in[w], self.s_ffn_in[w]))
            pairs.append((self.w_ffn_out[w], self.s_ffn_out[w]))
        pairs += getattr(self, "extra_pairs", [])
        for src, dst in pairs:
            rows, cols = src.shape
            step = max(1, (1 << 20) // cols)
            for r0 in range(0, rows, step):
                r1 = min(rows, r0 + step)
                S.dma("pool", out=dst[r0:r1, :], in_=src[r0:r1, :], reads=[], writes=[self.R_scr], sem="pre")

    def macro_tile(self, xin, yout, t0, NC, prompt, first, last):
        S = self.S
        T = 8 * NC
        X8 = self.X8
        S.dma("sp", out=X8[:NC, :, :], in_=xin[t0:t0 + T, :].rearrange("(c j) d -> c j d", j=8),
              reads=[], writes=[self.rX8], sem="xin")
        subt = [(0, 4), (4, 4)] if NC == 128 else [(0, 8)]
        for l in range(2):
            if "ffn" in self.stages:
                self.norm_fm(NC, 3 * l + 0)
                self.ffn(NC, 2 * l + 0, subt)
            if l == 0 and "mix0" in self.stages:
                self.mixer0(NC, t0, prompt, first, last)
            if l == 1 and "mix1" in self.stages:
                self.mixer1(NC, prompt, first, last)
            if "ffn" in self.stages:
                self.norm_fm(NC, 3 * l + 2)
                self.ffn(NC, 2 * l + 1, subt)
        S.dma("sp", out=yout[t0:t0 + T, :].rearrange("(c j) d -> c j d", j=8), in_=X8[:NC, :, :],
              reads=[self.rX8], writes=[], sem="yout")

    def norm_stats(self, NC):
        S = self.S
        X8, XS, SS, RS = self.X8, self.XS, self.SS, self.RS
        S.op("dve", lambda e: e.memset(SS[:NC, 0:8], 0.0), writes=[self.rSS])
        for j in range(8):
            S.op("act", lambda e, j=j: e.activation(out=XS[:NC, j, :], in_=X8[:NC, j, :], func=AF.Square,
                                                      accum_out=SS[:NC, j:j + 1]),
                 reads=[self.rX8, self.rSS], writes=[self.rXS, self.rSS])
        S.op("dve", lambda e: e.tensor_scalar(out=RS[:NC, 0:8], in0=SS[:NC, 0:8], scalar1=1.0 / D, scalar2=EPS,
                                              op0=ALU.mult, op1=ALU.add), reads=[self.rSS], writes=[self.rRS])
        S.op("act", lambda e: e.activation(out=RS[:NC, 0:8], in_=RS[:NC, 0:8], func=AF.Sqrt),
             reads=[self.rRS], writes=[self.rRS])
        S.op("dve", lambda e: e.reciprocal(out=RS[:NC, 0:8], in_=RS[:NC, 0:8]), reads=[self.rRS], writes=[self.rRS])
        for j in range(8):
            S.op("pool", lambda e, j=j: e.tensor_scalar(out=XS[:NC, j, :], in0=X8[:NC, j, :],
                                                         scalar1=RS[:NC, j:j + 1], scalar2=None, op0=ALU.mult),
                 reads=[self.rX8, self.rRS], writes=[self.rXS.sub(j * 2048, (j + 1) * 2048)])

    def norm_fm(self, NC, gidx):
        S = self.S
        self.norm_stats(NC)
        T = 8 * NC
        for dk in range(8):
            for jh in range(2):
                pi = self.next_ps()
                pt = self.PS[pi][:, 0:256].bitcast(BF16).rearrange("p (j c) -> p j c", j=4)
                for jl in range(4):
                    j = 4 * jh + jl
                    S.op("pe", lambda e, jl=jl, j=j: e.transpose(out=pt[:, jl, :NC], in_=self.XS[:NC, j, dk * 128:(dk + 1) * 128],
                                                                 identity=self.IDB[:NC, :NC]),
                         reads=[self.rXS.sub(j * 2048, (j + 1) * 2048), self.rIDB], writes=[self.rPS[pi]], sig=(jl == 3))
                dst = self.XNT[:, dk, 0:T].rearrange("p (c j) -> p j c", j=8)[:, 4 * jh:4 * jh + 4, :]
                g = self.GAM[:, gidx, dk:dk + 1]
                wr = [self.rXNT.sub(dk * 2048, dk * 2048 + T * 2)]
                if (dk + jh) % 2 == 0:
                    S.op("act", lambda e: e.activation(out=dst, in_=pt[:, :, :NC], func=AF.Copy, scale=g),
                         reads=[self.rPS[pi], self.rGAM], writes=wr)
                else:
                    S.op("dve", lambda e: e.tensor_scalar(out=dst, in0=pt[:, :, :NC], scalar1=g, scalar2=None, op0=ALU.mult),
                         reads=[self.rPS[pi], self.rGAM], writes=wr)

    def ffn(self, NC, widx, subt):
        S = self.S
        T = 8 * NC
        for f0 in range(0, NF, 11):
            S.dma("sp", out=self.WOUT[:, f0:f0 + 11, :],
                  in_=self.s_ffn_out[widx][:, f0 * 1024:(f0 + 11) * 1024].rearrange("p (f n) -> p f n", f=11),
                  reads=[self.R_scr], writes=[self.rWOUT.sub(f0 * 2048, (f0 + 11) * 2048)], sem="wout%d" % (f0 // 11))
        for (j0, nj) in subt:
            N = NC * nj
            rhs = [self.XNT[:, dk, 0:T].rearrange("p (c j) -> p c j", j=8)[:, :, j0:j0 + nj] for dk in range(8)]
            rd_x = [self.rXNT]
            for f in range(NF):
                sl = self.win_rr
                self.win_rr = (self.win_rr + 1) % 2
                S.dma("sp", out=self.WIN[sl][:, :, :],
                      in_=self.s_ffn_in[widx][f * 128:(f + 1) * 128, :].rearrange("p (k n) -> p k n", k=8),
                      reads=[self.R_scr], writes=[self.rWIN[sl]], sem="win%d" % sl)
                pg, pu = self.next_ps(), self.next_ps()
                for half, pi in ((0, pg), (1, pu)):
                    out = self.PS[pi][:, 0:N].rearrange("p (c j) -> p c j", j=nj)
                    for dk in range(8):
                        S.op("pe", lambda e, dk=dk, out=out, half=half: e.matmul(
                            out, lhsT=self.WIN[sl][:, dk, half * 128:(half + 1) * 128], rhs=rhs[dk],
                            start=(dk == 0), stop=(dk == 7)),
                            reads=[self.rWIN[sl]] + rd_x, writes=[self.rPS[pi]], sig=(dk == 7))
                sg = self.sg_rr
                self.sg_rr = (self.sg_rr + 1) % 2
                S.op("act", lambda e: e.activation(out=self.SG[sg][:, 0:N], in_=self.PS[pg][:, 0:N], func=AF.Silu),
                     reads=[self.rPS[pg]], writes=[self.rSG[sg]])
                S.op("dve", lambda e: e.tensor_tensor(out=self.H[:, f, 0:N], in0=self.SG[sg][:, 0:N],
                                                      in1=self.PS[pu][:, 0:N], op=ALU.mult),
                     reads=[self.rSG[sg], self.rPS[pu]], writes=[self.rH.sub(f * 1024, (f + 1) * 1024)])
            for jl in range(nj):
                for dh in range(2):
                    pi = self.next_ps()
                    for f in range(NF):
                        lhsT = self.H[:, f, 0:N].rearrange("p (c j) -> p c j", j=nj)[:, :, jl]
                        S.op("pe", lambda e, f=f, lhsT=lhsT: e.matmul(
                            self.PS[pi][:NC, :], lhsT=lhsT, rhs=self.WOUT[:, f, dh * 512:(dh + 1) * 512],
                            start=(f == 0), stop=(f == NF - 1)),
                            reads=[self.rH.sub(f * 1024, (f + 1) * 1024), self.rWOUT.sub(f * 2048, (f + 1) * 2048)],
                            writes=[self.rPS[pi]], sig=(f == NF - 1))
                    j = j0 + jl
                    xv = self.X8[:NC, j, dh * 512:(dh + 1) * 512]
                    S.op("dve", lambda e, xv=xv: e.scalar_tensor_tensor(out=xv, in0=self.PS[pi][:NC, :], scalar=0.5, in1=xv,
                                                                        op0=ALU.mult, op1=ALU.add),
                         reads=[self.rPS[pi], self.rX8], writes=[self.rX8])

    def setup_mix0(self):
        A, S, nc = self.A, self.S, self.nc
        z = self.zone
        ZK = 66 * 1024
        self.rKF = A.alloc(4 * 1536 * 2, at=z + ZK)
        self.rVA = A.alloc(12 * 520 * 2, at=self.rKF.hi)
        o = self.rVA.hi
        self.rMIXT = A.alloc(8 * 1024 * 2, at=o); o = self.rMIXT.hi
        self.rPT = [A.alloc(8 * 512 * 2, at=o), A.alloc(8 * 512 * 2, at=o + 8192)]; o += 16384
        self.rYB = A.alloc(8 * 512 * 2, at=o); o = self.rYB.hi
        self.mix_end = o
        o = z
        self.rABO = A.alloc(8 * 1024 * 2, at=o); o = self.rABO.hi
        self.rQF = A.alloc(4 * 1024 * 2, at=o); o = self.rQF.hi
        self.rXA = A.alloc(1028 * 4, at=o); o = self.rXA.hi
        self.rGA = A.alloc(1024 * 4, at=o); o = self.rGA.hi
        self.rXC = A.alloc(1024 * 4, at=o); o = self.rXC.hi
        self.rXCB = A.alloc(1024 * 2, at=o); o = self.rXCB.hi
        self.rT = []
        for _ in range(4):
            self.rT.append(A.alloc(1024 * 4, at=o)); o = self.rT[-1].hi
        self.rWV = A.alloc(8 * 512 * 2, at=o); o = self.rWV.hi
        assert o <= z + ZK, (o - z)
        self.KF = A.ap(self.rKF, BF16, "p (h t) -> p h t", h=4)
        self.VA = A.ap(self.rVA, BF16, "p (m h e) -> p m h e", n=12 * 520, m=12, h=8)
        self.MIXT = A.ap(self.rMIXT, BF16, "p (k t) -> p k t", k=8)
        self.PT = [A.ap(r, BF16, "p (m q) -> p m q", m=8) for r in self.rPT]
        self.YB = A.ap(self.rYB, BF16, "p (i c) -> p i c", i=8)
        self.ABO = A.ap(self.rABO, BF16, "p (k n) -> p k n", k=8)
        self.QF = A.ap(self.rQF, BF16, "p (h t) -> p h t", h=4)
        self.XA = A.ap(self.rXA, F32, n=1028)
        self.GA = A.ap(self.rGA, F32)
        self.XC = A.ap(self.rXC, F32)
        self.XCB = A.ap(self.rXCB, BF16)
        self.T = [A.ap(r, F32) for r in self.rT]
        self.WV = A.ap(self.rWV, BF16, "p (k n) -> p k n", k=8)

    def alloc_mix0_consts(self):
        A = self.A
        self.rM0 = A.alloc(self.NM0 * 4)
        self.M0 = A.ap(self.rM0, F32, n=self.NM0)
        self.rBD = A.alloc(2 * 4 * 128 * 2)
        self.BD = A.ap(self.rBD, BF16, "p (w c n) -> p w c n", w=2, c=4)
        self.rBB = A.alloc(8 * 5 * 64 * 2)
        self.BB = A.ap(self.rBB, BF16, "p (h j q) -> p h j q", h=8, j=5)
        self.rBON = A.alloc(128 * 2)
        self.BON = A.ap(self.rBON, BF16, n=128)
        self.rST = A.alloc(PAGE)
        self.ST = A.ap(self.rST, F32, n=64)

    def setup_mix0_consts(self):
        A, S = self.A, self.S
        NM0 = self.NM0
        S.dma("sp", out=self.M0[:, :], in_=self.m0c[:, :], reads=[], writes=[self.rM0], sem="const2")
        M0 = self.M0
        o = self.m0off
        stg = A.ap(self.rT[0].sub(0, 4096), F32, "p (w c n) -> p w c n", w=2, c=4)
        S.dma("sp", out=stg, in_=self.wbd[:, :].rearrange("p (w c n) -> p w c n", w=2, c=4), reads=[], writes=[self.rT[0]], sem="const3")
        S.op("dve", lambda e: e.tensor_copy(out=self.BD[:, :, :, :], in_=stg), reads=[self.rT[0]], writes=[self.rBD])
        rstg2 = Reg("sb", self.rT[1].lo, self.rT[3].hi)
        stg2 = A.ap(rstg2, F32, "p (h j q) -> p h j q", n=2560, h=8, j=5)
        S.dma("sp", out=stg2, in_=self.bblk[:, :].rearrange("p (h j q) -> p h j q", h=8, j=5), reads=[], writes=[rstg2], sem="const4")
        for h in range(8):
            S.op("dve", lambda e, h=h: e.tensor_scalar(out=self.BB[:, h, :, :], in0=stg2[:, h, :, :],
                                                        scalar1=M0[:, o["bc"] + h:o["bc"] + h + 1], scalar2=None, op0=ALU.subtract),
                 reads=[rstg2, self.rM0], writes=[self.rBB])
        S.op("act", lambda e: e.activation(out=self.BB[:, :, :, :], in_=self.BB[:, :, :, :], func=AF.Exp), reads=[self.rBB], writes=[self.rBB])
        S.op("pool", lambda e: e.memset(self.BON[:, :], 0.0), writes=[self.rBON])
        S.op("pool", lambda e: e.memset(self.BON[0:64, 0:64], 1.0), writes=[self.rBON])
        S.op("pool", lambda e: e.memset(self.BON[64:128, 64:128], 1.0), writes=[self.rBON])
        c1 = M0[:, o["c1"]:o["c1"] + 4]
        S.op("act", lambda e: e.activation(out=c1, in_=M0[:, o["lam"]:o["lam"] + 4], func=AF.Exp, scale=-1.0),
             reads=[self.rM0], writes=[self.rM0])
        S.op("act", lambda e: e.activation(out=c1, in_=c1, func=AF.Ln, bias=1.0), reads=[self.rM0], writes=[self.rM0])
        S.op("dve", lambda e: e.tensor_scalar(out=c1, in0=c1, scalar1=-4.0, scalar2=None, op0=ALU.mult),
             reads=[self.rM0], writes=[self.rM0])
        for nm in ("ba", "bx"):
            v = M0[:, o[nm]:o[nm] + 4]
            S.op("dve", lambda e, v=v: e.tensor_scalar(out=v, in0=v, scalar1=0.5, scalar2=None, op0=ALU.mult),
                 reads=[self.rM0], writes=[self.rM0])
        v = M0[:, o["gq"]:o["gq"] + 1]
        S.op("dve", lambda e: e.tensor_scalar(out=v, in0=v, scalar1=0.125, scalar2=None, op0=ALU.mult),
             reads=[self.rM0], writes=[self.rM0])
        S.op("pool", lambda e: e.memset(self.VA[:, :, :, 64:65], 1.0), writes=[self.rVA])

    def inproj_fm(self, NC, oc, consume):
        S = self.S
        T = 8 * NC
        sl = self.win_rr
        self.win_rr = (self.win_rr + 1) % 2
        w = self.WIN[sl][:, :, 0:128]
        S.dma("sp", out=w, in_=self.s_abin[oc * 128:(oc + 1) * 128, :].rearrange("p (k n) -> p k n", k=8),
              reads=[self.R_scr], writes=[self.rWIN[sl]], sem="win%d" % sl)
        for c0 in range(0, T, 512):
            n = min(512, T - c0)
            pi = self.next_ps()
            for dk in range(8):
                S.op("pe", lambda e, dk=dk: e.matmul(self.PS[pi][:, 0:n], lhsT=w[:, dk, :], rhs=self.XNT[:, dk, c0:c0 + n],
                                                     start=(dk == 0), stop=(dk == 7)),
                     reads=[self.rWIN[sl], self.rXNT], writes=[self.rPS[pi]], sig=(dk == 7))
            consume(pi, c0, n)

    def mixer0(self, NC, t0, prompt, first, last):
        S, A = self.S, self.A
        T = 8 * NC
        M0, o = self.M0, self.m0off
        self.norm_fm(NC, 1)
        ST = self.ST
        S.dma("sp", out=self.ABO[:, :, :], in_=self.s_about[:, :].rearrange("p (k n) -> p k n", k=8),
              reads=[self.R_scr], writes=[self.rABO], sem="abo")
        S.dma("sp", out=self.WV[:, :, :], in_=self.s_abv[:, :].rearrange("p (k n) -> p k n", k=8),
              reads=[self.R_scr], writes=[self.rWV], sem="wv")
        if first:
            if prompt:
                S.op("dve", lambda e: e.memset(ST[:, 0:16], 0.0), writes=[self.rST])
            else:
                S.dma("sp", out=ST[:, 0:16], in_=self.st_rg[:, :], reads=[], writes=[self.rST], sem="const5")
        if os.environ.get("K_MARK"): print("MARK rg", S.nops)
        XA, GA, XC, XCB, Tt = self.XA, self.GA, self.XC, self.XCB, self.T
        rT = self.rT
        for c in range(4):
            if "rg" in _SKIP:
                S.op("dve", lambda e: e.memset(self.MIXT[:, c, 0:T], 0.0), writes=[self.rMIXT.sub(c * 2048, (c + 1) * 2048)])
                continue
            S.op("dve", lambda e: e.tensor_copy(out=XA[:, 0:3], in_=ST[:, 4 + 3 * c:7 + 3 * c]), reads=[self.rST], writes=[self.rXA])
            self.inproj_fm(NC, c, lambda pi, c0, n: S.op(
                "act", lambda e: e.activation(out=XA[:, 3 + c0:3 + c0 + n], in_=self.PS[pi][:, 0:n], func=AF.Copy),
                reads=[self.rPS[pi]], writes=[self.rXA]))
            self.inproj_fm(NC, 4 + c, lambda pi, c0, n: S.op(
                "act", lambda e: e.activation(out=GA[:, c0:c0 + n], in_=self.PS[pi][:, 0:n], func=AF.Copy),
                reads=[self.rPS[pi]], writes=[self.rGA]))
            S.op("dve", lambda e: e.tensor_copy(out=ST[:, 4 + 3 * c:7 + 3 * c], in_=XA[:, T:T + 3]), reads=[self.rXA], writes=[self.rST])
            cw = lambda k: M0[:, o["cw"] + 4 * c + k:o["cw"] + 4 * c + k + 1]
            S.op("dve", lambda e: e.tensor_scalar(out=XC[:, 0:T], in0=XA[:, 0:T], scalar1=cw(0),
                                                   scalar2=M0[:, o["cb"] + c:o["cb"] + c + 1], op0=ALU.mult, op1=ALU.add),
                 reads=[self.rXA, self.rM0], writes=[self.rXC])
            for k in range(1, 4):
                S.op("dve", lambda e, k=k: e.scalar_tensor_tensor(out=XC[:, 0:T], in0=XA[:, k:k + T], scalar=cw(k), in1=XC[:, 0:T],
                                                                   op0=ALU.mult, op1=ALU.add),
                     reads=[self.rXA, self.rXC, self.rM0], writes=[self.rXC])
            S.op("pool", lambda e: e.tensor_copy(out=XCB[:, 0:T], in_=XC[:, 0:T]), reads=[self.rXC], writes=[self.rXCB])
            for c0 in range(0, T, 512):
                n = min(512, T - c0)
                for w, dst, bn in ((0, Tt[0], "ba"), (1, Tt[1], "bx")):
                    pi = self.next_ps()
                    S.op("pe", lambda e, w=w: e.matmul(self.PS[pi][:, 0:n], lhsT=self.BD[:, w, c, :], rhs=XCB[:, c0:c0 + n],
                                                       start=True, stop=True),
                         reads=[self.rBD, self.rXCB], writes=[self.rPS[pi]])
                    S.op("act", lambda e, dst=dst, bn=bn: e.activation(out=dst[:, c0:c0 + n], in_=self.PS[pi][:, 0:n], func=AF.Tanh,
                                                                       scale=0.5, bias=M0[:, o[bn] + c:o[bn] + c + 1]),
                         reads=[self.rPS[pi], self.rM0], writes=[rT[w]])
            c1 = M0[:, o["c1"] + c:o["c1"] + c + 1]
            S.op("act", lambda e: e.activation(out=Tt[2][:, 0:T], in_=Tt[0][:, 0:T], func=AF.Exp, scale=c1, bias=c1),
                 reads=[rT[0], self.rM0], writes=[rT[2]])
            S.op("act", lambda e: e.activation(out=Tt[3][:, 0:T], in_=Tt[0][:, 0:T], func=AF.Tanh, scale=c1, bias=c1),
                 reads=[rT[0], self.rM0], writes=[rT[3]])
            if c == 0 and first and prompt:
                self.dbg("tr", Tt[0][:, 0:T], rT[0], [128, T])
                self.dbg("a", Tt[2][:, 0:T], rT[2], [128, T])
                self.dbg("th", Tt[3][:, 0:T], rT[3], [128, T])
                self.dbg("m0", M0[:, :], self.rM0, [128, self.NM0])
                self.dbg("xc", XC[:, 0:T], self.rXC, [128, T])
            S.op("dve", lambda e: e.tensor_tensor(out=Tt[0][:, 0:T], in0=Tt[2][:, 0:T], in1=Tt[2][:, 0:T], op=ALU.mult),
                 reads=[rT[2]], writes=[rT[0]])
            S.op("dve", lambda e: e.scalar_tensor_tensor(out=Tt[0][:, 0:T], in0=Tt[0][:, 0:T], scalar=1.0, in1=Tt[3][:, 0:T],
                                                         op0=ALU.add, op1=ALU.mult), reads=[rT[0], rT[3]], writes=[rT[0]])
            S.op("dve", lambda e: e.tensor_scalar(out=Tt[0][:, 0:T], in0=Tt[0][:, 0:T], scalar1=-1.0, scalar2=0.0, op0=ALU.mult, op1=ALU.max),
                 reads=[rT[0]], writes=[rT[0]])
            S.op("act", lambda e: e.activation(out=Tt[0][:, 0:T], in_=Tt[0][:, 0:T], func=AF.Sqrt),
                 reads=[rT[0]], writes=[rT[0]])
            S.op("dve", lambda e: e.scalar_tensor_tensor(out=Tt[1][:, 0:T], in0=Tt[1][:, 0:T], scalar=1.0, in1=XC[:, 0:T],
                                                         op0=ALU.add, op1=ALU.mult), reads=[rT[1], self.rXC], writes=[rT[1]])
            S.op("dve", lambda e: e.scalar_tensor_tensor(out=Tt[1][:, 0:T], in0=Tt[1][:, 0:T], scalar=0.5, in1=Tt[0][:, 0:T],
                                                         op0=ALU.mult, op1=ALU.mult), reads=[rT[1], rT[0]], writes=[rT[1]])
            S.op("dve", lambda e: e.tensor_tensor_scan(out=Tt[3][:, 0:T], data0=Tt[2][:, 0:T], data1=Tt[1][:, 0:T],
                                                       initial=ST[:, c:c + 1], op0=ALU.mult, op1=ALU.add),
                 reads=[rT[2], rT[1], self.rST], writes=[rT[3]])
            S.op("dve", lambda e: e.tensor_copy(out=ST[:, c:c + 1], in_=Tt[3][:, T - 1:T]), reads=[rT[3]], writes=[self.rST])
            S.op("act", lambda e: e.activation(out=Tt[0][:, 0:T], in_=GA[:, 0:T], func=AF.Square), reads=[self.rGA], writes=[rT[0]])
            S.op("dve", lambda e: e.tensor_scalar(out=Tt[0][:, 0:T], in0=Tt[0][:, 0:T], scalar1=0.044715, scalar2=1.0,
                                                  op0=ALU.mult, op1=ALU.add), reads=[rT[0]], writes=[rT[0]])
            S.op("dve", lambda e: e.tensor_tensor(out=Tt[0][:, 0:T], in0=Tt[0][:, 0:T], in1=GA[:, 0:T], op=ALU.mult),
                 reads=[rT[0], self.rGA], writes=[rT[0]])
            S.op("act", lambda e: e.activation(out=Tt[0][:, 0:T], in_=Tt[0][:, 0:T], func=AF.Tanh, scale=0.7978845608028654),
                 reads=[rT[0]], writes=[rT[0]])
            S.op("dve", lambda e: e.scalar_tensor_tensor(out=Tt[0][:, 0:T], in0=Tt[0][:, 0:T], scalar=1.0, in1=GA[:, 0:T],
                                                         op0=ALU.add, op1=ALU.mult), reads=[rT[0], self.rGA], writes=[rT[0]])
            S.op("dve", lambda e: e.scalar_tensor_tensor(out=self.MIXT[:, c, 0:T], in0=Tt[0][:, 0:T], scalar=0.5, in1=Tt[3][:, 0:T],
                                                         op0=ALU.mult, op1=ALU.mult),
                 reads=[rT[0], rT[3]], writes=[self.rMIXT.sub(c * 2048, (c + 1) * 2048)])
        if last and "stout" not in _SKIP:
            oc_, oh_ = (self.o_pconv, self.o_ph) if prompt else (self.o_sconv, self.o_sh)
            S.dma("sp", out=oh_.rearrange("(c p) -> p c", p=128), in_=ST[:, 0:4], reads=[self.rST], writes=[], sem="o_st", slow=True)
            for c in range(4):
                S.dma("sp", out=oc_[:, c * 128:(c + 1) * 128].rearrange("k p -> p k"), in_=ST[:, 4 + 3 * c:7 + 3 * c],
                      reads=[self.rST], writes=[], sem="o_st", slow=True)
        if os.environ.get("K_MARK"): print("MARK qk", S.nops)
        KF, QF, VA = self.KF, self.QF, self.VA
        if first and not prompt and "cache" not in _SKIP:
            self.load_cache()
        want_kv = last and "kvout" not in _SKIP
        kv_lo = max(0, T - 512)
        for which in (0, 1):
            for hp in range(4):
                oc = 8 + 4 * which + hp
                gcol = o["gq"] if which == 0 else o["gk"]

                def consume(pi, c0, n, which=which, hp=hp, gcol=gcol):
                    S.op("act", lambda e: e.activation(out=self.SG[0][:, 0:n].bitcast(BF16)[:, 0:n], in_=self.PS[pi][:, 0:n], func=AF.Square),
                         reads=[self.rPS[pi]], writes=[self.rSG[0]])
                    p2 = self.next_ps()
                    S.op("pe", lambda e: e.matmul(self.PS[p2][:, 0:n], lhsT=self.BON[:, :], rhs=self.SG[0][:, 0:n].bitcast(BF16)[:, 0:n],
                                                  start=True, stop=True), reads=[self.rBON, self.rSG[0]], writes=[self.rPS[p2]])
                    if self.debug:
                        S.op("dve", lambda e: e.tensor_copy(out=self.T[3][:, 0:n], in_=self.PS[p2][:, 0:n]), reads=[self.rPS[p2]], writes=[self.rT[3]])
                        S.op("dve", lambda e: e.tensor_copy(out=self.T[2][:, 0:n], in_=self.PS[pi][:, 0:n]), reads=[self.rPS[pi]], writes=[self.rT[2]])
                        if which == 0 and hp == 0:
                            self.dbg("ss", self.T[3][:, 0:n], self.rT[3], [128, n])
                            self.dbg("q", self.T[2][:, 0:n], self.rT[2], [128, n])
                            self.dbg("sq", self.SG[0][:, 0:n], self.rSG[0], [128, n])
                        S.op("dve", lambda e: e.tensor_scalar(out=self.SG[1][:, 0:n], in0=self.PS[p2][:, 0:n], scalar1=0.0, scalar2=None, op0=ALU.max),
                             reads=[self.rPS[p2]], writes=[self.rSG[1]])
                        S.op("act", lambda e: e.activation(out=self.SG[1][:, 0:n], in_=self.SG[1][:, 0:n], func=AF.Sqrt, scale=1.0 / 64, bias=EPS),
                             reads=[self.rSG[1]], writes=[self.rSG[1]])
                    else:
                      S.op("act", lambda e: e.activation(out=self.SG[1][:, 0:n], in_=self.PS[p2][:, 0:n], func=AF.Sqrt, scale=1.0 / 64, bias=EPS),
                         reads=[self.rPS[p2]], writes=[self.rSG[1]])
                    S.op("dve", lambda e: e.reciprocal(out=self.SG[1][:, 0:n], in_=self.SG[1][:, 0:n]), reads=[self.rSG[1]], writes=[self.rSG[1]])
                    if which == 0:
                        S.op("dve", lambda e: e.scalar_tensor_tensor(out=QF[:, hp, c0:c0 + n], in0=self.PS[pi][:, 0:n],
                                                                     scalar=M0[:, gcol:gcol + 1], in1=self.SG[1][:, 0:n], op0=ALU.mult, op1=ALU.mult),
                             reads=[self.rPS[pi], self.rSG[1], self.rM0], writes=[self.rQF])
                    else:
                        S.op("dve", lambda e: e.scalar_tensor_tensor(out=self.SG[1][:, 0:n], in0=self.PS[pi][:, 0:n],
                                                                     scalar=M0[:, gcol:gcol + 1], in1=self.SG[1][:, 0:n], op0=ALU.mult, op1=ALU.mult),
                             reads=[self.rPS[pi], self.rSG[1], self.rM0], writes=[self.rSG[1]])
                        S.op("act", lambda e: e.activation(out=KF[:, hp, 512 + c0:512 + c0 + n], in_=self.SG[1][:, 0:n], func=AF.Copy),
                             reads=[self.rSG[1]], writes=[self.rKF])
                        if want_kv and c0 + n > kv_lo and "kout" not in _SKIP:
                            self.emit_k_out(prompt, hp, c0, n, kv_lo)
                self.inproj_fm(NC, oc, consume)
        if os.environ.get("K_MARK"): print("MARK v", S.nops)
        for tb in range((T + 127) // 128):
            rows = min(128, T - tb * 128)
            pi = self.next_ps()
            for dk in range(8):
                S.op("pe", lambda e, dk=dk: e.matmul(self.PS[pi][:rows, :], lhsT=self.XNT[:, dk, tb * 128:tb * 128 + rows], rhs=self.WV[:, dk, :],
                                                     start=(dk == 0), stop=(dk == 7)),
                     reads=[self.rXNT, self.rWV], writes=[self.rPS[pi]], sig=(dk == 7))
            S.op("act", lambda e: e.activation(out=VA[:rows, 4 + tb, :, 0:64], in_=self.PS[pi][:rows, :].rearrange("p (h e) -> p h e", h=8), func=AF.Copy),
                 reads=[self.rPS[pi]], writes=[self.rVA])
            if want_kv and tb * 128 + rows > kv_lo and "vout" not in _SKIP:
                veng = "act"
                if veng == "act":
                    S.op("act", lambda e: e.activation(out=self.SG[0][:rows, :], in_=self.PS[pi][:rows, :], func=AF.Copy), reads=[self.rPS[pi]], writes=[self.rSG[0]])
                else:
                    S.op("dve", lambda e: e.tensor_copy(out=self.SG[0][:rows, :], in_=self.PS[pi][:rows, :]), reads=[self.rPS[pi]], writes=[self.rSG[0]])
                ov = self.o_pv if prompt else self.o_sv
                r0 = tb * 128 - kv_lo
                S.dma("sp", out=ov[r0:r0 + rows, :], in_=self.SG[0][:rows, :], reads=[self.rSG[0]], writes=[], sem="o_sg0")
        if os.environ.get("K_MARK"): print("MARK attn", S.nops)
        nq = T // 64
        KCn = 8 + nq
        kmin = 8 if (prompt and first) else 0
        bc = o["bc"]
        for qs in range(0, nq, 8):
            if "attn" in _SKIP:
                for cc in range(4):
                    S.op("dve", lambda e, cc=cc: e.memset(self.MIXT[:, 4 + cc, 0:T], 0.0), writes=[self.rMIXT.sub((4 + cc) * 2048, (5 + cc) * 2048)])
                break
            qe = min(nq, qs + 8)
            blocks = [m for m in range(qs // 2, (qe + 8 + 1) // 2) if 2 * m >= kmin and 2 * m < KCn]
            for h in range(8):
                hp, base = h // 2, 64 * (h % 2)
                ptb = h % 2
                PTh = self.PT[ptb]
                binfo = {}
                for m in blocks:
                    rows = 128 if 2 * m + 1 < KCn else 64
                    i0, i1 = max(qs, 2 * m - 8), min(qe - 1, 2 * m + 1)
                    n = (i1 - i0 + 1) * 64
                    ml = m - blocks[0]
                    binfo[m] = (rows, i0, ml)
                    pi = self.next_ps()
                    ja, jb = max(0, i0 - (2 * m - 8)), min(4, i1 - (2 * m - 8))
                    hasb = jb >= ja and "nobias" not in _SKIP
                    S.op("pe", lambda e: e.matmul(self.PS[pi][:rows, 0:n], lhsT=KF[base:base + 64, hp, m * 128:m * 128 + rows],
                                                  rhs=QF[base:base + 64, hp, i0 * 64:(i1 + 1) * 64], start=True, stop=True),
                         reads=[self.rKF, self.rQF], writes=[self.rPS[pi]])
                    S.op("act", lambda e: e.activation(out=PTh[:rows, ml, 0:n], in_=self.PS[pi][:rows, 0:n], func=AF.Exp,
                                                       bias=M0[:rows, bc + h:bc + h + 1]),
                         reads=[self.rPS[pi], self.rM0], writes=[self.rPT[ptb].sub(ml * 1024, (ml + 1) * 1024)])
                    if hasb:
                        ia = ja + 2 * m - 8
                        nb = (jb - ja + 1) * 64
                        pv_ = PTh[:rows, ml, (ia - i0) * 64:(ia - i0) * 64 + nb].rearrange("p (j q) -> p j q", q=64)
                        S.op("dve", lambda e: e.tensor_tensor(out=pv_, in0=pv_, in1=self.BB[:rows, h, ja:jb + 1, :], op=ALU.mult),
                             reads=[self.rBB, self.rPT[ptb].sub(ml * 1024, (ml + 1) * 1024)], writes=[self.rPT[ptb].sub(ml * 1024, (ml + 1) * 1024)])
                for i in range(qs, qe):
                    if i % 2 == 1 and (i - 1) // 2 in binfo:
                        rows_, i0_, ml_ = binfo[(i - 1) // 2]
                        if rows_ == 128:
                            zc = PTh[0:64, ml_, (i - i0_) * 64:(i - i0_ + 1) * 64]
                            S.op("dve", lambda e, zc=zc: e.memset(zc, 0.0), writes=[self.rPT[ptb].sub(ml_ * 1024, (ml_ + 1) * 1024)])
                for g0 in range(qs, qe, 4):
                    if "pv" in _SKIP:
                        S.op("dve", lambda e: e.memset(self.YB[:64, :, h * 64:(h + 1) * 64], 0.0), writes=[self.rYB])
                        break
                    g1 = min(qe, g0 + 4)
                    pi = self.next_ps()
                    pv = self.PS[pi][:64, 0:260].rearrange("p (i e) -> p i e", e=65)
                    for i in range(g0, g1):
                        ms = [m for m in range(i // 2, (i + 8) // 2 + 1) if m in binfo]
                        parts = []
                        for m in ms:
                            rows, i0, ml = binfo[m]
                            r0 = 0 if (i <= 2 * m <= i + 8) else 64
                            r1 = 128 if (i <= 2 * m + 1 <= i + 8 and rows == 128) else 64
                            if r1 > r0:
                                if r0 == 64:
                                    r0 = 0
                                parts.append((m, r0, r1, i0, ml))
                        for pi_, (m, r0, r1, i0, ml) in enumerate(parts):
                            S.op("pe", lambda e, m=m, r0=r0, r1=r1, i0=i0, ml=ml, pi_=pi_: e.matmul(
                                pv[:, i - g0, :], lhsT=PTh[r0:r1, ml, (i - i0) * 64:(i - i0 + 1) * 64], rhs=VA[r0:r1, m, h, :],
                                start=(pi_ == 0), stop=(pi_ == len(parts) - 1)),
                                reads=[self.rPT[ptb].sub(ml * 1024, (ml + 1) * 1024), self.rVA], writes=[self.rPS[pi]],
                                sig=(pi_ == len(parts) - 1 and i == g1 - 1))
                    ng = g1 - g0
                    rd = self.RS[:64, 8:8 + ng]
                    S.op("dve", lambda e: e.reciprocal(out=rd, in_=pv[:, 0:ng, 64]), reads=[self.rPS[pi]], writes=[self.rRS])
                    S.op("dve", lambda e: e.tensor_tensor(out=self.YB[:64, g0 - qs:g1 - qs, h * 64:(h + 1) * 64], in0=pv[:, 0:ng, 0:64],
                                                          in1=rd.unsqueeze(2).to_broadcast([64, ng, 64]), op=ALU.mult),
                         reads=[self.rPS[pi], self.rRS], writes=[self.rYB])
            nqh = qe - qs
            for cc in range(4):
                pi = self.next_ps()
                pt = self.PS[pi][:, 0:256].bitcast(BF16).rearrange("p (i q) -> p i q", i=8)
                for i in range(nqh):
                    S.op("pe", lambda e, i=i: e.transpose(out=pt[:, i, :], in_=self.YB[:64, i, cc * 128:(cc + 1) * 128], identity=self.IDB[:64, :64]),
                         reads=[self.rYB, self.rIDB], writes=[self.rPS[pi]], sig=(i == nqh - 1))
                S.op("act", lambda e: e.activation(out=self.MIXT[:, 4 + cc, qs * 64:qe * 64].rearrange("p (i q) -> p i q", q=64),
                                                   in_=pt[:, 0:nqh, :], func=AF.Copy),
                     reads=[self.rPS[pi]], writes=[self.rMIXT.sub((4 + cc) * 2048, (5 + cc) * 2048)])
        if not last:
            S.op("pool", lambda e: e.tensor_copy(out=KF[:, :, 0:512], in_=KF[:, :, T:T + 512]), reads=[self.rKF], writes=[self.rKF])
            S.op("pool", lambda e: e.tensor_copy(out=VA[:, 0:4, :, 0:64], in_=VA[:, T // 128:T // 128 + 4, :, 0:64]), reads=[self.rVA], writes=[self.rVA])
        if os.environ.get("K_MARK"): print("MARK outproj", S.nops)
        for j in range(8):
            for dh in range(2):
                pi = self.next_ps()
                for kc in range(8):
                    lhsT = self.MIXT[:, kc, 0:T].rearrange("p (c j) -> p c j", j=8)[:, :, j]
                    S.op("pe", lambda e, kc=kc, lhsT=lhsT: e.matmul(self.PS[pi][:NC, :], lhsT=lhsT, rhs=self.ABO[:, kc, dh * 512:(dh + 1) * 512],
                                                                    start=(kc == 0), stop=(kc == 7)),
                         reads=[self.rMIXT, self.rABO], writes=[self.rPS[pi]], sig=(kc == 7))
                xv = self.X8[:NC, j, dh * 512:(dh + 1) * 512]
                S.op("dve", lambda e, xv=xv: e.tensor_tensor(out=xv, in0=self.PS[pi][:NC, :], in1=xv, op=ALU.add),
                     reads=[self.rPS[pi], self.rX8], writes=[self.rX8])

    def emit_k_out(self, prompt, hp, c0, n, kv_lo):
        S = self.S
        ok = self.o_pk if prompt else self.o_sk
        for b0 in range(0, n, 128):
            nb = min(128, n - b0)
            t_lo = c0 + b0
            if t_lo + nb <= kv_lo:
                continue
            pi = self.next_ps()
            S.op("pe", lambda e: e.matmul(self.PS[pi][:nb, 0:128], lhsT=self.SG[1][:, b0:b0 + nb], rhs=self.IDF[:, :], start=True, stop=True),
                 reads=[self.rSG[1], self.rIDF], writes=[self.rPS[pi]])
            S.op("dve", lambda e: e.tensor_copy(out=self.SG[0][:nb, 0:128], in_=self.PS[pi][:nb, 0:128]), reads=[self.rPS[pi]], writes=[self.rSG[0]])
            r0 = t_lo - kv_lo
            S.dma("sp", out=ok[r0:r0 + nb, hp * 128:(hp + 1) * 128], in_=self.SG[0][:nb, 0:128], reads=[self.rSG[0]], writes=[], sem="o_sg0")

    def load_cache(self):
        S = self.S
        kb16 = self.SG[1][:, 0:256].bitcast(BF16)
        for kb in range(4):
            S.dma("sp", out=self.SG[0][:, 0:512], in_=self.ck[kb * 128:(kb + 1) * 128, :], reads=[], writes=[self.rSG[0]], sem="ck")
            S.op("dve", lambda e: e.tensor_copy(out=kb16, in_=self.SG[0][:, 0:512]), reads=[self.rSG[0]], writes=[self.rSG[1]])
            for hp in range(4):
                pi = self.next_ps()
                pt = self.PS[pi][:, 0:64].bitcast(BF16)
                S.op("pe", lambda e, hp=hp, pt=pt: e.transpose(out=pt, in_=kb16[:, hp * 128:(hp + 1) * 128], identity=self.IDB[:, :]),
                     reads=[self.rSG[1], self.rIDB], writes=[self.rPS[pi]])
                S.op("act", lambda e, hp=hp, pt=pt: e.activation(out=self.KF[:, hp, kb * 128:(kb + 1) * 128], in_=pt, func=AF.Copy),
                     reads=[self.rPS[pi]], writes=[self.rKF])
            S.dma("sp", out=self.SG[0][:, 0:512], in_=self.cv[kb * 128:(kb + 1) * 128, :], reads=[], writes=[self.rSG[0]], sem="ck")
            S.op("dve", lambda e: e.tensor_copy(out=self.VA[:, kb, :, 0:64], in_=self.SG[0][:, 0:512].rearrange("p (h e) -> p h e", h=8)),
                 reads=[self.rSG[0]], writes=[self.rVA])

    def alloc_mix1_consts(self):
        A = self.A
        self.rS5 = A.alloc(16 * 64 * 4)
        self.S5 = A.ap(self.rS5, F32, "p (s g) -> p s g", s=16)
        self.rDFM = A.alloc(PAGE)
        self.DFM = A.ap(self.rDFM, F32, n=8)
        self.rSST = A.alloc(128 * 4)
        self.SST = A.ap(self.rSST, F32, n=128)

    def setup_mix1(self):
        A, S = self.A, self.S
        z = self.zone
        self.rBU = A.alloc(2 * 64 * 128 * 2, at=z)
        self.rHI = A.alloc(2 * 64 * 128 * 2, at=z + 32768)
        o = z + 66 * 1024 + (4 * 1536 * 2) + ((12 * 520 * 2 + PAGE - 1) // PAGE * PAGE)
        self.rBP = A.alloc(64 * 2 * 64 * 2, at=o); o = self.rBP.hi
        self.rCP = A.alloc(2 * 64 * 16 * 2, at=o); o = self.rCP.hi
        self.rGW = [A.alloc(8 * 512 * 2, at=o), A.alloc(8 * 512 * 2, at=o + 8192)]; o += 16384
        self.rYT = A.alloc(1024 * 2, at=o); o = self.rYT.hi
        self.rTM = A.alloc(3 * 128 * 4, at=o); o = self.rTM.hi
        self.BU = A.ap(self.rBU, BF16, "p (r g t) -> p r g t", r=2, g=64)
        self.HI = A.ap(self.rHI, BF16, "p (r g t) -> p r g t", r=2, g=64)
        self.BP = A.ap(self.rBP, BF16, "p (g r q) -> p g r q", g=64, r=2)
        self.CP = A.ap(self.rCP, BF16, "p (r g i) -> p r g i", r=2, g=64)
        self.GW = [A.ap(r, BF16, "p (k n) -> p k n", k=8) for r in self.rGW]
        self.YT = A.ap(self.rYT, BF16, n=1024)
        self.TM = A.ap(self.rTM, F32, "p (s w) -> p s w", n=384, s=3)
        S5 = self.S5
        stg = A.ap(Reg("sb", z, z + 3 * 64 * 4), F32, "p (s g) -> p s g", n=192, s=3)
        rstg = Reg("sb", z, z + 1024)
        S.dma("sp", out=stg[0:64], in_=self.s5p[:, :].rearrange("p (s g) -> p s g", s=3), reads=[], writes=[rstg], sem="c5a")
        S.dma("sp", out=self.DFM[:, :], in_=self.dfm[:, :], reads=[], writes=[self.rDFM], sem="c5b")
        rS5 = self.rS5
        P64 = slice(0, 64)

        def dv(fn, rd=(rS5, rstg), wr=(rS5,)):
            S.op("dve", fn, reads=list(rd), writes=list(wr))

        def ac(fn, rd=(rS5, rstg), wr=(rS5,)):
            S.op("act", fn, reads=list(rd), writes=list(wr))
        are, aim, ldt = stg[P64, 0, :], stg[P64, 1, :], stg[P64, 2, :]
        sl = lambda k: S5[P64, k, :]
        ac(lambda e: e.activation(out=sl(0), in_=ldt, func=AF.Exp))
        dv(lambda e: e.tensor_tensor(out=sl(1), in0=are, in1=sl(0), op=ALU.mult))
        dv(lambda e: e.tensor_tensor(out=sl(2), in0=aim, in1=sl(0), op=ALU.mult))
        ac(lambda e: e.activation(out=sl(3), in_=sl(1), func=AF.Exp))
        TWO_PI = 2.0 * np.pi
        I32 = mybir.dt.int32

        def sin_of(dst, shift):
            dv(lambda e: e.tensor_scalar(out=sl(13), in0=sl(2), scalar1=shift, scalar2=None, op0=ALU.add))
            dv(lambda e: e.tensor_scalar(out=sl(14), in0=sl(13), scalar1=1.0 / TWO_PI, scalar2=None, op0=ALU.mult))
            dv(lambda e: e.tensor_copy(out=sl(15).bitcast(I32), in_=sl(14)))
            dv(lambda e: e.tensor_copy(out=sl(14), in_=sl(15).bitcast(I32)))
            dv(lambda e: e.scalar_tensor_tensor(out=sl(13), in0=sl(14), scalar=-TWO_PI, in1=sl(13), op0=ALU.mult, op1=ALU.add))
            dv(lambda e: e.tensor_scalar(out=sl(14), in0=sl(13), scalar1=float(np.pi), scalar2=None, op0=ALU.is_gt))
            dv(lambda e: e.scalar_tensor_tensor(out=sl(13), in0=sl(14), scalar=-TWO_PI, in1=sl(13), op0=ALU.mult, op1=ALU.add))
            dv(lambda e: e.tensor_scalar(out=sl(14), in0=sl(13), scalar1=-float(np.pi), scalar2=None, op0=ALU.is_lt))
            dv(lambda e: e.scalar_tensor_tensor(out=sl(13), in0=sl(14), scalar=TWO_PI, in1=sl(13), op0=ALU.mult, op1=ALU.add))
            ac(lambda e: e.activation(out=dst, in_=sl(13), func=AF.Sin))
        sin_of(sl(4), 0.0)
        sin_of(sl(5), float(np.pi / 2))
        dv(lambda e: e.tensor_tensor(out=sl(6), in0=sl(3), in1=sl(5), op=ALU.mult))
        dv(lambda e: e.tensor_tensor(out=sl(7), in0=sl(3), in1=sl(4), op=ALU.mult))
        dv(lambda e: e.tensor_scalar(out=sl(8), in0=sl(7), scalar1=-1.0, scalar2=None, op0=ALU.mult))
        dv(lambda e: e.tensor_tensor(out=sl(15), in0=are, in1=are, op=ALU.mult))
        dv(lambda e: e.tensor_tensor(out=sl(13), in0=aim, in1=aim, op=ALU.mult))
        dv(lambda e: e.tensor_tensor(out=sl(15), in0=sl(15), in1=sl(13), op=ALU.add))
        dv(lambda e: e.reciprocal(out=sl(15), in_=sl(15)))
        dv(lambda e: e.tensor_scalar(out=sl(14), in0=sl(6), scalar1=-1.0, scalar2=None, op0=ALU.add))
        dv(lambda e: e.tensor_tensor(out=sl(9), in0=sl(14), in1=are, op=ALU.mult))
        dv(lambda e: e.tensor_tensor(out=sl(13), in0=sl(7), in1=aim, op=ALU.mult))
        dv(lambda e: e.tensor_tensor(out=sl(9), in0=sl(9), in1=sl(13), op=ALU.add))
        dv(lambda e: e.tensor_tensor(out=sl(9), in0=sl(9), in1=sl(15), op=ALU.mult))
        dv(lambda e: e.tensor_tensor(out=sl(10), in0=sl(7), in1=are, op=ALU.mult))
        dv(lambda e: e.tensor_tensor(out=sl(13), in0=sl(14), in1=aim, op=ALU.mult))
        dv(lambda e: e.tensor_tensor(out=sl(10), in0=sl(10), in1=sl(13), op=ALU.subtract))
        dv(lambda e: e.tensor_tensor(out=sl(10), in0=sl(10), in1=sl(15), op=ALU.mult))
        dv(lambda e: e.tensor_tensor(out=sl(15), in0=sl(9), in1=sl(9), op=ALU.mult))
        dv(lambda e: e.tensor_tensor(out=sl(13), in0=sl(10), in1=sl(10), op=ALU.mult))
        dv(lambda e: e.tensor_tensor(out=sl(15), in0=sl(15), in1=sl(13), op=ALU.add))
        dv(lambda e: e.reciprocal(out=sl(15), in_=sl(15)))
        dv(lambda e: e.tensor_tensor(out=sl(11), in0=sl(9), in1=sl(15), op=ALU.mult))
        dv(lambda e: e.tensor_tensor(out=sl(12), in0=sl(10), in1=sl(15), op=ALU.mult))
        dv(lambda e: e.tensor_scalar(out=sl(12), in0=sl(12), scalar1=-1.0, scalar2=None, op0=ALU.mult))
        rc = Reg("sb", z + 4096, z + 4096 + 2 * 64 * 16 * 4)
        cst = A.ap(rc, F32, "p (r g i) -> p r g i", r=2, g=64)
        S.dma("sp", out=cst[P64], in_=self.s5c[:, :].rearrange("p (r g i) -> p r g i", r=2, g=64), reads=[], writes=[rc], sem="c5c")
        ro = Reg("sb", z + 16384, z + 16384 + 2 * 64 * 16 * 4)
        cot = A.ap(ro, F32, "p (r g i) -> p r g i", r=2, g=64)
        rt = Reg("sb", z + 28672, z + 28672 + 64 * 16 * 4)
        tmp = A.ap(rt, F32, "p (g i) -> p g i", g=64)
        bc = lambda k: S5[P64, k, :].unsqueeze(2).to_broadcast([64, 64, 16])
        S.op("dve", lambda e: e.tensor_tensor(out=cot[P64, 0], in0=cst[P64, 0], in1=bc(9), op=ALU.mult), reads=[rc, rS5], writes=[ro])
        S.op("dve", lambda e: e.tensor_tensor(out=tmp[P64], in0=cst[P64, 1], in1=bc(10), op=ALU.mult), reads=[rc, rS5], writes=[rt])
        S.op("dve", lambda e: e.tensor_tensor(out=cot[P64, 0], in0=cot[P64, 0], in1=tmp[P64], op=ALU.subtract), reads=[ro, rt], writes=[ro])
        S.op("dve", lambda e: e.tensor_tensor(out=cot[P64, 1], in0=cst[P64, 0], in1=bc(10), op=ALU.mult), reads=[rc, rS5], writes=[ro])
        S.op("dve", lambda e: e.tensor_tensor(out=tmp[P64], in0=cst[P64, 1], in1=bc(9), op=ALU.mult), reads=[rc, rS5], writes=[rt])
        S.op("dve", lambda e: e.tensor_tensor(out=cot[P64, 1], in0=cot[P64, 1], in1=tmp[P64], op=ALU.add), reads=[ro, rt], writes=[ro])
        S.op("dve", lambda e: e.tensor_scalar(out=cot[P64, 1], in0=cot[P64, 1], scalar1=-1.0, scalar2=None, op0=ALU.mult), reads=[ro], writes=[ro])
        rcb = Reg("sb", z + 36864, z + 36864 + 2 * 64 * 16 * 2)
        cob = A.ap(rcb, BF16, "p (r g i) -> p r g i", r=2, g=64)
        S.op("dve", lambda e: e.tensor_copy(out=cob[P64], in_=cot[P64]), reads=[ro], writes=[rcb])
        S.dma("sp", out=self.s_cp[:, :].rearrange("p (r g i) -> p r g i", r=2, g=64), in_=cob[P64], reads=[rcb], writes=[self.R_scr], sem="c5d")

    def mixer1(self, NC, prompt, first, last):
        S, A = self.S, self.A
        T = 8 * NC
        S5, SST, TM = self.S5, self.SST, self.TM
        P64 = slice(0, 64)
        self.norm_fm(NC, 4)
        S.dma("sp", out=self.BP[:, :, :, :], in_=self.s_bp[:, :].rearrange("p (g r q) -> p g r q", g=64, r=2), reads=[self.R_scr], writes=[self.rBP], sem="bp")
        S.dma("sp", out=self.CP[P64], in_=self.s_cp[:, :].rearrange("p (r g i) -> p r g i", r=2, g=64), reads=[self.R_scr], writes=[self.rCP], sem="cp")
        rS5, rSST, rTM = self.rS5, self.rSST, self.rTM
        sre, sim = SST[P64, 0:64], SST[P64, 64:128]
        if first:
            if prompt:
                S.op("dve", lambda e: e.memset(SST[P64, :], 0.0), writes=[rSST])
            else:
                S.dma("sp", out=TM[P64, 0, :], in_=self.s0[:, :], reads=[], writes=[rTM], sem="s0")
                a, b = TM[P64, 0, 0:64], TM[P64, 0, 64:128]
                S.op("dve", lambda e: e.tensor_tensor(out=sre, in0=a, in1=S5[P64, 11, :], op=ALU.mult), reads=[rTM, rS5], writes=[rSST])
                S.op("dve", lambda e: e.tensor_tensor(out=TM[P64, 1, 0:64], in0=b, in1=S5[P64, 12, :], op=ALU.mult), reads=[rTM, rS5], writes=[rTM])
                S.op("dve", lambda e: e.tensor_tensor(out=sre, in0=sre, in1=TM[P64, 1, 0:64], op=ALU.subtract), reads=[rTM, rSST], writes=[rSST])
                S.op("dve", lambda e: e.tensor_tensor(out=sim, in0=a, in1=S5[P64, 12, :], op=ALU.mult), reads=[rTM, rS5], writes=[rSST])
                S.op("dve", lambda e: e.tensor_tensor(out=TM[P64, 1, 0:64], in0=b, in1=S5[P64, 11, :], op=ALU.mult), reads=[rTM, rS5], writes=[rTM])
                S.op("dve", lambda e: e.tensor_tensor(out=sim, in0=sim, in1=TM[P64, 1, 0:64], op=ALU.add), reads=[rTM, rSST], writes=[rSST])
        lr2 = S5[P64, 6:7, :].to_broadcast([64, 2, 64])
        st2 = SST[P64, :].rearrange("p (r g) -> p r g", r=2)
        t1 = TM[P64, 1, :]
        t1v = TM[P64, 1, :].rearrange("p (r g) -> p r g", r=2)
        t2 = TM[P64, 2, :]
        for t0 in range(0, T, 128):
            nt = min(128, T - t0)
            for g4 in range(0, 64, 2):
                pi = self.next_ps()
                for gi in range(2):
                    g = g4 + gi
                    for r in range(2):
                        S.op("pe", lambda e, g=g, r=r, gi=gi: e.matmul(self.PS[pi][:64, (gi * 2 + r) * 128:(gi * 2 + r) * 128 + nt], lhsT=self.BP[:, g, r, :],
                                                                      rhs=self.XNT[:, g // 8, t0:t0 + nt], start=True, stop=True),
                             reads=[self.rBP, self.rXNT], writes=[self.rPS[pi]], sig=(gi == 1 and r == 1))
                src = self.PS[pi][:64, :].rearrange("p (g r t) -> p r g t", g=2, r=2)[:, :, :, 0:nt]
                dst = self.BU[P64, :, g4:g4 + 2, 0:nt]
                eng = "act" if (g4 // 2) % 2 == 0 else "dve"
                if eng == "act":
                    for r in range(2):
                        S.op("act", lambda e, r=r: e.activation(out=dst[:, r], in_=src[:, r], func=AF.Copy), reads=[self.rPS[pi]], writes=[self.rBU])
                else:
                    for r in range(2):
                        S.op("dve", lambda e, r=r: e.tensor_copy(out=dst[:, r], in_=src[:, r]), reads=[self.rPS[pi]], writes=[self.rBU])
            rT1, rT2a, rT2b = rTM.sub(512, 1024), rTM.sub(1024, 1280), rTM.sub(1280, 1536)
            for t in range(nt):
                S.op("dve", lambda e: e.tensor_tensor(out=t1v, in0=st2, in1=lr2, op=ALU.mult), reads=[rSST, rS5], writes=[rT1])
                S.op("pool", lambda e: e.tensor_tensor(out=t2[:, 0:64], in0=sim, in1=S5[P64, 8, :], op=ALU.mult), reads=[rSST, rS5], writes=[rT2a])
                S.op("pool", lambda e: e.tensor_tensor(out=t2[:, 64:128], in0=sre, in1=S5[P64, 7, :], op=ALU.mult), reads=[rSST, rS5], writes=[rT2b])
                S.op("dve", lambda e: e.tensor_tensor(out=t1, in0=t1, in1=t2, op=ALU.add), reads=[rT1, rT2a, rT2b], writes=[rT1])
                S.op("dve", lambda e, t=t: e.tensor_tensor(out=SST[P64, :], in0=t1, in1=self.BU[P64, :, :, t].rearrange("p r g -> p (r g)"), op=ALU.add),
                     reads=[rT1, self.rBU], writes=[rSST])
                S.op("act", lambda e, t=t: e.activation(out=self.HI[P64, :, :, t].rearrange("p r g -> p (r g)"), in_=SST[P64, :], func=AF.Copy),
                     reads=[rSST], writes=[self.rHI])
            pa, pb = self.next_ps(), self.next_ps()
            for g in range(64):
                pi = pa if g < 32 else pb
                col = (g % 32) * 16
                for r in range(2):
                    S.op("pe", lambda e, g=g, r=r: e.matmul(self.PS[pi][:nt, col:col + 16], lhsT=self.HI[P64, r, g, 0:nt], rhs=self.CP[P64, r, g, :],
                                                             start=(r == 0), stop=(r == 1)),
                         reads=[self.rHI, self.rCP], writes=[self.rPS[pi]], sig=(r == 1 and g % 32 == 31))
            for hh, pi in ((0, pa), (1, pb)):
                S.op("act", lambda e: e.activation(out=self.YT[:nt, hh * 512:(hh + 1) * 512], in_=self.PS[pi][:nt, :], func=AF.Copy),
                     reads=[self.rPS[pi]], writes=[self.rYT])
            for kc in range(8):
                pi = self.next_ps()
                pt = self.PS[pi][:, 0:64].bitcast(BF16)
                S.op("pe", lambda e: e.transpose(out=pt[:, 0:nt], in_=self.YT[:nt, kc * 128:(kc + 1) * 128], identity=self.IDB[:nt, :nt]),
                     reads=[self.rYT, self.rIDB], writes=[self.rPS[pi]])
                xc = self.XNT[:, kc, t0:t0 + nt]
                S.op("dve", lambda e, xc=xc: e.scalar_tensor_tensor(out=xc, in0=xc, scalar=self.DFM[:, kc:kc + 1], in1=pt[:, 0:nt], op0=ALU.mult, op1=ALU.add),
                     reads=[self.rPS[pi], self.rXNT, self.rDFM], writes=[self.rXNT])
        if last:
            ore, oim = (self.o_pre, self.o_pim) if prompt else (self.o_sre, self.o_sim)
            fr, fi = TM[P64, 1, 0:64], TM[P64, 1, 64:128]
            S.op("dve", lambda e: e.tensor_tensor(out=fr, in0=sre, in1=S5[P64, 9, :], op=ALU.mult), reads=[rSST, rS5], writes=[rTM])
            S.op("dve", lambda e: e.tensor_tensor(out=t2[:, 0:64], in0=sim, in1=S5[P64, 10, :], op=ALU.mult), reads=[rSST, rS5], writes=[rTM])
            S.op("dve", lambda e: e.tensor_tensor(out=fr, in0=fr, in1=t2[:, 0:64], op=ALU.subtract), reads=[rTM], writes=[rTM])
            S.op("dve", lambda e: e.tensor_tensor(out=fi, in0=sre, in1=S5[P64, 10, :], op=ALU.mult), reads=[rSST, rS5], writes=[rTM])
            S.op("dve", lambda e: e.tensor_tensor(out=t2[:, 0:64], in0=sim, in1=S5[P64, 9, :], op=ALU.mult), reads=[rSST, rS5], writes=[rTM])
            S.op("dve", lambda e: e.tensor_tensor(out=fi, in0=fi, in1=t2[:, 0:64], op=ALU.add), reads=[rTM], writes=[rTM])
            for src, od in ((fr, ore), (fi, oim)):
                pi = self.next_ps()
                S.op("pe", lambda e, src=src: e.matmul(self.PS[pi][:64, 0:64], lhsT=src, rhs=self.IDF[0:64, 0:64], start=True, stop=True),
                     reads=[rTM, self.rIDF], writes=[self.rPS[pi]])
                S.op("act", lambda e: e.activation(out=self.SG[0][:64, 0:64], in_=self.PS[pi][:64, 0:64], func=AF.Copy), reads=[self.rPS[pi]], writes=[self.rSG[0]])
                S.dma("sp", out=od[:, :], in_=self.SG[0][:64, 0:64], reads=[self.rSG[0]], writes=[], sem="o_sg0")
        for q in range(4):
            sl = q % 2
            gw = self.GW[sl]
            for half in range(2):
                S.dma("sp", out=gw[:, :, half * 256:(half + 1) * 256],
                      in_=self.s_glu[:, :].rearrange("p (k n) -> p k n", k=8)[:, :, half * 1024 + q * 256:half * 1024 + (q + 1) * 256],
                      reads=[self.R_scr], writes=[self.rGW[sl]], sem="gw%d_%d" % (sl, half))
            for j in range(8):
                pa, pb = self.next_ps(), self.next_ps()
                for half, pi in ((0, pa), (1, pb)):
                    for kc in range(8):
                        lhsT = self.XNT[:, kc, 0:T].rearrange("p (c j) -> p c j", j=8)[:, :, j]
                        S.op("pe", lambda e, kc=kc, lhsT=lhsT, half=half: e.matmul(self.PS[pi][:NC, 0:256], lhsT=lhsT, rhs=gw[:, kc, half * 256:(half + 1) * 256],
                                                                                   start=(kc == 0), stop=(kc == 7)),
                             reads=[self.rXNT, self.rGW[sl]], writes=[self.rPS[pi]], sig=(kc == 7))
                sg = self.SG[1][:NC, 0:256]
                S.op("act", lambda e: e.activation(out=sg, in_=self.PS[pb][:NC, 0:256], func=AF.Tanh, scale=0.5), reads=[self.rPS[pb]], writes=[self.rSG[1]])
                S.op("dve", lambda e: e.scalar_tensor_tensor(out=sg, in0=sg, scalar=1.0, in1=self.PS[pa][:NC, 0:256], op0=ALU.add, op1=ALU.mult),
                     reads=[self.rSG[1], self.rPS[pa]], writes=[self.rSG[1]])
                xv = self.X8[:NC, j, q * 256:(q + 1) * 256]
                S.op("dve", lambda e, xv=xv: e.scalar_tensor_tensor(out=xv, in0=sg, scalar=0.5, in1=xv, op0=ALU.mult, op1=ALU.add),
                     reads=[self.rSG[1], self.rX8], writes=[self.rX8])


def _lay_win(w):
    a = w.reshape(8, 128, 2, NF, 128)
    return np.ascontiguousarray(a.transpose(3, 1, 0, 2, 4)).reshape(NF * 128, 2048)


def _lay_rows(w):
    k = w.shape[0] // 128
    return np.ascontiguousarray(w.reshape(k, 128, w.shape[1]).transpose(1, 0, 2)).reshape(128, k * w.shape[1])


def _lay_cols(w, c0, nchunks):
    a = w[:, c0:c0 + nchunks * 128].reshape(8, 128, nchunks, 128)
    return np.ascontiguousarray(a.transpose(2, 1, 0, 3)).reshape(nchunks * 128, 1024)


def _fm(v):
    v = np.asarray(v, np.float32).reshape(-1, 4, 128)
    return np.ascontiguousarray(v.transpose(2, 0, 1)).reshape(128, -1)


def host_common(inp):
    f32 = np.float32
    d = {}
    w_in = [inp["ffn1_w_in"][0], inp["ffn2_w_in"][0], inp["ffn1_w_in"][1], inp["ffn2_w_in"][1]]
    w_out = [inp["ffn1_w_out"][0], inp["ffn2_w_out"][0], inp["ffn1_w_out"][1], inp["ffn2_w_out"][1]]
    d["w_ffn_in"] = np.stack([_lay_win(np.asarray(w, f32)) for w in w_in])
    d["w_ffn_out"] = np.stack([_lay_rows(np.asarray(w, f32)) for w in w_out])
    gam = np.stack([inp["ffn1_norm"][0], inp["mix_norm"][0], inp["ffn2_norm"][0],
                    inp["ffn1_norm"][1], inp["mix_norm"][1], inp["ffn2_norm"][1]]).astype(f32)
    d["gam"] = np.ascontiguousarray(gam.reshape(6, 8, 128).transpose(2, 0, 1)).reshape(128, 48)
    wab = np.asarray(inp["ab_w_in"][0], f32)
    d["w_abin"] = _lay_cols(wab, 0, 16)
    d["w_abv"] = _lay_rows(np.ascontiguousarray(wab[:, 2048:2560]))
    d["w_about"] = _lay_rows(np.asarray(inp["ab_w_out"][0], f32))
    m0c = np.zeros((128, 46), f32)
    cw = np.asarray(inp["conv_w"][0], f32)
    for c in range(4):
        for k in range(4):
            m0c[:, 4 * c + k] = cw[k, c * 128:(c + 1) * 128]
    m0c[:, 16:20] = _fm(inp["conv_b"][0])
    m0c[:, 20:24] = _fm(inp["lru_ba"][0])
    m0c[:, 24:28] = _fm(inp["lru_bx"][0])
    m0c[:, 28:32] = _fm(inp["lru_lambda"][0])
    m0c[:, 36] = np.tile(np.asarray(inp["q_norm"][0], f32), 2)
    m0c[:, 37] = np.tile(np.asarray(inp["k_norm"][0], f32), 2)
    rb = np.asarray(inp["rel_bias"][0], f32)
    m0c[:, 38:46] = rb[256][None, :]
    d["m0c"] = m0c
    wbd = np.zeros((128, 2, 4, 128), f32)
    for w, nm in enumerate(("lru_wa", "lru_wx")):
        W = np.asarray(inp[nm][0], f32)
        for c in range(4):
            wbd[0:64, w, c, 0:64] = W[2 * c]
            wbd[64:128, w, c, 64:128] = W[2 * c + 1]
    d["wbd"] = wbd.reshape(128, -1)
    p = np.arange(128)
    kl = np.where(p < 64, p, p - 64)
    bb = np.zeros((128, 8, 5, 64), f32)
    q = np.arange(64)
    for jj in range(5):
        cp = np.where(p < 64, jj, jj - 1)
        rel = q[None, :] - kl[:, None] + 64 * cp[:, None]
        idx = np.clip(rel, -128, 128) + 128
        bb[:, :, jj, :] = rb[idx].transpose(0, 2, 1)
    d["bblk"] = bb.reshape(128, -1)
    are = np.asarray(inp["ssm_A_re"][0], f32).T
    aim = np.asarray(inp["ssm_A_im"][0], f32).T
    ldt = np.broadcast_to(np.asarray(inp["ssm_log_dt"][0], f32)[None, :], (64, 64))
    d["s5p"] = np.ascontiguousarray(np.concatenate([are, aim, ldt], axis=1))
    cre = np.asarray(inp["ssm_C_re"][0], f32).transpose(2, 0, 1)
    cim = np.asarray(inp["ssm_C_im"][0], f32).transpose(2, 0, 1)
    d["s5c"] = np.ascontiguousarray(np.stack([cre, cim], axis=1)).reshape(64, 2048)
    bp = np.zeros((128, 64, 2, 64), f32)
    for r, nm in enumerate(("ssm_B_re", "ssm_B_im")):
        Bm = np.asarray(inp[nm][0], f32)
        for g in range(64):
            bp[16 * (g % 8):16 * (g % 8) + 16, g, r, :] = Bm[g].T
    d["w_bp"] = bp.reshape(128, 8192)
    d["w_glu"] = _lay_rows(np.asarray(inp["glu_w"][0], f32))
    d["dfm"] = np.ascontiguousarray(np.asarray(inp["ssm_D"][0], f32).reshape(8, 128).T)
    return d


def host_core(inp, common, c, SEQ):
    f32 = np.float32
    d = dict(common)
    d["xp"] = np.ascontiguousarray(np.asarray(inp["x_prompt"][c % inp["x_prompt"].shape[0]], f32)[:SEQ])
    d["xs"] = np.ascontiguousarray(np.asarray(inp["x_sample"][c], f32))
    st = np.zeros((128, 16), f32)
    st[:, 0:4] = _fm(inp["state_rglru_h"][0, c])
    cv = np.asarray(inp["state_rglru_conv"][0, c], f32)
    for cc in range(4):
        for k in range(3):
            st[:, 4 + 3 * cc + k] = cv[k, cc * 128:(cc + 1) * 128]
    d["st_rg"] = st
    d["ck"] = np.ascontiguousarray(np.asarray(inp["cache_band_k"][0, c], f32).reshape(512, 512))
    d["cv"] = np.ascontiguousarray(np.asarray(inp["cache_band_v"][0, c], f32).reshape(512, 512))
    d["s0"] = np.ascontiguousarray(np.concatenate([np.asarray(inp["state_ssm_re"][0, c], f32).T, np.asarray(inp["state_ssm_im"][0, c], f32).T], axis=1))
    return d


_STAGES = ("ffn", "mix0", "mix1")


def kernel(**inputs):
    inp = {k: np.asarray(v) for k, v in inputs.items()}
    SEQ = inp["x_prompt"].shape[1]
    B = inp["x_prompt"].shape[0]
    NS = inp["x_sample"].shape[0]
    common = host_common(inp)
    in_maps = [host_core(inp, common, c, SEQ) for c in range(8)]
    b = Builder(SEQ=SEQ, stages=_STAGES)
    nc = b.build()
    res = run_bass_kernel_spmd(nc, in_maps, core_ids=list(range(8)))
    rs = res.results
    f32 = np.float32
    KR = min(512, SEQ)

    def st(name, n, shape):
        return np.stack([np.asarray(rs[c][name], f32).reshape(shape) for c in range(n)])[None]

    y_prompt = np.stack([np.asarray(rs[c]["yp"], f32) for c in range(B)])
    y_sample = np.stack([np.asarray(rs[c]["ys"], f32) for c in range(NS)])
    return (y_prompt, y_sample,
            st("o_pconv", B, (3, 512)), st("o_ph", B, (512,)), st("o_pk", B, (KR, 8, 64)), st("o_pv", B, (KR, 8, 64)),
            st("o_pre", B, (64, 64)), st("o_pim", B, (64, 64)),
            st("o_sconv", NS, (3, 512)), st("o_sh", NS, (512,)), st("o_sk", NS, (64, 8, 64)), st("o_sv", NS, (64, 8, 64)),
            st("o_sre", NS, (64, 64)), st("o_sim", NS, (64, 64)))
```
